# Optimizing a Trainium2 kernel written in Bass

```python
import math
import jax, jax.numpy as jnp
from jax import lax
import numpy as np

D_MODEL = 1024
BATCH = 16
SEQ = 256
DEPTH = 2
DEC_BATCH = 4
DEC_SEQ = 4096
PAST_LEN = 256

GRID_W = 64
HEAD_DIM = 64
NA_HEADS = 8
NA_WIDTH = NA_HEADS * HEAD_DIM
WIN_H = 8
WIN_W = 16
SSM_WIDTH = D_MODEL - NA_WIDTH
SSM_GROUP = 16
SSM_GROUPS = SSM_WIDTH // SSM_GROUP
SSM_STATE = 64
GQA_HEADS = 16
GQA_KV_HEADS = 4
ROPE_THETA = 10000.0
D_FF = 2816
Q_BLOCK = 128
EPS = 1e-6
N_EVEN = (DEPTH + 1) // 2
N_ODD = DEPTH // 2
EVEN_IN = 3 * NA_WIDTH + SSM_WIDTH
ODD_IN = (GQA_HEADS + 2 * GQA_KV_HEADS) * HEAD_DIM

kernel_name = 'hybrid_natten_s5_gqa_diffusion_step'


def rmsnorm(x, g):
    xf = x.astype(jnp.float32)
    xf = xf * lax.rsqrt(jnp.mean(xf * xf, axis=-1, keepdims=True) + EPS)
    return (xf * g.astype(jnp.float32)).astype(x.dtype)


def adaln(cvec, w, b):
    m = jnp.dot(jax.nn.silu(cvec), w) + b
    return jnp.split(m[..., None, :], 6, axis=-1)


def modulate(h, shift, scale):
    return h * (1 + scale) + shift


def rope_2d(x):
    L = x.shape[1]
    t = jnp.arange(L)
    half = HEAD_DIM // 2
    nf = half // 2
    freqs = ROPE_THETA ** (-jnp.arange(nf, dtype=jnp.float32) / nf)

    def rotate(xh, pos):
        ang = pos.astype(jnp.float32)[:, None] * freqs[None, :]
        cos = jnp.cos(ang)[None, :, None, :]
        sin = jnp.sin(ang)[None, :, None, :]
        x1, x2 = xh[..., :nf], xh[..., nf:]
        return jnp.concatenate([x1 * cos - x2 * sin, x1 * sin + x2 * cos], axis=-1)

    xf = x.astype(jnp.float32)
    out = jnp.concatenate([rotate(xf[..., :half], t // GRID_W), rotate(xf[..., half:], t % GRID_W)], axis=-1)
    return out.astype(x.dtype)


def blocked_attention(q, k, v):
    B, S, H, Dh = q.shape
    Hkv = k.shape[2]
    G = H // Hkv
    nb = S // Q_BLOCK
    qb = q.reshape(B, nb, Q_BLOCK, Hkv, G, Dh).transpose(1, 0, 2, 3, 4, 5)
    scale = Dh ** -0.5

    def block(qi):
        s = jnp.einsum('bqkgd,btkd->bkgqt', qi, k, preferred_element_type=jnp.float32) * scale
        p = jax.nn.softmax(s, axis=-1).astype(v.dtype)
        return jnp.einsum('bkgqt,btkd->bqkgd', p, v)

    o = lax.map(block, qb)
    return o.transpose(1, 0, 2, 3, 4, 5).reshape(B, S, H, Dh)


def na_latent(q, k, v, k_ctx, v_ctx, rpb):
    B, L, H, Dh = q.shape
    rows = L // GRID_W
    kh = min(WIN_H, rows)
    kw = WIN_W
    qg = q.reshape(B, rows, GRID_W, H, Dh)
    kg = k.reshape(B, rows, GRID_W, H, Dh)
    vg = v.reshape(B, rows, GRID_W, H, Dh)
    cols = jnp.arange(GRID_W)
    col_start = jnp.clip(cols - kw // 2, 0, GRID_W - kw)
    col_idx = col_start[:, None] + jnp.arange(kw)[None, :]
    dc = col_idx - cols[:, None] + (WIN_W - 1)
    rpb_f = rpb.astype(jnp.float32)
    scale = Dh ** -0.5

    def row_fn(r):
        rs = jnp.clip(r - kh // 2, 0, rows - kh)
        q_r = lax.dynamic_index_in_dim(qg, r, axis=1, keepdims=False)
        k_band = lax.dynamic_slice_in_dim(kg, rs, kh, axis=1)
        v_band = lax.dynamic_slice_in_dim(vg, rs, kh, axis=1)
        k_win = k_band[:, :, col_idx]
        v_win = v_band[:, :, col_idx]
        s_loc = jnp.einsum('bwhd,biwjhd->bhwij', q_r, k_win, preferred_element_type=jnp.float32) * scale
        dr = rs + jnp.arange(kh) - r + (WIN_H - 1)
        bias = rpb_f[:, dr[None, :, None], dc[:, None, :]]
        s_loc = s_loc + bias[None]
        s_ctx = jnp.einsum('bwhd,blhd->bhwl', q_r, k_ctx, preferred_element_type=jnp.float32) * scale
        s = jnp.concatenate([s_loc.reshape(B, H, GRID_W, kh * kw), s_ctx], axis=-1)
        p = jax.nn.softmax(s, axis=-1).astype(v.dtype)
        p_loc = p[..., :kh * kw].reshape(B, H, GRID_W, kh, kw)
        p_ctx = p[..., kh * kw:]
        return (jnp.einsum('bhwij,biwjhd->bwhd', p_loc, v_win)
                + jnp.einsum('bhwl,blhd->bwhd', p_ctx, v_ctx))

    out = lax.map(row_fn, jnp.arange(rows))
    return out.transpose(1, 0, 2, 3, 4).reshape(B, L, H, Dh)


def zoh(a_re, a_im, log_dt, b_re, b_im):
    a_re = a_re.astype(jnp.float32)
    a_im = a_im.astype(jnp.float32)
    b_re = b_re.astype(jnp.float32)
    b_im = b_im.astype(jnp.float32)
    dt = jnp.exp(log_dt.astype(jnp.float32))[:, None]
    mag = jnp.exp(a_re * dt)
    abr = mag * jnp.cos(a_im * dt)
    abi = mag * jnp.sin(a_im * dt)
    den = a_re * a_re + a_im * a_im
    nr = abr - 1.0
    ni = abi
    kr = (nr * a_re + ni * a_im) / den
    ki = (ni * a_re - nr * a_im) / den
    bbr = kr[..., None] * b_re - ki[..., None] * b_im
    bbi = kr[..., None] * b_im + ki[..., None] * b_re
    return abr, abi, bbr, bbi


def _complex_affine_combine(e1, e2):
    a1r, a1i, b1r, b1i = e1
    a2r, a2i, b2r, b2i = e2
    return (a2r * a1r - a2i * a1i,
            a2r * a1i + a2i * a1r,
            a2r * b1r - a2i * b1i + b2r,
            a2r * b1i + a2i * b1r + b2i)


def diag_scan(abr, abi, bur, bui, h0, reverse):
    L = bur.shape[1]
    ar = jnp.broadcast_to(abr, (1, L) + abr.shape)
    ai = jnp.broadcast_to(abi, (1, L) + abi.shape)
    Ar, Ai, Hr, Hi = lax.associative_scan(_complex_affine_combine, (ar, ai, bur, bui), reverse=reverse, axis=1)
    if h0 is not None:
        h0r = h0[0][:, None]
        h0i = h0[1][:, None]
        Hr = Hr + Ar * h0r - Ai * h0i
        Hi = Hi + Ar * h0i + Ai * h0r
    return Hr, Hi


def s5_mixer(u, ssm, init):
    a_re, a_im, log_dt, b_re, b_im, c_re, c_im, d, w_glu, b_glu = ssm
    B, L, _ = u.shape
    uf = u.astype(jnp.float32).reshape(B, L, SSM_GROUPS, SSM_GROUP)
    y = uf * d.astype(jnp.float32)
    finals = []
    for dr in range(2):
        rev = dr == 1
        abr, abi, bbr, bbi = zoh(a_re[dr], a_im[dr], log_dt[dr], b_re[dr], b_im[dr])
        bur = jnp.einsum('blgc,gpc->blgp', uf, bbr)
        bui = jnp.einsum('blgc,gpc->blgp', uf, bbi)
        h0 = None if init is None else (init[0][:, dr].astype(jnp.float32), init[1][:, dr].astype(jnp.float32))
        hr, hi = diag_scan(abr, abi, bur, bui, h0, rev)
        y = (y + jnp.einsum('gcp,blgp->blgc', c_re[dr].astype(jnp.float32), hr)
             - jnp.einsum('gcp,blgp->blgc', c_im[dr].astype(jnp.float32), hi))
        if init is None:
            end = 0 if rev else L - 1
            finals.append((hr[:, end], hi[:, end]))
    y = jax.nn.gelu(y.reshape(B, L, SSM_WIDTH))
    y = y * jax.nn.sigmoid(y @ w_glu.astype(jnp.float32) + b_glu.astype(jnp.float32))
    y = y.astype(u.dtype)
    if init is None:
        s_re = jnp.stack([f[0] for f in finals], axis=1).astype(u.dtype)
        s_im = jnp.stack([f[1] for f in finals], axis=1).astype(u.dtype)
        return y, (s_re, s_im)
    return y, None


def even_mixer(h, ev, ctx):
    w_in, w_out, q_g, k_g, rpb = ev[:5]
    ssm = ev[5:]
    B, L, _ = h.shape
    hp = h @ w_in
    q, k, v, u = jnp.split(hp, [NA_WIDTH, 2 * NA_WIDTH, 3 * NA_WIDTH], axis=-1)
    q = rmsnorm(q.reshape(B, L, NA_HEADS, HEAD_DIM), q_g)
    k = rmsnorm(k.reshape(B, L, NA_HEADS, HEAD_DIM), k_g)
    v = v.reshape(B, L, NA_HEADS, HEAD_DIM)
    if ctx is None:
        na = blocked_attention(q, k, v)
        y_ssm, (s_re, s_im) = s5_mixer(u, ssm, None)
        out = jnp.concatenate([na.reshape(B, L, NA_WIDTH), y_ssm], axis=-1) @ w_out
        return out, (k, v, s_re, s_im)
    k_ctx, v_ctx, s_re, s_im = ctx
    na = na_latent(q, k, v, k_ctx, v_ctx, rpb)
    y_ssm, _ = s5_mixer(u, ssm, (s_re, s_im))
    return jnp.concatenate([na.reshape(B, L, NA_WIDTH), y_ssm], axis=-1) @ w_out, None


def odd_mixer(h, od, ctx):
    w_in, w_out, q_g, k_g = od
    B, L, _ = h.shape
    hp = h @ w_in
    q, k, v = jnp.split(hp, [GQA_HEADS * HEAD_DIM, (GQA_HEADS + GQA_KV_HEADS) * HEAD_DIM], axis=-1)
    q = rmsnorm(q.reshape(B, L, GQA_HEADS, HEAD_DIM), q_g)
    k = rmsnorm(k.reshape(B, L, GQA_KV_HEADS, HEAD_DIM), k_g)
    v = v.reshape(B, L, GQA_KV_HEADS, HEAD_DIM)
    if ctx is None:
        o = blocked_attention(q, k, v)
        return o.reshape(B, L, GQA_HEADS * HEAD_DIM) @ w_out, (k, v)
    k_ctx, v_ctx = ctx
    q = rope_2d(q)
    k = rope_2d(k)
    o = blocked_attention(q, jnp.concatenate([k, k_ctx], axis=1), jnp.concatenate([v, v_ctx], axis=1))
    return o.reshape(B, L, GQA_HEADS * HEAD_DIM) @ w_out, None


def dwconv3(x, w, b):
    xp = jnp.pad(x, ((0, 0), (1, 1), (0, 0)))
    return xp[:, :-2] * w[0] + xp[:, 1:-1] * w[1] + xp[:, 2:] * w[2] + b


def conv_ffn(h, w_up, conv_w, conv_b, w_down):
    gate, val = jnp.split(h @ w_up, 2, axis=-1)
    gate = dwconv3(gate, conv_w, conv_b)
    return (jax.nn.silu(gate) * val) @ w_down


def setup_inputs(seed: int = 0) -> dict:
    key = jax.random.key(seed)
    ks = jax.random.split(key, 40)

    def nrm(k, shape, s):
        return jax.random.normal(k, shape, jnp.float32) * s

    G, P, C = SSM_GROUPS, SSM_STATE, SSM_GROUP
    n_idx = jnp.arange(P, dtype=jnp.float32)
    return {
        'x_prompt': nrm(ks[0], (BATCH, SEQ, D_MODEL), 1.0),
        'x_sample': nrm(ks[1], (DEC_BATCH, DEC_SEQ, D_MODEL), 1.0),
        'cache_na_k': nrm(ks[2], (DEC_BATCH, N_EVEN, PAST_LEN, NA_HEADS, HEAD_DIM), 1.0),
        'cache_na_v': nrm(ks[3], (DEC_BATCH, N_EVEN, PAST_LEN, NA_HEADS, HEAD_DIM), 1.0),
        'state_ssm_re': nrm(ks[4], (DEC_BATCH, N_EVEN, 2, G, P), 0.3),
        'state_ssm_im': nrm(ks[5], (DEC_BATCH, N_EVEN, 2, G, P), 0.3),
        'cache_gqa_k': nrm(ks[6], (DEC_BATCH, N_ODD, PAST_LEN, GQA_KV_HEADS, HEAD_DIM), 1.0),
        'cache_gqa_v': nrm(ks[7], (DEC_BATCH, N_ODD, PAST_LEN, GQA_KV_HEADS, HEAD_DIM), 1.0),
        'c': nrm(ks[8], (DEC_BATCH, D_MODEL), 1.0),
        'c_ctx': nrm(ks[9], (D_MODEL,), 1.0),
        'norm1_g': 1.0 + nrm(ks[10], (DEPTH, D_MODEL), 0.02),
        'norm2_g': 1.0 + nrm(ks[11], (DEPTH, D_MODEL), 0.02),
        'ada_w': nrm(ks[12], (DEPTH, D_MODEL, 6 * D_MODEL), D_MODEL ** -0.5),
        'ada_b': nrm(ks[13], (DEPTH, 6 * D_MODEL), 0.01),
        'ffn_w_up': nrm(ks[14], (DEPTH, D_MODEL, 2 * D_FF), D_MODEL ** -0.5),
        'ffn_conv_w': nrm(ks[15], (DEPTH, 3, D_FF), 3 ** -0.5),
        'ffn_conv_b': nrm(ks[16], (DEPTH, D_FF), 0.01),
        'ffn_w_down': nrm(ks[17], (DEPTH, D_FF, D_MODEL), D_FF ** -0.5),
        'ev_w_in': nrm(ks[18], (N_EVEN, D_MODEL, EVEN_IN), D_MODEL ** -0.5),
        'ev_w_out': nrm(ks[19], (N_EVEN, NA_WIDTH + SSM_WIDTH, D_MODEL), (NA_WIDTH + SSM_WIDTH) ** -0.5),
        'na_q_g': 1.0 + nrm(ks[20], (N_EVEN, HEAD_DIM), 0.02),
        'na_k_g': 1.0 + nrm(ks[21], (N_EVEN, HEAD_DIM), 0.02),
        'na_rpb': nrm(ks[22], (N_EVEN, NA_HEADS, 2 * WIN_H - 1, 2 * WIN_W - 1), 0.02),
        'ssm_a_re': -0.5 + nrm(ks[23], (N_EVEN, 2, G, P), 0.01),
        'ssm_a_im': math.pi * n_idx + nrm(ks[24], (N_EVEN, 2, G, P), 0.01),
        'ssm_log_dt': jax.random.uniform(ks[25], (N_EVEN, 2, G), jnp.float32, math.log(1e-3), math.log(1e-1)),
        'ssm_b_re': nrm(ks[26], (N_EVEN, 2, G, P, C), (2 * C) ** -0.5),
        'ssm_b_im': nrm(ks[27], (N_EVEN, 2, G, P, C), (2 * C) ** -0.5),
        'ssm_c_re': nrm(ks[28], (N_EVEN, 2, G, C, P), (2 * P) ** -0.5),
        'ssm_c_im': nrm(ks[29], (N_EVEN, 2, G, C, P), (2 * P) ** -0.5),
        'ssm_d': nrm(ks[30], (N_EVEN, G, C), 1.0),
        'ssm_w_glu': nrm(ks[31], (N_EVEN, SSM_WIDTH, SSM_WIDTH), SSM_WIDTH ** -0.5),
        'ssm_b_glu': nrm(ks[32], (N_EVEN, SSM_WIDTH), 0.01),
        'od_w_in': nrm(ks[33], (N_ODD, D_MODEL, ODD_IN), D_MODEL ** -0.5),
        'od_w_out': nrm(ks[34], (N_ODD, GQA_HEADS * HEAD_DIM, D_MODEL), (GQA_HEADS * HEAD_DIM) ** -0.5),
        'gqa_q_g': 1.0 + nrm(ks[35], (N_ODD, HEAD_DIM), 0.02),
        'gqa_k_g': 1.0 + nrm(ks[36], (N_ODD, HEAD_DIM), 0.02),
    }


def reference(x_prompt, x_sample, cache_na_k, cache_na_v, state_ssm_re, state_ssm_im, cache_gqa_k, cache_gqa_v,
              c, c_ctx, norm1_g, norm2_g, ada_w, ada_b, ffn_w_up, ffn_conv_w, ffn_conv_b, ffn_w_down,
              ev_w_in, ev_w_out, na_q_g, na_k_g, na_rpb, ssm_a_re, ssm_a_im, ssm_log_dt, ssm_b_re, ssm_b_im,
              ssm_c_re, ssm_c_im, ssm_d, ssm_w_glu, ssm_b_glu, od_w_in, od_w_out, gqa_q_g, gqa_k_g):

    def even_params(i):
        return (ev_w_in[i], ev_w_out[i], na_q_g[i], na_k_g[i], na_rpb[i],
                ssm_a_re[i], ssm_a_im[i], ssm_log_dt[i], ssm_b_re[i], ssm_b_im[i],
                ssm_c_re[i], ssm_c_im[i], ssm_d[i], ssm_w_glu[i], ssm_b_glu[i])

    def odd_params(i):
        return (od_w_in[i], od_w_out[i], gqa_q_g[i], gqa_k_g[i])

    def layer(l, x, cond, ctx):
        sh1, sc1, g1, sh2, sc2, g2 = adaln(cond, ada_w[l], ada_b[l])
        h = modulate(rmsnorm(x, norm1_g[l]), sh1, sc1)
        if l % 2 == 0:
            out, cache = even_mixer(h, even_params(l // 2), ctx)
        else:
            out, cache = odd_mixer(h, odd_params(l // 2), ctx)
        x = x + g1 * out
        h = modulate(rmsnorm(x, norm2_g[l]), sh2, sc2)
        x = x + g2 * conv_ffn(h, ffn_w_up[l], ffn_conv_w[l], ffn_conv_b[l], ffn_w_down[l])
        return x, cache

    xp = x_prompt
    na_k, na_v, s_re, s_im, g_k, g_v = [], [], [], [], [], []
    for l in range(DEPTH):
        xp, cache = layer(l, xp, c_ctx, None)
        if l % 2 == 0:
            na_k.append(cache[0]); na_v.append(cache[1]); s_re.append(cache[2]); s_im.append(cache[3])
        else:
            g_k.append(cache[0]); g_v.append(cache[1])
    y_prompt = xp

    xs = x_sample
    for l in range(DEPTH):
        i = l // 2
        if l % 2 == 0:
            ctx = (cache_na_k[:, i], cache_na_v[:, i], state_ssm_re[:, i], state_ssm_im[:, i])
        else:
            ctx = (cache_gqa_k[:, i], cache_gqa_v[:, i])
        xs, _ = layer(l, xs, c, ctx)
    y_sample = xs

    new_na_k = jnp.stack(na_k, axis=1)
    new_na_v = jnp.stack(na_v, axis=1)
    new_ssm_re = jnp.stack(s_re, axis=1)
    new_ssm_im = jnp.stack(s_im, axis=1)
    new_gqa_k = jnp.stack(g_k, axis=1)
    new_gqa_v = jnp.stack(g_v, axis=1)
    return (y_prompt, y_sample, new_na_k, new_na_v, new_ssm_re, new_ssm_im, new_gqa_k, new_gqa_v)
```

```python
import numpy as np
from contextlib import ExitStack
import concourse.bass as bass
import concourse.mybir as mybir
from concourse.bass_utils import run_bass_kernel_spmd

F32 = mybir.dt.float32
BF16 = mybir.dt.bfloat16
I32 = mybir.dt.int32
ALU = mybir.AluOpType
AF = mybir.ActivationFunctionType
AX = mybir.AxisListType

SAME_ENGINE_SYNC = True


class Buf:
    __slots__ = ("w", "r", "name")

    def __init__(self, name=""):
        self.w = None
        self.r = {}
        self.name = name


class Sched:
    ENGS = ("pe", "act", "dve", "pool", "sp")
    NDMA = 8

    def __init__(self, nc, es):
        self.nc = nc
        self.sems = {}
        for e in ("pe", "act", "dve", "pool"):
            self.sems[e] = es.enter_context(nc.semaphore("c_" + e))
        self.cnt = {e: 0 for e in ("pe", "act", "dve", "pool")}
        self.dsems = {}
        self.dcnt = {}
        self.drr = {}
        for q in ("sp", "pool", "act"):
            self.dsems[q] = [es.enter_context(nc.semaphore("d_%s%d" % (q, i))) for i in range(self.NDMA)]
            self.dcnt[q] = [0] * self.NDMA
            self.drr[q] = 0
        self.semobj = {}
        for e, s in self.sems.items():
            self.semobj[("c", e)] = s
        for q, l in self.dsems.items():
            for i, s in enumerate(l):
                self.semobj[("d", q, i)] = s
        self.waited = {e: {} for e in self.ENGS}
        self.ops = {e: [] for e in self.ENGS}
        self.bufs = []
        self.nops = 0

    def buf(self, name=""):
        b = Buf(name)
        self.bufs.append(b)
        return b

    def bufs_n(self, n, name=""):
        return [self.buf(name + str(i)) for i in range(n)]

    def _deps(self, eng, reads, writes):
        deps = {}

        def add(tok):
            if tok is None:
                return
            k, v = tok
            if deps.get(k, 0) < v:
                deps[k] = v
        for b in reads:
            add(b.w)
        for b in writes:
            add(b.w)
            for k, v in b.r.items():
                add((k, v))
        out = []
        for k, v in deps.items():
            if k[0] == "c" and k[1] == eng:
                if eng == "pe" or not SAME_ENGINE_SYNC:
                    continue
            if self.waited[eng].get(k, 0) >= v:
                continue
            self.waited[eng][k] = v
            out.append((self.semobj[k], v))
        return out

    def _mark(self, tok, reads, writes):
        k, v = tok
        for b in reads:
            if b.r.get(k, 0) < v:
                b.r[k] = v
        for b in writes:
            b.w = tok
            b.r = {}

    def op(self, eng, fn, reads=(), writes=(), signal=True):
        waits = self._deps(eng, reads, writes)
        k = ("c", eng)
        if signal:
            self.cnt[eng] += 1
            tok = (k, self.cnt[eng])
        else:
            tok = (k, self.cnt[eng] + 1)
        sem = self.sems[eng]

        def emit(e):
            for s, v in waits:
                e.wait_ge(s, v)
            ins = fn(e)
            if signal:
                ins.then_inc(sem, 1)
        self.ops[eng].append(emit)
        self._mark(tok, reads, writes)
        self.nops += 1
        return tok

    def dma(self, q, out, in_, reads=(), writes=(), **kw):
        slot = self.drr[q]
        self.drr[q] = (slot + 1) % self.NDMA
        k = ("d", q, slot)
        prev = self.dcnt[q][slot]
        waits = self._deps(q, reads, writes)
        if prev > 0 and self.waited[q].get(k, 0) < prev:
            self.waited[q][k] = prev
            waits.append((self.semobj[k], prev))
        self.dcnt[q][slot] = prev + 16
        tok = (k, prev + 16)
        sem = self.semobj[k]

        def emit(e):
            for s, v in waits:
                e.wait_ge(s, v)
            e.dma_start(out=out, in_=in_, **kw).then_inc(sem, 16)
        self.ops[q].append(emit)
        self._mark(tok, reads, writes)
        self.nops += 1
        return tok

    def flush(self):
        nc = self.nc
        for q in ("sp", "pool", "act"):
            waits = []
            for i in range(self.NDMA):
                k = ("d", q, i)
                v = self.dcnt[q][i]
                if v > 0 and self.waited[q].get(k, 0) < v:
                    self.waited[q][k] = v
                    waits.append((self.semobj[k], v))
            if waits:
                def emit(e, waits=waits):
                    for s, v in waits:
                        e.wait_ge(s, v)
                self.ops[q].append(emit)
        ops = self.ops
        with nc.Block() as block:
            if ops["pe"]:
                @block.tensor
                def _(e):
                    for f in ops["pe"]:
                        f(e)
            if ops["act"]:
                @block.scalar
                def _(e):
                    for f in ops["act"]:
                        f(e)
            if ops["dve"]:
                @block.vector
                def _(e):
                    for f in ops["dve"]:
                        f(e)
            if ops["pool"]:
                @block.gpsimd
                def _(e):
                    for f in ops["pool"]:
                        f(e)
            if ops["sp"]:
                @block.sync
                def _(e):
                    for f in ops["sp"]:
                        f(e)
        self.ops = {e: [] for e in self.ENGS}
        for e in self.ENGS:
            for ce in ("pe", "act", "dve", "pool"):
                self.waited[e][("c", ce)] = self.cnt[ce]
        for b in self.bufs:
            b.w = None
            b.r = {}
        self.bufs = []


D = 1024
NS = 4096
NPR = 512
NT0 = NS + NPR
NW = 2176
NT1 = NW + NPR
DFF = 2816
NJ = 22
EPS = 1e-6
TWO_PI = 6.28318
NEG = -30000.0


class K:
    pass


def build_program():
    nc = bass.Bass("TRN2", target_bir_lowering=False)
    g = K()

    def din(name, shape, dt=F32):
        return nc.dram_tensor(name, list(shape), dt, kind="ExternalInput").ap()

    def dout(name, shape, dt=F32):
        return nc.dram_tensor(name, list(shape), dt, kind="ExternalOutput").ap()

    def dscr(name, shape, dt=F32):
        return nc.dram_tensor(name, list(shape), dt, kind="Internal").ap()

    xp = din("xp", [NPR, D]); xs = din("xs", [NS, D])
    na_kc = din("na_kc", [256, 512]); na_vc = din("na_vc", [256, 512])
    sre_in = din("sre_in", [2, 32, 64]); sim_in = din("sim_in", [2, 32, 64])
    gq_kc = din("gq_kc", [256, 256]); gq_vc = din("gq_vc", [256, 256])
    cvec = din("cvec", [2, D])
    wsel = din("wsel", [128, 2])
    pos_all = din("pos_all", [NS, 2]); pos_win = din("pos_win", [NW, 2])
    natb = din("natb", [8, 15, 64, 64])
    norm1_g = din("norm1_g", [2, D]); norm2_g = din("norm2_g", [2, D])
    ada_w = din("ada_w", [2, D, 6 * D]); ada_b = din("ada_b", [2, 6 * D])
    ffn_w_up = din("ffn_w_up", [2, D, 2 * DFF]); ffn_conv_w = din("ffn_conv_w", [2, 3, DFF])
    ffn_conv_b = din("ffn_conv_b", [2, DFF]); ffn_w_down = din("ffn_w_down", [2, DFF, D])
    ev_w_in = din("ev_w_in", [1, D, 2048]); ev_w_out = din("ev_w_out", [1, D, D])
    na_q_g = din("na_q_g", [1, 64]); na_k_g = din("na_k_g", [1, 64])
    ssm_a_re = din("ssm_a_re", [1, 2, 32, 64]); ssm_a_im = din("ssm_a_im", [1, 2, 32, 64])
    ssm_log_dt = din("ssm_log_dt", [1, 2, 32])
    ssm_b_re = din("ssm_b_re", [1, 2, 32, 64, 16]); ssm_b_im = din("ssm_b_im", [1, 2, 32, 64, 16])
    ssm_c_re = din("ssm_c_re", [1, 2, 32, 16, 64]); ssm_c_im = din("ssm_c_im", [1, 2, 32, 16, 64])
    ssm_d = din("ssm_d", [1, 32, 16]); ssm_w_glu = din("ssm_w_glu", [1, 512, 512]); ssm_b_glu = din("ssm_b_glu", [1, 512])
    od_w_in = din("od_w_in", [1, D, 1536]); od_w_out = din("od_w_out", [1, D, D])
    gqa_q_g = din("gqa_q_g", [1, 64]); gqa_k_g = din("gqa_k_g", [1, 64])
    yp_o = dout("yp_o", [NPR, D]); ysw_o = dout("ysw_o", [NW, D])
    nak_o = dout("nak_o", [NPR, 512]); nav_o = dout("nav_o", [NPR, 512])
    sre_o = dout("sre_o", [2, 2, 32, 64]); sim_o = dout("sim_o", [2, 2, 32, 64])
    gk_o = dout("gk_o", [NPR, 256]); gv_o = dout("gv_o", [NPR, 256])
    modv = dscr("modv", [2, 2, 6, D])
    q0 = dscr("q0", [NT0, 512], BF16); k0 = dscr("k0", [NT0, 512], BF16)
    v0 = dscr("v0", [NT0, 512], BF16); u0 = dscr("u0", [NT0, 512], BF16)
    yssmT = dscr("yssmT", [512, NT0], BF16)
    x1 = dscr("x1", [NT0, D]); x2 = dscr("x2", [NT0, D])
    x2w = dscr("x2w", [NT1, D]); x3 = dscr("x3", [NT1, D])
    q1 = dscr("q1", [NT1, D], BF16); k1 = dscr("k1", [NT0, 256], BF16); v1 = dscr("v1", [NT0, 256], BF16)
    wup_bf = dscr("wup_bf", [2, D, 2 * DFF], BF16); wdn_bf = dscr("wdn_bf", [2, DFF, D], BF16)

    with ExitStack() as top:
        S = Sched(nc, top)
        top.enter_context(nc.allow_non_contiguous_dma("small strided parameter loads"))

        def SB(es, name, shape, dt=F32):
            return es.enter_context(nc.sbuf_tensor(name, list(shape), dt))

        def PS(es, name, shape, dt=F32):
            return es.enter_context(nc.psum_tensor(name, list(shape), dt))

        ident = SB(top, "ident", [128, 128], BF16); identf = SB(top, "identf", [128, 128], F32)
        ones_b = SB(top, "ones_b", [128, 128], BF16)
        BTre = SB(top, "BTre", [128, 32, 128], BF16); BTim = SB(top, "BTim", [128, 32, 128], BF16)
        CTre = SB(top, "CTre", [128, 32, 128], BF16); CTim = SB(top, "CTim", [128, 32, 128], BF16)
        RHO = SB(top, "RHO", [128, 32]); FR = SB(top, "FR", [128, 32])
        H0re = SB(top, "H0re", [128, 32]); H0im = SB(top, "H0im", [128, 32])

        def rr(lst, i):
            return lst[i % len(lst)]

        _sc_tmp = {}

        def sincos(es, tag, f_ap, shape, sin_out, cos_out, bufs_r, bufs_w, eng="dve"):
            key = (id(es), tuple(shape))
            if key not in _sc_tmp:
                nm = "sct%d" % len(_sc_tmp)
                _sc_tmp[key] = (SB(es, nm + "_ti", shape, I32), SB(es, nm + "_tf", shape), SB(es, nm + "_tg", shape), S.buf(), S.buf(), S.buf())
            ti, tf, tg, b1, b2, b3 = _sc_tmp[key]
            if b1 not in S.bufs:
                S.bufs.extend([b1, b2, b3])
            for (off, outp) in ((0.0, sin_out), (0.25, cos_out)):
                S.op(eng, lambda e, off=off: e.tensor_scalar(out=tg[:], in0=f_ap, scalar1=1.0, scalar2=off, op0=ALU.mult, op1=ALU.add), bufs_r, [b3])
                S.op(eng, lambda e: e.tensor_copy(out=ti[:], in_=tg[:]), [b3], [b1])
                S.op(eng, lambda e: e.tensor_copy(out=tf[:], in_=ti[:]), [b1], [b2])
                S.op(eng, lambda e: e.tensor_tensor(out=tg[:], in0=tg[:], in1=tf[:], op=ALU.subtract), [b2, b3], [b3])
                S.op("act", lambda e, outp=outp: e.activation(out=outp, in_=tg[:], func=AF.Sin, scale=TWO_PI), [b3], bufs_w)

        def load_bc(es, name, src_row_ap, n=D, q="sp"):
            t = SB(es, name, [128, n]); b = S.buf()
            S.dma(q, t[:], src_row_ap.partition_broadcast(128), writes=[b])
            return t, b

        with ExitStack() as es:
            bI = S.buf()
            S.op("pool", lambda e: e.memset(ident[:], 0.0), [], [bI])
            S.op("pool", lambda e: e.affine_select(out=ident[:], in_=ident[:], pattern=[[-1, 128]], compare_op=ALU.not_equal, fill=1.0, base=0, channel_multiplier=1), [bI], [bI])
            S.op("pool", lambda e: e.memset(identf[:], 0.0), [], [bI])
            S.op("pool", lambda e: e.affine_select(out=identf[:], in_=identf[:], pattern=[[-1, 128]], compare_op=ALU.not_equal, fill=1.0, base=0, channel_multiplier=1), [bI], [bI])
            S.op("pool", lambda e: e.memset(ones_b[:], 1.0), [], [bI])
            for l in range(2):
                src = ffn_w_up[l].rearrange("a (b c) -> (a b) c", c=1408)
                dst = wup_bf[l].rearrange("a (b c) -> (a b) c", c=1408)
                for i in range(4):
                    S.dma("pool", dst[i * 1024:(i + 1) * 1024, :], src[i * 1024:(i + 1) * 1024, :])
                for i in range(2):
                    S.dma("pool", wdn_bf[l, i * 1408:(i + 1) * 1408, :], ffn_w_down[l, i * 1408:(i + 1) * 1408, :])
            ct = SB(es, "ct", [128, 8, 2]); bct = S.buf()
            for cnd in range(2):
                S.dma("sp", ct[:, :, cnd], cvec[cnd].rearrange("(k p) -> p k", p=128), writes=[bct])
            S.op("act", lambda e: e.activation(out=ct[:], in_=ct[:], func=AF.Silu), [bct], [bct])
            slabs = [SB(es, "adas%d" % i, [128, 8, 512]) for i in range(2)]; bsl = S.bufs_n(2)
            pm = [PS(es, "pm%d" % i, [2, 512]) for i in range(2)]; bpm = S.bufs_n(2)
            for l in range(2):
                mrow = SB(es, "mrow%d" % l, [2, 6 * D]); bm = S.buf()
                adab = SB(es, "adab%d" % l, [2, 6 * D]); bab = S.buf()
                S.dma("sp", adab[:], ada_b[l].partition_broadcast(2), writes=[bab])
                ng = SB(es, "ng%d" % l, [2, 2, D]); bng = S.buf()
                S.dma("sp", ng[:, 0, :], norm1_g[l].partition_broadcast(2), writes=[bng])
                S.dma("sp", ng[:, 1, :], norm2_g[l].partition_broadcast(2), writes=[bng])
                for cgi in range(12):
                    n = l * 12 + cgi
                    sl = rr(slabs, n); bs = rr(bsl, n); p = rr(pm, n); bp = rr(bpm, n)
                    for kh in range(2):
                        S.dma("sp", sl[:, kh * 4:(kh + 1) * 4, :], ada_w[l].rearrange("(k p) n -> p k n", p=128)[:, kh * 4:(kh + 1) * 4, cgi * 512:(cgi + 1) * 512], writes=[bs])
                    for k in range(8):
                        S.op("pe", lambda e, p=p, sl=sl, k=k: e.matmul(p[:], lhsT=ct[:, k, :], rhs=sl[:, k, :], start=(k == 0), stop=(k == 7)), [bct, bs], [bp], signal=(k == 7))
                    S.op("dve", lambda e, p=p, cgi=cgi, mrow=mrow, adab=adab: e.tensor_tensor(out=mrow[:, cgi * 512:(cgi + 1) * 512], in0=p[:], in1=adab[:, cgi * 512:(cgi + 1) * 512], op=ALU.add), [bp, bab], [bm])
                for (slot, gi) in ((1, 0), (4, 1)):
                    S.op("dve", lambda e, slot=slot, gi=gi, mrow=mrow, ng=ng: e.scalar_tensor_tensor(out=mrow[:, slot * D:(slot + 1) * D], in0=mrow[:, slot * D:(slot + 1) * D], scalar=1.0, in1=ng[:, gi, :], op0=ALU.add, op1=ALU.mult), [bm, bng], [bm])
                S.dma("sp", modv[l].rearrange("c s d -> c (s d)"), mrow[:], reads=[bm])
            S.flush()

        def mod_bc(es, l, cnd, slot, name):
            return load_bc(es, name, modv[l, cnd, slot])

        class NormCtx:
            pass

        def make_norm_ctx(es, l, which, tag):
            c = NormCtx()
            c.A = []; c.B = []
            for cnd in range(2):
                a, ba = mod_bc(es, l, cnd, 1 + 3 * which, "%s_A%d" % (tag, cnd))
                b, bb = mod_bc(es, l, cnd, 0 + 3 * which, "%s_B%d" % (tag, cnd))
                c.A.append((a, ba)); c.B.append((b, bb))
            c.junk = SB(es, tag + "_junk", [128, D], BF16); c.bjunk = S.buf()
            c.ss = [SB(es, tag + "_ss%d" % i, [128, 4]) for i in range(2)]; c.bss = S.bufs_n(2)
            c.h32 = [SB(es, tag + "_h32%d" % i, [128, D]) for i in range(2)]; c.bh32 = S.bufs_n(2)
            c.hb = [SB(es, tag + "_hb%d" % i, [128, D], BF16) for i in range(2)]; c.bhb = S.bufs_n(2)
            c.pst = [PS(es, tag + "_pst%d" % i, [128, 8, 128], BF16) for i in range(2)]; c.bpst = S.bufs_n(2)
            c.n = 0
            return c

        def norm_tile(c, xt, bx, cnd, hT_out, bhT, col0, nrows=128):
            i = c.n; c.n += 1
            ss = rr(c.ss, i); bss = rr(c.bss, i); h32 = rr(c.h32, i); bh32 = rr(c.bh32, i)
            hb = rr(c.hb, i); bhb = rr(c.bhb, i); pst = rr(c.pst, i); bpst = rr(c.bpst, i)
            A, bA = c.A[cnd]; Bt, bB = c.B[cnd]
            r = nrows
            S.op("act", lambda e: e.activation(out=c.junk[0:r, :], in_=xt[0:r, :], func=AF.Square, accum_out=ss[0:r, 0:1]), [bx], [c.bjunk, bss])
            S.op("act", lambda e: e.activation(out=ss[0:r, 1:2], in_=ss[0:r, 0:1], func=AF.Sqrt, scale=1.0 / D, bias=EPS), [bss], [bss])
            S.op("dve", lambda e: e.reciprocal(out=ss[0:r, 2:3], in_=ss[0:r, 1:2]), [bss], [bss])
            S.op("dve", lambda e: e.scalar_tensor_tensor(out=h32[0:r, :], in0=xt[0:r, :], scalar=ss[0:r, 2:3], in1=A[0:r, :], op0=ALU.mult, op1=ALU.mult), [bx, bss, bA], [bh32])
            S.op("dve", lambda e: e.tensor_tensor(out=hb[0:r, :], in0=h32[0:r, :], in1=Bt[0:r, :], op=ALU.add), [bh32, bB], [bhb])
            for k in range(8):
                S.op("pe", lambda e, k=k: e.transpose(out=pst[:, k, 0:r], in_=hb[0:r, k * 128:(k + 1) * 128], identity=ident[0:r, 0:r]), [bhb], [bpst], signal=(k == 7))
            S.op("act", lambda e: e.copy(out=hT_out[:, :, col0:col0 + r], in_=pst[:, :, 0:r]), [bpst], [bhT])

        def head_norm(wk, ps_ap, nh, g_bc, bg, out_f32, bout, rd, extra_w=()):
            i = wk.n; wk.n += 1
            sq = rr(wk.sq, i); bsq = rr(wk.bsq, i); st = rr(wk.st, i); bst = rr(wk.bst, i)
            w = nh * 64
            S.op("act", lambda e: e.activation(out=sq[:, 0:w], in_=ps_ap, func=AF.Square), rd, [bsq])
            S.op("dve", lambda e: e.tensor_reduce(out=st[:, 0:nh], in_=sq[:, 0:w].rearrange("p (h d) -> p h d", d=64), axis=AX.X, op=ALU.add), [bsq], [bst])
            S.op("act", lambda e: e.activation(out=st[:, 16:16 + nh], in_=st[:, 0:nh], func=AF.Sqrt, scale=1.0 / 64, bias=EPS), [bst], [bst])
            S.op("dve", lambda e: e.reciprocal(out=st[:, 32:32 + nh], in_=st[:, 16:16 + nh]), [bst], [bst])
            S.op("dve", lambda e: e.tensor_tensor(out=sq[:, 0:w].rearrange("p (h d) -> p h d", d=64), in0=ps_ap.rearrange("p (h d) -> p h d", d=64), in1=st[:, 32:32 + nh].unsqueeze(2).broadcast_to([128, nh, 64]), op=ALU.mult), rd + [bst], [bsq])
            S.op("pool", lambda e: e.tensor_tensor(out=out_f32, in0=sq[:, 0:w], in1=g_bc[:, 0:w], op=ALU.mult), [bsq, bg], [bout] + list(extra_w))

        class HN:
            pass

        def make_hn(es, tag):
            wk = HN(); wk.n = 0
            wk.sq = [SB(es, tag + "_sq%d" % i, [128, D]) for i in range(2)]; wk.bsq = S.bufs_n(2)
            wk.st = [SB(es, tag + "_st%d" % i, [128, 48]) for i in range(2)]; wk.bst = S.bufs_n(2)
            return wk

        def load_gbc(es, name, g_ap, nh, scale=None):
            t = SB(es, name, [128, nh, 64]); b = S.buf()
            S.dma("sp", t[:], g_ap.partition_broadcast(128).unsqueeze(1).broadcast_to([128, nh, 64]), writes=[b])
            if scale is not None:
                S.op("act", lambda e: e.mul(out=t[:], in_=t[:], mul=scale), [b], [b])
            return t, b

        with ExitStack() as es:
            Win = SB(es, "Win0", [128, 8, 2048], BF16); bW = S.buf()
            for kh in range(4):
                S.dma("pool", Win[:, kh * 2:(kh + 1) * 2, :], ev_w_in[0].rearrange("(k p) n -> p k n", p=128)[:, kh * 2:(kh + 1) * 2, :], writes=[bW])
            ncx = make_norm_ctx(es, 0, 0, "nA")
            hn = make_hn(es, "hA")
            qg, bqg = load_gbc(es, "qg0", na_q_g[0], 8, scale=0.125)
            kg, bkg = load_gbc(es, "kg0", na_k_g[0], 8)
            qg2 = qg[:].rearrange("p h d -> p (h d)"); kg2 = kg[:].rearrange("p h d -> p (h d)")
            xts = [SB(es, "xtA%d" % i, [128, D]) for i in range(2)]; bxs = S.bufs_n(2)
            hTs = [SB(es, "hTA%d" % i, [128, 8, 128], BF16) for i in range(2)]; bhTs = S.bufs_n(2)
            pcs = [PS(es, "pcA%d" % i, [128, 512]) for i in range(4)]; bpcs = S.bufs_n(4)
            of32 = [SB(es, "ofA%d" % i, [128, 512]) for i in range(2)]; bof = S.bufs_n(2)
            ob = [SB(es, "obA%d" % i, [128, 512], BF16) for i in range(4)]; bob = S.bufs_n(4)
            no = 0; nb = 0
            for i in range(NT0 // 128):
                cnd = 1 if i < 32 else 0
                src = xs[i * 128:(i + 1) * 128, :] if i < 32 else xp[(i - 32) * 128:(i - 31) * 128, :]
                xt = rr(xts, i); bx = rr(bxs, i); hT = rr(hTs, i); bhT = rr(bhTs, i)
                S.dma("sp", xt[:], src, writes=[bx])
                norm_tile(ncx, xt, bx, cnd, hT, bhT, 0)
                for cgi in range(4):
                    for k in range(8):
                        S.op("pe", lambda e, cgi=cgi, k=k, hT=hT: e.matmul(pcs[cgi][:], lhsT=hT[:, k, :], rhs=Win[:, k, cgi * 512:(cgi + 1) * 512], start=(k == 0), stop=(k == 7)), [bhT, bW], [bpcs[cgi]], signal=(k == 7))
                rows = slice(i * 128, (i + 1) * 128)
                prow = slice((i - 32) * 128, (i - 31) * 128)
                for cgi, (dst, gb, bg) in enumerate(((q0, qg2, bqg), (k0, kg2, bkg))):
                    o32 = rr(of32, no); bo = rr(bof, no); no += 1
                    head_norm(hn, pcs[cgi][:], 8, gb, bg, o32[:], bo, [bpcs[cgi]])
                    o16 = rr(ob, nb); b16 = rr(bob, nb); nb += 1
                    S.op("act", lambda e, o16=o16, o32=o32: e.copy(out=o16[:], in_=o32[:]), [bo], [b16])
                    S.dma("sp", dst[rows, :], o16[:], reads=[b16])
                    if cgi == 1 and i >= 32:
                        S.dma("sp", nak_o[prow, :], o32[:], reads=[bo])
                for cgi, dst in ((2, v0), (3, u0)):
                    o16 = rr(ob, nb); b16 = rr(bob, nb); nb += 1
                    S.op("act", lambda e, o16=o16, cgi=cgi: e.copy(out=o16[:], in_=pcs[cgi][:]), [bpcs[cgi]], [b16])
                    S.dma("sp", dst[rows, :], o16[:], reads=[b16])
                    if cgi == 2 and i >= 32:
                        o32 = rr(of32, no); bo = rr(bof, no); no += 1
                        S.op("dve", lambda e, o32=o32, cgi=cgi: e.tensor_copy(out=o32[:], in_=pcs[cgi][:]), [bpcs[cgi]], [bo])
                        S.dma("sp", nav_o[prow, :], o32[:], reads=[bo])
            S.flush()

        with ExitStack() as es:
            def pq(ap):
                return ap.rearrange("d (P gl) p -> (gl p) (d P)", gl=2)
            ARE = SB(es, "ARE", [128, 32]); AIM = SB(es, "AIM", [128, 32]); LDT = SB(es, "LDT", [128, 32])
            bare = S.buf(); baim = S.buf(); bldt = S.buf(); bh0 = S.buf()
            for qq in range(4):
                qsl = slice(qq * 8, (qq + 1) * 8)
                S.dma("sp", ARE[:, qsl], pq(ssm_a_re[0])[:, qsl], writes=[bare])
                S.dma("sp", AIM[:, qsl], pq(ssm_a_im[0])[:, qsl], writes=[baim])
                S.dma("sp", H0re[:, qsl], pq(sre_in)[:, qsl], writes=[bh0])
                S.dma("sp", H0im[:, qsl], pq(sim_in)[:, qsl], writes=[bh0])
            for gl in range(2):
                S.dma("sp", LDT[gl * 64:(gl + 1) * 64, :], ssm_log_dt[0].rearrange("d (P gl) -> gl (d P)", gl=2)[gl].partition_broadcast(64), writes=[bldt])
            BRE = SB(es, "BRE", [128, 32, 16]); BIM = SB(es, "BIM", [128, 32, 16]); bbre = S.buf(); bbim = S.buf()
            for qq in range(4):
                qsl = slice(qq * 8, (qq + 1) * 8)
                S.dma("sp", BRE[:, qsl, :], ssm_b_re[0].rearrange("d (P gl) p c -> (gl p) (d P) c", gl=2)[:, qsl, :], writes=[bbre])
                S.dma("sp", BIM[:, qsl, :], ssm_b_im[0].rearrange("d (P gl) p c -> (gl p) (d P) c", gl=2)[:, qsl, :], writes=[bbim])
            CBDr = SB(es, "CBDr", [32, 32, 128]); CBDi = SB(es, "CBDi", [32, 32, 128]); bcr = S.buf(); bci = S.buf()
            S.op("pool", lambda e: e.memset(CBDr[:], 0.0), [], [bcr])
            S.op("pool", lambda e: e.memset(CBDi[:], 0.0), [], [bci])
            for gl in range(2):
                S.dma("sp", CBDr[gl * 16:(gl + 1) * 16, :, gl * 64:(gl + 1) * 64], ssm_c_re[0].rearrange("d (P gl) c p -> gl c (d P) p", gl=2)[gl], writes=[bcr])
                S.dma("sp", CBDi[gl * 16:(gl + 1) * 16, :, gl * 64:(gl + 1) * 64], ssm_c_im[0].rearrange("d (P gl) c p -> gl c (d P) p", gl=2)[gl], writes=[bci])
            sm = {}
            for nm in ("DT", "ARDT", "AIDT", "F", "TF", "SIN", "COS", "ABR", "ABI", "DEN", "T1", "T2", "KR", "KI"):
                sm[nm] = (SB(es, "s_" + nm, [128, 32]), S.buf())
            TI = SB(es, "s_TI", [128, 32], I32); bti = S.buf()

            def tt(o, a, b, op, eng="dve"):
                S.op(eng, lambda e: e.tensor_tensor(out=sm[o][0][:], in0=sm[a][0][:] if isinstance(a, str) else a[0][:], in1=sm[b][0][:] if isinstance(b, str) else b[0][:], op=op),
                     [sm[a][1] if isinstance(a, str) else a[1], sm[b][1] if isinstance(b, str) else b[1]], [sm[o][1]])
            S.op("act", lambda e: e.activation(out=sm["DT"][0][:], in_=LDT[:], func=AF.Exp), [bldt], [sm["DT"][1]])
            tt("ARDT", (ARE, bare), "DT", ALU.mult)
            tt("AIDT", (AIM, baim), "DT", ALU.mult)
            S.op("act", lambda e: e.activation(out=RHO[:], in_=sm["ARDT"][0][:], func=AF.Exp), [sm["ARDT"][1]], [bh0])
            S.op("dve", lambda e: e.tensor_scalar(out=sm["F"][0][:], in0=sm["AIDT"][0][:], scalar1=1.0 / (2.0 * np.pi), scalar2=None, op0=ALU.mult), [sm["AIDT"][1]], [sm["F"][1]])
            S.op("dve", lambda e: e.tensor_copy(out=TI[:], in_=sm["F"][0][:]), [sm["F"][1]], [bti])
            S.op("dve", lambda e: e.tensor_copy(out=sm["TF"][0][:], in_=TI[:]), [bti], [sm["TF"][1]])
            S.op("dve", lambda e: e.tensor_tensor(out=FR[:], in0=sm["F"][0][:], in1=sm["TF"][0][:], op=ALU.subtract), [sm["F"][1], sm["TF"][1]], [bh0])
            sincos(es, "sc0", FR[:], [128, 32], sm["SIN"][0][:], sm["COS"][0][:], [bh0], [sm["SIN"][1], sm["COS"][1]])
            RHOb = (RHO, bh0)
            tt("ABR", RHOb, "COS", ALU.mult)
            tt("ABI", RHOb, "SIN", ALU.mult)
            S.op("dve", lambda e: e.tensor_scalar(out=sm["ABR"][0][:], in0=sm["ABR"][0][:], scalar1=-1.0, scalar2=None, op0=ALU.add), [sm["ABR"][1]], [sm["ABR"][1]])
            tt("DEN", (ARE, bare), (ARE, bare), ALU.mult)
            tt("T1", (AIM, baim), (AIM, baim), ALU.mult)
            tt("DEN", "DEN", "T1", ALU.add)
            S.op("dve", lambda e: e.reciprocal(out=sm["DEN"][0][:], in_=sm["DEN"][0][:]), [sm["DEN"][1]], [sm["DEN"][1]])
            tt("T1", "ABR", (ARE, bare), ALU.mult); tt("T2", "ABI", (AIM, baim), ALU.mult); tt("KR", "T1", "T2", ALU.add); tt("KR", "KR", "DEN", ALU.mult)
            tt("T1", "ABI", (ARE, bare), ALU.mult); tt("T2", "ABR", (AIM, baim), ALU.mult); tt("KI", "T1", "T2", ALU.subtract); tt("KI", "KI", "DEN", ALU.mult)
            BBr = SB(es, "BBr", [128, 32, 16]); BBi = SB(es, "BBi", [128, 32, 16]); Tb1 = SB(es, "Tb1", [128, 32, 16]); Tb2 = SB(es, "Tb2", [128, 32, 16])
            bBBr = S.buf(); bBBi = S.buf(); bT1 = S.buf(); bT2 = S.buf()
            krb = sm["KR"][0][:].unsqueeze(2).broadcast_to([128, 32, 16]); kib = sm["KI"][0][:].unsqueeze(2).broadcast_to([128, 32, 16])
            S.op("dve", lambda e: e.tensor_tensor(out=Tb1[:], in0=BRE[:], in1=krb, op=ALU.mult), [bbre, sm["KR"][1]], [bT1])
            S.op("dve", lambda e: e.tensor_tensor(out=Tb2[:], in0=BIM[:], in1=kib, op=ALU.mult), [bbim, sm["KI"][1]], [bT2])
            S.op("dve", lambda e: e.tensor_tensor(out=BBr[:], in0=Tb1[:], in1=Tb2[:], op=ALU.subtract), [bT1, bT2], [bBBr])
            S.op("dve", lambda e: e.tensor_tensor(out=Tb1[:], in0=BIM[:], in1=krb, op=ALU.mult), [bbim, sm["KR"][1]], [bT1])
            S.op("dve", lambda e: e.tensor_tensor(out=Tb2[:], in0=BRE[:], in1=kib, op=ALU.mult), [bbre, sm["KI"][1]], [bT2])
            S.op("dve", lambda e: e.tensor_tensor(out=BBi[:], in0=Tb1[:], in1=Tb2[:], op=ALU.add), [bT1, bT2], [bBBi])
            bCT = S.buf(); bBT = S.buf()
            S.op("pool", lambda e: e.memset(CTre[:], 0.0), [], [bCT])
            S.op("pool", lambda e: e.memset(CTim[:], 0.0), [], [bCT])
            SRC = [SB(es, "SRC%d" % i, [128, 128]) for i in range(4)]; bSRC = S.bufs_n(4)
            pT = [PS(es, "pT%d" % i, [128, 128]) for i in range(2)]; bpT = S.bufs_n(2)
            pC = [PS(es, "pC%d" % i, [128, 32]) for i in range(2)]; bpC = S.bufs_n(2)
            n = 0
            for q in range(32):
                P = q % 16; slot = P % 4
                for (BB, bBB, BT) in ((BBr, bBBr, BTre), (BBi, bBBi, BTim)):
                    src = rr(SRC, n); bs = rr(bSRC, n); pt = rr(pT, n); bp = rr(bpT, n); n += 1
                    S.op("pool", lambda e, src=src: e.memset(src[:], 0.0), [], [bs])
                    for gl in range(2):
                        S.op("pool", lambda e, src=src, gl=gl, BB=BB, q=q, slot=slot: e.tensor_copy(out=src[gl * 64:(gl + 1) * 64, slot * 32 + gl * 16: slot * 32 + gl * 16 + 16], in_=BB[gl * 64:(gl + 1) * 64, q, :]), [bBB], [bs])
                    S.op("pe", lambda e, src=src, pt=pt: e.matmul(pt[:], lhsT=src[:], rhs=identf[:], start=True, stop=True), [bs, bI], [bp])
                    S.op("act", lambda e, pt=pt, BT=BT, q=q: e.copy(out=BT[:, q, :], in_=pt[:]), [bp], [bBT])
                for (CBD, bcb, CT, sgn) in ((CBDr, bcr, CTre, 1.0), (CBDi, bci, CTim, -1.0)):
                    pc = rr(pC, n); bp = rr(bpC, n); n += 1
                    S.op("pe", lambda e, CBD=CBD, pc=pc, q=q: e.matmul(pc[:], lhsT=CBD[:, q, :], rhs=identf[0:32, 0:32], start=True, stop=True), [bcb, bI], [bp])
                    S.op("act", lambda e, pc=pc, CT=CT, q=q, slot=slot, sgn=sgn: e.mul(out=CT[:, q, slot * 32:(slot + 1) * 32], in_=pc[:], mul=sgn), [bp], [bCT])
            S.flush()

        with ExitStack() as es:
            J1 = SB(es, "J1", [128, 512]); bJ1 = S.buf()
            J1i = SB(es, "J1i", [128, 512], I32)
            S.op("pool", lambda e: e.iota(J1i[:], pattern=[[1, 512]], base=1, channel_multiplier=0), [], [bJ1])
            S.op("pool", lambda e: e.tensor_copy(out=J1[:], in_=J1i[:]), [bJ1], [bJ1])
            ONES = SB(es, "ONESf", [128, 512]); bON = S.buf()
            S.op("pool", lambda e: e.memset(ONES[:], 1.0), [], [bON])
            DCOL = SB(es, "DCOL", [128, 4]); bDC = S.buf()
            S.dma("sp", DCOL[:], ssm_d[0].rearrange("g c -> (g c)").rearrange("(k p) -> p k", p=128), writes=[bDC])
            BGL = SB(es, "BGL", [128, 4]); bBG = S.buf()
            S.dma("sp", BGL[:], ssm_b_glu[0].rearrange("(k p) -> p k", p=128), writes=[bBG])
            WGL = SB(es, "WGL", [128, 4, 512], BF16); bWG = S.buf()
            S.dma("pool", WGL[:], ssm_w_glu[0].rearrange("(k p) n -> p k n", p=128), writes=[bWG])
            uT = SB(es, "uT", [128, NT0], BF16); buT = S.buf()
            yacc = SB(es, "yacc", [128, NT0]); bya = S.buf()
            ygT = SB(es, "ygT", [128, 4, NT0], BF16); byg = S.buf()
            utl = [SB(es, "utl%d" % i, [128, 4, 128], BF16) for i in range(2)]; butl = S.bufs_n(2)
            ptr = PS(es, "ptrS", [128, 4, 128], BF16); bptr = S.buf()
            COS = [SB(es, "COS%d" % s, [128, 512]) for s in range(4)]; SIN = [SB(es, "SIN%d" % s, [128, 512]) for s in range(4)]
            RT = [SB(es, "RT%d" % s, [128, 512]) for s in range(4)]
            bCOS = S.bufs_n(4); bSIN = S.bufs_n(4); bRT = S.bufs_n(4)
            FT = SB(es, "FT", [128, 512]); bFT = S.buf()
            CAR = SB(es, "CAR", [128, 4, 2]); bCAR = S.bufs_n(4)
            FRE = SB(es, "FRE", [128, 2, 32]); FIM = SB(es, "FIM", [128, 2, 32]); bFRE = S.buf()
            pA = [PS(es, "pA%d" % i, [128, 512]) for i in range(2)]; bpA = S.bufs_n(2)
            pB = [PS(es, "pB%d" % i, [128, 512]) for i in range(2)]; bpB = S.bufs_n(2)
            pY = [PS(es, "pY%d" % i, [128, 512]) for i in range(2)]; bpY = S.bufs_n(2)
            W2 = lambda nm, k=2, dt=F32: ([SB(es, "%s%d" % (nm, i), [128, 512], dt) for i in range(k)], S.bufs_n(k))
            br_, bbr_ = W2("wbr"); bi_, bbi_ = W2("wbi"); wr_, bwr_ = W2("wwr"); wi_, bwi_ = W2("wwi")
            ta_, bta_ = W2("wta", 1); tb_, btb_ = W2("wtb", 1); gr_, bgr_ = W2("wgr", 1); gi_, bgi_ = W2("wgi", 1)
            tc_, btc_ = W2("wtc", 1); td_, btd_ = W2("wtd", 1); hr_, bhr_ = W2("whr", 1); hi_, bhi_ = W2("whi", 1)
            hrb = [SB(es, "hrb%d" % s, [128, 512], BF16) for s in range(4)]; hib = [SB(es, "hib%d" % s, [128, 512], BF16) for s in range(4)]
            bhrb = S.bufs_n(4); bhib = S.bufs_n(4)
            seqs = [(0, NS, True, -1), (NS, 256, False, 0), (NS + 256, 256, False, 1)]
            nck = 0; nypc = [0]
            for ctile in range(4):
                for i4 in range(NT0 // 512):
                    ut = rr(utl, i4); bu = rr(butl, i4)
                    S.dma("sp", ut[:], u0[i4 * 512:(i4 + 1) * 512, ctile * 128:(ctile + 1) * 128].rearrange("(a p) c -> p a c", p=128), writes=[bu])
                    for a in range(4):
                        S.op("pe", lambda e, ut=ut, a=a: e.transpose(out=ptr[:, a, :], in_=ut[:, a, :], identity=ident[:]), [bu, bI], [bptr], signal=(a == 3))
                    S.op("act", lambda e, i4=i4: e.copy(out=uT[:, i4 * 512:(i4 + 1) * 512], in_=ptr[:].rearrange("p a t -> p (a t)")), [bptr], [buT])
                S.op("dve", lambda e, ctile=ctile: e.tensor_scalar(out=yacc[:], in0=uT[:], scalar1=DCOL[:, ctile:ctile + 1], scalar2=None, op0=ALU.mult), [buT, bDC], [bya])
                for d in range(2):
                    qs = [d * 16 + ctile * 4 + s for s in range(4)]
                    for s in range(4):
                        q = qs[s]
                        S.op("dve", lambda e, q=q: e.tensor_scalar(out=FT[:], in0=J1[:], scalar1=FR[:, q:q + 1], scalar2=None, op0=ALU.mult), [bJ1], [bFT])
                        sincos(es, "sc_%d_%d_%d" % (ctile, d, s), FT[:], [128, 512], SIN[s][:], COS[s][:], [bFT], [bSIN[s], bCOS[s]])
                        S.op("dve", lambda e, q=q, s=s: e.tensor_scalar(out=RT[s][:], in0=ONES[:], scalar1=RHO[:, q:q + 1], scalar2=None, op0=ALU.mult), [bON], [bRT[s]])
                    items = []
                    for (base, L, has_init, sidx) in seqs:
                        n = min(512, L)
                        starts = list(range(0, L, n))
                        if d == 1:
                            starts = starts[::-1]
                        for ci, c0 in enumerate(starts):
                            lo = base + c0
                            for s in range(4):
                                q = qs[s]
                                it = {"pre": [], "post": []}
                                if ci == 0:
                                    if has_init:
                                        def init(s=s, q=q):
                                            S.op("act", lambda e: e.copy(out=CAR[:, s, 0:1], in_=H0re[:, q:q + 1]), [], [bCAR[s]])
                                            S.op("act", lambda e: e.copy(out=CAR[:, s, 1:2], in_=H0im[:, q:q + 1]), [], [bCAR[s]])
                                    else:
                                        def init(s=s):
                                            S.op("pool", lambda e: e.memset(CAR[:, s, :], 0.0), [], [bCAR[s]])
                                    it["pre"].append(init)
                                k_ = nck; nck += 1
                                A = rr(pA, k_); bA_ = rr(bpA, k_); B = rr(pB, k_); bB_ = rr(bpB, k_)
                                br = rr(br_, k_); bbr = rr(bbr_, k_); bi = rr(bi_, k_); bbi = rr(bbi_, k_)
                                wr = rr(wr_, k_); bwr = rr(bwr_, k_); wi = rr(wi_, k_); bwi = rr(bwi_, k_)

                                def rvs(t, n=n, d=d):
                                    if d == 0:
                                        return t[:, 0:n]
                                    return t[:, 0:n][:, ::-1]

                                def TT(eng, o, a, b, op, rd, wrb, n=n):
                                    S.op(eng, lambda e: e.tensor_tensor(out=o[:, 0:n], in0=a[:, 0:n], in1=b[:, 0:n], op=op), rd, wrb)

                                def s1(s=s, q=q, A=A, bA_=bA_, B=B, bB_=bB_, br=br, bbr=bbr, bi=bi, bbi=bbi, wr=wr, bwr=bwr, wi=wi, bwi=bwi, lo=lo, n=n, rvs=rvs, TT=TT):
                                    ta = ta_[0]; bta = bta_[0]; tb = tb_[0]; btb = btb_[0]
                                    cs_, sn_ = COS[s], SIN[s]
                                    S.op("pe", lambda e: e.matmul(A[:, 0:n], lhsT=BTre[:, q, :], rhs=uT[:, lo:lo + n], start=True, stop=True), [buT], [bA_])
                                    S.op("pe", lambda e: e.matmul(B[:, 0:n], lhsT=BTim[:, q, :], rhs=uT[:, lo:lo + n], start=True, stop=True), [buT], [bB_])
                                    S.op("act", lambda e: e.copy(out=br[:, 0:n], in_=rvs(A)), [bA_], [bbr])
                                    S.op("act", lambda e: e.copy(out=bi[:, 0:n], in_=rvs(B)), [bB_], [bbi])
                                    TT("pool", ta, br, cs_, ALU.mult, [bbr, bCOS[s]], [bta])
                                    TT("pool", tb, bi, sn_, ALU.mult, [bbi, bSIN[s]], [btb])
                                    TT("pool", wr, ta, tb, ALU.add, [bta, btb], [bwr])
                                    TT("pool", ta, bi, cs_, ALU.mult, [bbi, bCOS[s]], [bta])
                                    TT("pool", tb, br, sn_, ALU.mult, [bbr, bSIN[s]], [btb])
                                    TT("pool", wi, ta, tb, ALU.subtract, [bta, btb], [bwi])

                                def s2(s=s, q=q, wr=wr, bwr=bwr, wi=wi, bwi=bwi, n=n, rvs=rvs, TT=TT):
                                    gr = gr_[0]; bgr = bgr_[0]; gi = gi_[0]; bgi = bgi_[0]
                                    tc = tc_[0]; btc = btc_[0]; td = td_[0]; btd = btd_[0]; hr = hr_[0]; bhr = bhr_[0]; hi = hi_[0]; bhi = bhi_[0]
                                    cs_, sn_ = COS[s], SIN[s]
                                    S.op("dve", lambda e: e.tensor_tensor_scan(out=gr[:, 0:n], data0=RT[s][:, 0:n], data1=wr[:, 0:n], initial=CAR[:, s, 0:1], op0=ALU.mult, op1=ALU.add), [bRT[s], bwr, bCAR[s]], [bgr])
                                    S.op("dve", lambda e: e.tensor_tensor_scan(out=gi[:, 0:n], data0=RT[s][:, 0:n], data1=wi[:, 0:n], initial=CAR[:, s, 1:2], op0=ALU.mult, op1=ALU.add), [bRT[s], bwi, bCAR[s]], [bgi])
                                    TT("dve", tc, gr, cs_, ALU.mult, [bgr, bCOS[s]], [btc])
                                    TT("dve", td, gi, sn_, ALU.mult, [bgi, bSIN[s]], [btd])
                                    TT("dve", hr, tc, td, ALU.subtract, [btc, btd], [bhr])
                                    TT("dve", tc, gr, sn_, ALU.mult, [bgr, bSIN[s]], [btc])
                                    TT("dve", td, gi, cs_, ALU.mult, [bgi, bCOS[s]], [btd])
                                    TT("dve", hi, tc, td, ALU.add, [btc, btd], [bhi])
                                    S.op("act", lambda e: e.copy(out=CAR[:, s, 0:1], in_=hr[:, n - 1:n]), [bhr], [bCAR[s]])
                                    S.op("act", lambda e: e.copy(out=CAR[:, s, 1:2], in_=hi[:, n - 1:n]), [bhi], [bCAR[s]])
                                    S.op("act", lambda e: e.copy(out=hrb[s][:, 0:n], in_=rvs(hr)), [bhr], [bhrb[s]])
                                    S.op("act", lambda e: e.copy(out=hib[s][:, 0:n], in_=rvs(hi)), [bhi], [bhib[s]])
                                it["s1"] = s1; it["s2"] = s2
                                if s == 3:
                                    def tail(lo=lo, n=n):
                                        nonlocal_n = nypc[0]; nypc[0] += 1
                                        py = rr(pY, nonlocal_n); bpy = rr(bpY, nonlocal_n)
                                        for s_ in range(4):
                                            q_ = qs[s_]
                                            S.op("pe", lambda e, s_=s_, q_=q_: e.matmul(py[:, 0:n], lhsT=CTre[:, q_, :], rhs=hrb[s_][:, 0:n], start=(s_ == 0), stop=False), [bhrb[s_]], [bpy], signal=False)
                                            S.op("pe", lambda e, s_=s_, q_=q_: e.matmul(py[:, 0:n], lhsT=CTim[:, q_, :], rhs=hib[s_][:, 0:n], start=False, stop=(s_ == 3)), [bhib[s_]], [bpy], signal=(s_ == 3))
                                        S.op("dve", lambda e: e.tensor_tensor(out=yacc[:, lo:lo + n], in0=py[:, 0:n], in1=yacc[:, lo:lo + n], op=ALU.add), [bpy, bya], [bya])
                                    it["post"].append(tail)
                                if (not has_init) and ci == len(starts) - 1:
                                    def fin(s=s, q=q, sidx=sidx):
                                        S.op("act", lambda e: e.copy(out=FRE[:, sidx, q:q + 1], in_=CAR[:, s, 0:1]), [bCAR[s]], [bFRE])
                                        S.op("act", lambda e: e.copy(out=FIM[:, sidx, q:q + 1], in_=CAR[:, s, 1:2]), [bCAR[s]], [bFRE])
                                    it["post"].append(fin)
                                items.append(it)
                    for i_, it in enumerate(items):
                        if i_ == 0:
                            it["s1"]()
                        if i_ + 1 < len(items):
                            items[i_ + 1]["s1"]()
                        for f_ in it["pre"]:
                            f_()
                        it["s2"]()
                        for f_ in it["post"]:
                            f_()
                for i9 in range(NT0 // 512):
                    sl = slice(i9 * 512, (i9 + 1) * 512)
                    ta = ta_[0]; bta = bta_[0]; tb = tb_[0]; btb = btb_[0]
                    S.op("dve", lambda e, sl=sl: e.tensor_tensor(out=ta[:], in0=yacc[:, sl], in1=yacc[:, sl], op=ALU.mult), [bya], [bta])
                    S.op("dve", lambda e: e.tensor_scalar(out=ta[:], in0=ta[:], scalar1=0.044715, scalar2=1.0, op0=ALU.mult, op1=ALU.add), [bta], [bta])
                    S.op("dve", lambda e, sl=sl: e.tensor_tensor(out=ta[:], in0=ta[:], in1=yacc[:, sl], op=ALU.mult), [bta, bya], [bta])
                    S.op("act", lambda e: e.activation(out=tb[:], in_=ta[:], func=AF.Sigmoid, scale=1.5957691216), [bta], [btb])
                    S.op("dve", lambda e, sl=sl, ctile=ctile: e.tensor_tensor(out=ygT[:, ctile, sl], in0=tb[:], in1=yacc[:, sl], op=ALU.mult), [btb, bya], [byg])
            yo = [SB(es, "yo%d" % i, [128, 512], BF16) for i in range(2)]; byo = S.bufs_n(2)
            n = 0
            for cho in range(4):
                for i9 in range(NT0 // 512):
                    sl = slice(i9 * 512, (i9 + 1) * 512)
                    py = rr(pY, n); bpy = rr(bpY, n); o = rr(yo, n); bo = rr(byo, n); n += 1
                    for k in range(4):
                        S.op("pe", lambda e, py=py, k=k, cho=cho, sl=sl: e.matmul(py[:], lhsT=WGL[:, k, cho * 128:(cho + 1) * 128], rhs=ygT[:, k, sl], start=(k == 0), stop=(k == 3)), [bWG, byg], [bpy], signal=(k == 3))
                    tb = tb_[0]; btb = btb_[0]
                    S.op("act", lambda e, py=py, cho=cho: e.activation(out=tb[:], in_=py[:], func=AF.Sigmoid, bias=BGL[:, cho:cho + 1]), [bpy, bBG], [btb])
                    S.op("dve", lambda e, o=o, cho=cho, sl=sl: e.tensor_tensor(out=o[:], in0=tb[:], in1=ygT[:, cho, sl], op=ALU.mult), [btb, byg], [bo])
                    S.dma("sp", yssmT[cho * 128:(cho + 1) * 128, sl], o[:], reads=[bo])
            for sq in range(2):
                for qq in range(4):
                    qsl = slice(qq * 8, (qq + 1) * 8)
                    S.dma("sp", sre_o.rearrange("s d (P gl) p -> (gl p) s (d P)", gl=2)[:, sq, qsl], FRE[:, sq, qsl], reads=[bFRE])
                    S.dma("sp", sim_o.rearrange("s d (P gl) p -> (gl p) s (d P)", gl=2)[:, sq, qsl], FIM[:, sq, qsl], reads=[bFRE])
            S.flush()

        with ExitStack() as es:
            kT = SB(es, "kT", [128, 4, NS], BF16); bkT = S.buf()
            kTc = SB(es, "kTc", [128, 4, 256], BF16); bkTc = S.buf()
            Vctx = SB(es, "Vctx", [128, 2, 512], BF16); bVc = S.buf()
            NATB = SB(es, "NATB", [128, 8, 15, 64], BF16); bNB = S.buf()
            WoN = SB(es, "WoN", [64, 8, D], BF16); WoS = SB(es, "WoS", [128, 4, D], BF16); bWo = S.buf()
            S.dma("pool", Vctx[:], na_vc.rearrange("(c p) f -> p c f", p=128), writes=[bVc])
            for h in range(8):
                S.dma("pool", NATB[0:64, h, :, :], natb[h].rearrange("r wp w -> wp r w"), writes=[bNB])
                S.dma("pool", NATB[64:128, h, :, :], natb[h].rearrange("r wp w -> wp r w"), writes=[bNB])
            S.dma("pool", WoN[:], ev_w_out[0, 0:512, :].rearrange("(h d) n -> d h n", d=64), writes=[bWo])
            S.dma("pool", WoS[:], ev_w_out[0, 512:1024, :].rearrange("(k p) n -> p k n", p=128), writes=[bWo])
            G1 = [mod_bc(es, 0, cnd, 2, "G1c%d" % cnd) for cnd in range(2)]
            ktl = [SB(es, "ktl%d" % i, [128, 4, 512], BF16) for i in range(2)]; bktl = S.bufs_n(2)
            ptr = PS(es, "ptrC", [128, 4, 128], BF16); bptr = S.buf()
            sbp = [PS(es, "sbp%d" % i, [128, 512]) for i in range(2)]; bsbp = S.bufs_n(2)
            msc = [PS(es, "msc%d" % i, [128, 512]) for i in range(2)]; bmsc = S.bufs_n(2)
            po = PS(es, "poC", [128, D]); bpo = S.buf()
            def build_kT(src_rows_ap, dst, bdst, ntile4, n0):
                for i4 in range(ntile4):
                    kt = rr(ktl, n0 + i4); bk = rr(bktl, n0 + i4)
                    S.dma("sp", kt[:], src_rows_ap[i4 * 512:(i4 + 1) * 512, :].rearrange("(a p) f -> p a f", p=128), writes=[bk])
                    for j in range(4):
                        for a in range(4):
                            S.op("pe", lambda e, kt=kt, a=a, j=j: e.transpose(out=ptr[:, a, :], in_=kt[:, a, j * 128:(j + 1) * 128], identity=ident[:]), [bk], [bptr], signal=(a == 3))
                        S.op("act", lambda e, j=j, i4=i4, dst=dst: e.copy(out=dst[:, j, i4 * 512:(i4 + 1) * 512], in_=ptr[:].rearrange("p a t -> p (a t)")), [bptr], [bdst])
            build_kT(k0[0:NS, :], kT, bkT, 8, 0)
            kcl = SB(es, "kcl", [128, 2, 512], BF16); bkcl = S.buf()
            S.dma("pool", kcl[:], na_kc.rearrange("(c p) f -> p c f", p=128), writes=[bkcl])
            for j in range(4):
                for a in range(2):
                    S.op("pe", lambda e, a=a, j=j: e.transpose(out=ptr[:, a, :], in_=kcl[:, a, j * 128:(j + 1) * 128], identity=ident[:]), [bkcl], [bptr], signal=(a == 1))
                S.op("act", lambda e, j=j: e.copy(out=kTc[:, j, :], in_=ptr[:, 0:2, :].rearrange("p a t -> p (a t)")), [bptr], [bkTc])
            qtl = [SB(es, "qtl%d" % i, [128, 512], BF16) for i in range(2)]; bqtl = S.bufs_n(2)
            qT = [SB(es, "qT%d" % i, [128, 4, 128], BF16) for i in range(2)]; bqT = S.bufs_n(2)
            VB = [SB(es, "VB%d" % i, [64, 8, 512], BF16) for i in range(2)]; bVB = S.bufs_n(2)
            PB = [SB(es, "PB%d" % i, [128, 512], BF16) for i in range(2)]; bPB = S.bufs_n(2)
            PC = [SB(es, "PC%d" % i, [128, 128], BF16) for i in range(2)]; bPC = S.bufs_n(2)
            rec = [SB(es, "rec%d" % i, [64, 128]) for i in range(2)]; brec = S.bufs_n(2)
            OT = [SB(es, "OT%d" % i, [64, 8, 128], BF16) for i in range(2)]; bOT = S.bufs_n(2)
            YT = [SB(es, "YT%d" % i, [128, 4, 128], BF16) for i in range(2)]; bYT = S.bufs_n(2)
            xtl = [SB(es, "xtC%d" % i, [128, D]) for i in range(2)]; bxtl = S.bufs_n(2)
            tmo = [SB(es, "tmo%d" % i, [128, D]) for i in range(2)]; btmo = S.bufs_n(2)
            kTp = SB(es, "kTp", [128, 4, 256], BF16); bkTp = S.buf()
            Vp = SB(es, "Vp", [128, 2, 512], BF16); bVp = S.buf()
            cnt = {"h": 0, "b": 0, "v": 0}

            def load_qT(row0):
                i = cnt["b"]
                ql = rr(qtl, i); bq = rr(bqtl, i); qt = rr(qT, i); bqt = rr(bqT, i)
                S.dma("sp", ql[:], q0[row0:row0 + 128, :], writes=[bq])
                for j in range(4):
                    S.op("pe", lambda e, j=j, ql=ql: e.transpose(out=ptr[:, j, :], in_=ql[:, j * 128:(j + 1) * 128], identity=ident[:]), [bq], [bptr], signal=(j == 3))
                S.op("act", lambda e, qt=qt: e.copy(out=qt[:], in_=ptr[:]), [bptr], [bqt])
                return qt, bqt

            def finish_block(ot, bot, col0, x_src, cnd, dst_rows):
                i = cnt["b"]; cnt["b"] += 1
                yt = rr(YT, i); byt = rr(bYT, i); xt = rr(xtl, i); bx = rr(bxtl, i); tm = rr(tmo, i); btm = rr(btmo, i)
                S.dma("sp", yt[:], yssmT[:, col0:col0 + 128].rearrange("(k p) t -> p k t", p=128), writes=[byt])
                S.dma("sp", xt[:], x_src, writes=[bx])
                for cgi in range(2):
                    cs = slice(cgi * 512, (cgi + 1) * 512)
                    for h in range(8):
                        S.op("pe", lambda e, h=h, cs=cs: e.matmul(po[:, cs], lhsT=ot[0:64, h, :], rhs=WoN[0:64, h, cs], start=(h == 0), stop=False), [bot, bWo], [bpo], signal=False)
                    for k in range(4):
                        S.op("pe", lambda e, k=k, cs=cs: e.matmul(po[:, cs], lhsT=yt[:, k, :], rhs=WoS[:, k, cs], start=False, stop=(k == 3)), [byt, bWo], [bpo], signal=(k == 3))
                Gt, bG = G1[cnd]
                for cgi in range(2):
                    cs = slice(cgi * 512, (cgi + 1) * 512)
                    S.op("dve", lambda e, cs=cs: e.tensor_tensor(out=tm[:, cs], in0=po[:, cs], in1=Gt[:, cs], op=ALU.mult), [bpo, bG], [btm])
                S.op("pool", lambda e: e.tensor_tensor(out=tm[:], in0=tm[:], in1=xt[:], op=ALU.add), [btm, bx], [btm])
                S.dma("sp", dst_rows, tm[:], reads=[btm])

            for m in range(32):
                qt, bqt = load_qT(m * 128)
                ot = rr(OT, m); bot = rr(bOT, m)
                for r2 in range(2):
                    r = 2 * m + r2
                    rs = min(max(r - 4, 0), 56); v = r - rs
                    vb = rr(VB, cnt["v"]); bvb = rr(bVB, cnt["v"]); cnt["v"] += 1
                    S.dma("sp", vb[:], v0[rs * 64:rs * 64 + 512, :].rearrange("(i w) f -> w i f", w=64), writes=[bvb])
                    qc = slice(r2 * 64, (r2 + 1) * 64)
                    NSTG = 9
                    for h in range(8 if NSTG >= 2 else 0):
                        e2 = h % 2; j = h // 2; pp = slice(e2 * 64, (e2 + 1) * 64)
                        n = cnt["h"]; cnt["h"] += 1
                        sp_ = rr(sbp, n); bsp = rr(bsbp, n); ms = rr(msc, n); bms = rr(bmsc, n)
                        pb = rr(PB, n); bpb = rr(bPB, n); pc = rr(PC, n); bpc = rr(bPC, n); rc = rr(rec, n); brc = rr(brec, n)
                        sband = sp_[0:64, :].rearrange("p (i w) -> p i w", w=64)
                        sctx = ms[:, 0:128].rearrange("p (c w) -> p c w", w=64)
                        od = ms[0:64, 128:256].rearrange("p (c w) -> p c w", w=64)
                        for i in range(8):
                            S.op("pe", lambda e, i=i, j=j, pp=pp, sband=sband, rs=rs, qt=qt, qc=qc: e.matmul(sband[:, i, :], lhsT=kT[pp, j, (rs + i) * 64:(rs + i + 1) * 64], rhs=qt[pp, j, qc], start=True, stop=False), [bkT, bqt], [bsp], signal=False)
                            if NSTG >= 3:
                                S.op("pe", lambda e, i=i, h=h, v=v, sband=sband: e.matmul(sband[:, i, :], lhsT=ident[(h % 2) * 64:(h % 2) * 64 + 64, (h % 2) * 64:(h % 2) * 64 + 64], rhs=NATB[(h % 2) * 64:(h % 2) * 64 + 64, h, i - v + 7, :], start=False, stop=True), [bNB], [bsp], signal=(i == 7))
                        for cc in range(2 if NSTG >= 4 else 0):
                            S.op("pe", lambda e, cc=cc, j=j, pp=pp, sctx=sctx, qt=qt, qc=qc: e.matmul(sctx[:, cc, :], lhsT=kTc[pp, j, cc * 128:(cc + 1) * 128], rhs=qt[pp, j, qc], start=True, stop=True), [bkTc, bqt], [bms], signal=(cc == 1))
                        if NSTG < 5:
                            continue
                        S.op("act", lambda e, pb=pb, sp_=sp_: e.activation(out=pb[0:64, :], in_=sp_[0:64, :], func=AF.Exp), [bsp], [bpb])
                        S.op("act", lambda e, pc=pc, ms=ms: e.activation(out=pc[:], in_=ms[:, 0:128], func=AF.Exp), [bms], [bpc])
                        if NSTG < 6:
                            continue
                        pbv = pb[0:64, :].rearrange("p (i w) -> p i w", w=64)
                        pcv = pc[:].rearrange("p (c w) -> p c w", w=64)
                        for part in range(2):
                            for i in range(8):
                                lh = (lambda vb=vb, i=i, h=h: vb[:, i, h * 64:(h + 1) * 64]) if part == 0 else (lambda: ones_b[0:64, 0:64])
                                S.op("pe", lambda e, part=part, i=i, lh=lh, pbv=pbv, od=od: e.matmul(od[:, part, :], lhsT=lh(), rhs=pbv[:, i, :], start=(i == 0), stop=False), [bvb, bpb], [bms], signal=False)
                            for cc in range(2):
                                lh = (lambda cc=cc, h=h: Vctx[:, cc, h * 64:(h + 1) * 64]) if part == 0 else (lambda: ones_b[:, 0:64])
                                S.op("pe", lambda e, part=part, cc=cc, lh=lh, pcv=pcv, od=od: e.matmul(od[:, part, :], lhsT=lh(), rhs=pcv[:, cc, :], start=False, stop=(cc == 1)), [bVc, bpc], [bms], signal=(cc == 1 and part == 1))
                        if NSTG < 7:
                            continue
                        S.op("act", lambda e, rc=rc, od=od: e.copy(out=rc[:, 0:64], in_=od[:, 1, :]), [bms], [brc])
                        S.op("dve", lambda e, rc=rc: e.reciprocal(out=rc[:, 0:64], in_=rc[:, 0:64]), [brc], [brc])
                        S.op("dve", lambda e, rc=rc, od=od, h=h, qc=qc, ot=ot: e.tensor_tensor(out=ot[0:64, h, qc], in0=od[:, 0, :], in1=rc[:, 0:64], op=ALU.mult), [bms, brc], [bot])
                if NSTG >= 8:
                    finish_block(ot, bot, m * 128, xs[m * 128:(m + 1) * 128, :], 1, x1[m * 128:(m + 1) * 128, :])

            for sq in range(2):
                base = NS + 256 * sq
                kt = rr(ktl, sq); bk = rr(bktl, sq)
                S.dma("sp", kt[:, 0:2, :], k0[base:base + 256, :].rearrange("(a p) f -> p a f", p=128), writes=[bk])
                for j in range(4):
                    for a in range(2):
                        S.op("pe", lambda e, kt=kt, a=a, j=j: e.transpose(out=ptr[:, a, :], in_=kt[:, a, j * 128:(j + 1) * 128], identity=ident[:]), [bk], [bptr], signal=(a == 1))
                    S.op("act", lambda e, j=j: e.copy(out=kTp[:, j, :], in_=ptr[:, 0:2, :].rearrange("p a t -> p (a t)")), [bptr], [bkTp])
                S.dma("sp", Vp[:], v0[base:base + 256, :].rearrange("(c p) f -> p c f", p=128), writes=[bVp])
                STG = 9
                for qb in range(2 if STG >= 2 else 0):
                    row0 = base + qb * 128
                    qt, bqt = load_qT(row0)
                    ot = rr(OT, qb); bot = rr(bOT, qb)
                    for h in range(8 if STG >= 3 else 0):
                        e2 = h % 2; j = h // 2; pp = slice(e2 * 64, (e2 + 1) * 64)
                        n = cnt["h"]; cnt["h"] += 1
                        sp_ = rr(sbp, n); bsp = rr(bsbp, n); ms = rr(msc, n); bms = rr(bmsc, n)
                        pb = rr(PB, n); bpb = rr(bPB, n); rc = rr(rec, n); brc = rr(brec, n)
                        sv = sp_[:, 0:256].rearrange("p (c t) -> p c t", t=128)
                        od = ms[0:64, 0:256].rearrange("p (c t) -> p c t", t=128)
                        for cc in range(2):
                            S.op("pe", lambda e, cc=cc, j=j, pp=pp, sv=sv, qt=qt: e.matmul(sv[:, cc, :], lhsT=kTp[pp, j, cc * 128:(cc + 1) * 128], rhs=qt[pp, j, :], start=True, stop=True), [bkTp, bqt], [bsp], signal=(cc == 1))
                        if STG < 4:
                            continue
                        S.op("act", lambda e, pb=pb, sp_=sp_: e.activation(out=pb[:, 0:256], in_=sp_[:, 0:256], func=AF.Exp), [bsp], [bpb])
                        if STG < 5:
                            continue
                        pbv = pb[:, 0:256].rearrange("p (c t) -> p c t", t=128)
                        for part in range(2):
                            for cc in range(2):
                                lh = (lambda cc=cc, h=h: Vp[:, cc, h * 64:(h + 1) * 64]) if part == 0 else (lambda: ones_b[:, 0:64])
                                S.op("pe", lambda e, part=part, cc=cc, lh=lh, pbv=pbv, od=od: e.matmul(od[:, part, :], lhsT=lh(), rhs=pbv[:, cc, :], start=(cc == 0), stop=(cc == 1)), [bVp, bpb], [bms], signal=(cc == 1 and part == 1))
                        if STG < 6:
                            continue
                        S.op("act", lambda e, rc=rc, od=od: e.copy(out=rc[:, :], in_=od[:, 1, :]), [bms], [brc])
                        S.op("dve", lambda e, rc=rc: e.reciprocal(out=rc[:, :], in_=rc[:, :]), [brc], [brc])
                        S.op("dve", lambda e, rc=rc, od=od, h=h, ot=ot: e.tensor_tensor(out=ot[0:64, h, :], in0=od[:, 0, :], in1=rc[:, :], op=ALU.mult), [bms, brc], [bot])
                    prow = 256 * sq + qb * 128
                    if STG < 7:
                        continue
                    finish_block(ot, bot, row0, xp[prow:prow + 128, :], 0, x1[row0:row0 + 128, :])
            S.flush()

        def ffn_phase(l, src, seqs, tag):
            with ExitStack() as es:
                Wdn = SB(es, tag + "Wdn", [128, NJ, D], BF16); bWdn = S.buf()
                for jj in range(2):
                    S.dma("sp", Wdn[:, jj * 11:(jj + 1) * 11, :], wdn_bf[l].rearrange("(j p) n -> p j n", p=128)[:, jj * 11:(jj + 1) * 11, :], writes=[bWdn])
                cw = SB(es, tag + "cw", [128, 3, NJ]); cb = SB(es, tag + "cb", [128, NJ]); bcw = S.buf()
                for i in range(3):
                    for (ja, jb) in ((0, 8), (8, 16), (16, NJ)):
                        S.dma("sp", cw[:, i, ja:jb], ffn_conv_w[l, i].rearrange("(j p) -> p j", p=128)[:, ja:jb], writes=[bcw])
                for (ja, jb) in ((0, 8), (8, 16), (16, NJ)):
                    S.dma("sp", cb[:, ja:jb], ffn_conv_b[l].rearrange("(j p) -> p j", p=128)[:, ja:jb], writes=[bcw])
                ncx = make_norm_ctx(es, l, 1, tag + "n")
                G2 = [mod_bc(es, l, cnd, 5, "%sG2c%d" % (tag, cnd)) for cnd in range(2)]
                hT = SB(es, tag + "hT", [128, 8, 512], BF16); bhT = S.buf()
                hid = SB(es, tag + "hid", [128, NJ, 512], BF16); bhid = S.buf()
                Wg = [SB(es, "%sWg%d" % (tag, i), [128, 8, 256], BF16) for i in range(2)]; bWg = S.bufs_n(2)
                Wv = [SB(es, "%sWv%d" % (tag, i), [128, 8, 256], BF16) for i in range(2)]; bWv = S.bufs_n(2)
                pg = [PS(es, "%spg%d" % (tag, i), [128, 512]) for i in range(2)]; bpg = S.bufs_n(2)
                pv = [PS(es, "%spv%d" % (tag, i), [128, 512]) for i in range(2)]; bpv = S.bufs_n(2)
                po = PS(es, tag + "po", [128, D]); bpo = S.buf()
                acc = [SB(es, "%sacc%d" % (tag, i), [128, 512]) for i in range(2)]; bacc = S.bufs_n(2)
                sg = [SB(es, "%ssg%d" % (tag, i), [128, 512]) for i in range(2)]; bsg = S.bufs_n(2)
                xn = [SB(es, "%sxn%d" % (tag, i), [128, D]) for i in range(2)]; bxn = S.bufs_n(2)
                xr = [SB(es, "%sxr%d" % (tag, i), [128, D]) for i in range(2)]; bxr = S.bufs_n(2)
                tm = [SB(es, "%stm%d" % (tag, i), [128, D]) for i in range(2)]; btm = S.bufs_n(2)
                wsrc = wup_bf[l].rearrange("(k p) n -> p k n", p=128)
                c = {"x": 0, "g": 0, "j": 0, "o": 0}
                for (row_base, L, cnd, dst_fn) in seqs:
                    t0 = 0
                    while t0 < L:
                        T = min(510, L - t0)
                        span = T + 2
                        c_lo = 1 if t0 == 0 else 0
                        c_hi = span - (1 if t0 + T == L else 0)
                        if c_lo == 1:
                            S.op("pool", lambda e: e.memset(hT[:, :, 0:1], 0.0), [], [bhT])
                        if c_hi == span - 1:
                            S.op("pool", lambda e, span=span: e.memset(hT[:, :, span - 1:span], 0.0), [], [bhT])
                        cc = c_lo
                        while cc < c_hi:
                            nr = min(128, c_hi - cc)
                            tok = t0 - 1 + cc
                            xt = rr(xn, c["x"]); bx = rr(bxn, c["x"]); c["x"] += 1
                            S.dma("sp", xt[0:nr, :], src[row_base + tok:row_base + tok + nr, :], writes=[bx])
                            norm_tile(ncx, xt, bx, cnd, hT, bhT, cc, nrows=nr)
                            cc += nr
                        for jg in range(NJ // 2):
                            wg = rr(Wg, c["g"]); bwg = rr(bWg, c["g"]); wv = rr(Wv, c["g"]); bwv = rr(bWv, c["g"]); c["g"] += 1
                            S.dma("sp", wg[:], wsrc[:, :, jg * 256:(jg + 1) * 256], writes=[bwg])
                            S.dma("sp", wv[:], wsrc[:, :, DFF + jg * 256:DFF + (jg + 1) * 256], writes=[bwv])
                            for jj in range(2):
                                j = jg * 2 + jj
                                n = c["j"]; c["j"] += 1
                                g_ = rr(pg, n); bg_ = rr(bpg, n); v_ = rr(pv, n); bv_ = rr(bpv, n)
                                ac = rr(acc, n); bac = rr(bacc, n); s_ = rr(sg, n); bs_ = rr(bsg, n)
                                for k in range(8):
                                    S.op("pe", lambda e, k=k, jj=jj, g_=g_, wg=wg, span=span: e.matmul(g_[:, 0:span], lhsT=wg[:, k, jj * 128:(jj + 1) * 128], rhs=hT[:, k, 0:span], start=(k == 0), stop=(k == 7)), [bwg, bhT], [bg_], signal=(k == 7))
                                for k in range(8):
                                    S.op("pe", lambda e, k=k, jj=jj, v_=v_, wv=wv, T=T: e.matmul(v_[:, 0:T], lhsT=wv[:, k, jj * 128:(jj + 1) * 128], rhs=hT[:, k, 1:T + 1], start=(k == 0), stop=(k == 7)), [bwv, bhT], [bv_], signal=(k == 7))
                                S.op("act", lambda e, ac=ac, g_=g_, j=j, T=T: e.activation(out=ac[:, 0:T], in_=g_[:, 1:T + 1], func=AF.Identity, scale=cw[:, 1, j:j + 1], bias=cb[:, j:j + 1]), [bg_, bcw], [bac])
                                S.op("dve", lambda e, ac=ac, g_=g_, j=j, T=T: e.scalar_tensor_tensor(out=ac[:, 0:T], in0=g_[:, 0:T], scalar=cw[:, 0, j:j + 1], in1=ac[:, 0:T], op0=ALU.mult, op1=ALU.add), [bg_, bcw, bac], [bac])
                                S.op("dve", lambda e, ac=ac, g_=g_, j=j, T=T: e.scalar_tensor_tensor(out=ac[:, 0:T], in0=g_[:, 2:T + 2], scalar=cw[:, 2, j:j + 1], in1=ac[:, 0:T], op0=ALU.mult, op1=ALU.add), [bg_, bcw, bac], [bac])
                                S.op("act", lambda e, ac=ac, s_=s_, T=T: e.activation(out=s_[:, 0:T], in_=ac[:, 0:T], func=AF.Silu), [bac], [bs_])
                                S.op("dve", lambda e, s_=s_, v_=v_, j=j, T=T: e.tensor_tensor(out=hid[:, j, 0:T], in0=v_[:, 0:T], in1=s_[:, 0:T], op=ALU.mult), [bv_, bs_], [bhid])
                        Gt, bG = G2[cnd]
                        for mb in range((T + 127) // 128):
                            nt = min(128, T - mb * 128)
                            o = c["o"]; c["o"] += 1
                            x2_ = rr(xr, o); bx2 = rr(bxr, o); tmo_ = rr(tm, o); btmo = rr(btm, o)
                            S.dma("sp", x2_[0:nt, :], src[row_base + t0 + mb * 128:row_base + t0 + mb * 128 + nt, :], writes=[bx2])
                            for cgi in range(2):
                                cs = slice(cgi * 512, (cgi + 1) * 512)
                                for j in range(NJ):
                                    S.op("pe", lambda e, j=j, cs=cs, mb=mb, nt=nt: e.matmul(po[0:nt, cs], lhsT=hid[:, j, mb * 128:mb * 128 + nt], rhs=Wdn[:, j, cs], start=(j == 0), stop=(j == NJ - 1)), [bhid, bWdn], [bpo], signal=(j == NJ - 1))
                            for cgi in range(2):
                                cs = slice(cgi * 512, (cgi + 1) * 512)
                                S.op("dve", lambda e, cs=cs, nt=nt, tmo_=tmo_, Gt=Gt: e.tensor_tensor(out=tmo_[0:nt, cs], in0=po[0:nt, cs], in1=Gt[0:nt, cs], op=ALU.mult), [bpo, bG], [btmo])
                            S.op("pool", lambda e, nt=nt, tmo_=tmo_, x2_=x2_: e.tensor_tensor(out=tmo_[0:nt, :], in0=tmo_[0:nt, :], in1=x2_[0:nt, :], op=ALU.add), [btmo, bx2], [btmo])
                            S.dma("sp", dst_fn(t0 + mb * 128, nt), tmo_[0:nt, :], reads=[btmo])
                        t0 += T
                S.flush()

        ffn_phase(0, x1, [(0, NS, 1, lambda t, n: x2[t:t + n, :]),
                          (NS, 256, 0, lambda t, n: x2[NS + t:NS + t + n, :]),
                          (NS + 256, 256, 0, lambda t, n: x2[NS + 256 + t:NS + 256 + t + n, :])], "f0")

        with ExitStack() as es:
            Win = SB(es, "Win1", [128, 8, 1536], BF16); bW = S.buf()
            for kh in range(4):
                S.dma("pool", Win[:, kh * 2:(kh + 1) * 2, :], od_w_in[0].rearrange("(k p) n -> p k n", p=128)[:, kh * 2:(kh + 1) * 2, :], writes=[bW])
            ncx = make_norm_ctx(es, 1, 0, "nB")
            hn = make_hn(es, "hB")
            qg, bqg = load_gbc(es, "qg1", gqa_q_g[0], 16, scale=0.125)
            kg, bkg = load_gbc(es, "kg1", gqa_k_g[0], 4)
            qg2 = qg[:].rearrange("p h d -> p (h d)"); kg2 = kg[:].rearrange("p h d -> p (h d)")
            wsl = SB(es, "wsl", [128, 2]); bwsl = S.buf()
            S.dma("sp", wsl[:], wsel, writes=[bwsl])
            fqi = SB(es, "fqi", [128, 16], I32); fq = SB(es, "fq", [128, 16]); bfq = S.buf()
            S.op("pool", lambda e: e.iota(fqi[:], pattern=[[1, 16]], base=0, channel_multiplier=0), [], [bfq])
            S.op("pool", lambda e: e.tensor_copy(out=fq[:], in_=fqi[:]), [bfq], [bfq])
            S.op("act", lambda e: e.activation(out=fq[:], in_=fq[:], func=AF.Exp, scale=-float(np.log(10000.0)) / 16.0), [bfq], [bfq])
            S.op("act", lambda e: e.mul(out=fq[:], in_=fq[:], mul=1.0 / (2.0 * np.pi)), [bfq], [bfq])
            xts = [SB(es, "xtB%d" % i, [128, D]) for i in range(2)]; bxs = S.bufs_n(2)
            xbs = [SB(es, "xbB%d" % i, [128, D]) for i in range(2)]; bxb = S.bufs_n(2)
            hTs = [SB(es, "hTB%d" % i, [128, 8, 128], BF16) for i in range(2)]; bhTs = S.bufs_n(2)
            pq_ = [PS(es, "pqB%d" % i, [128, 512]) for i in range(2)]; bpq = S.bufs_n(2)
            pkv = PS(es, "pkvB", [128, 512]); bpkv = S.buf()
            o32 = [SB(es, "o32B%d" % i, [128, D]) for i in range(2)]; bo32 = S.bufs_n(2)
            o16 = [SB(es, "o16B%d" % i, [128, D], BF16) for i in range(2)]; bo16 = S.bufs_n(2)
            k32 = [SB(es, "k32B%d" % i, [128, 256]) for i in range(2)]; bk32 = S.bufs_n(2)
            k16 = [SB(es, "k16B%d" % i, [128, 256], BF16) for i in range(2)]; bk16 = S.bufs_n(2)
            v16 = [SB(es, "v16B%d" % i, [128, 256], BF16) for i in range(2)]; bv16 = S.bufs_n(2)
            v32 = [SB(es, "v32B%d" % i, [128, 256]) for i in range(2)]; bv32 = S.bufs_n(2)
            pos = [SB(es, "posB%d" % i, [128, 2]) for i in range(2)]; bpos = S.bufs_n(2)
            ang = SB(es, "angB", [128, 2, 16]); bang = S.buf()
            SINt = [SB(es, "sinB%d" % i, [128, 2, 16]) for i in range(2)]; COSt = [SB(es, "cosB%d" % i, [128, 2, 16]) for i in range(2)]
            bSC = S.bufs_n(2)
            r1 = SB(es, "r1B", [128, 16, 2, 16]); r2 = SB(es, "r2B", [128, 16, 2, 16]); br1 = S.buf(); br2 = S.buf()
            cn = {"t": 0, "r": 0}

            def rope_tables(pos_rows_ap):
                i = cn["r"]; cn["r"] += 1
                p_ = rr(pos, i); bp = rr(bpos, i); sn = rr(SINt, i); cs = rr(COSt, i); bsc = rr(bSC, i)
                S.dma("sp", p_[:], pos_rows_ap, writes=[bp])
                for a in range(2):
                    S.op("dve", lambda e, a=a, p_=p_: e.tensor_scalar(out=ang[:, a, :], in0=fq[:], scalar1=p_[:, a:a + 1], scalar2=None, op0=ALU.mult), [bfq, bp], [bang])
                sincos(es, "scB", ang[:], [128, 2, 16], sn[:], cs[:], [bang], [bsc])
                return sn, cs, bsc

            def rope_apply(src32, bsrc, nh, dst16, bdst, sn, cs, bsc):
                xv = src32.rearrange("p (h a b i) -> p h a b i", a=2, b=2, i=16)
                ov = dst16.rearrange("p (h a b i) -> p h a b i", a=2, b=2, i=16)
                x1v = xv[:, :, :, 0, :]; x2v = xv[:, :, :, 1, :]
                cb_ = cs[:].unsqueeze(1).broadcast_to([128, nh, 2, 16]); sb_ = sn[:].unsqueeze(1).broadcast_to([128, nh, 2, 16])
                t1 = r1[:, 0:nh, :, :]; t2 = r2[:, 0:nh, :, :]
                S.op("dve", lambda e: e.tensor_tensor(out=t1, in0=x1v, in1=cb_, op=ALU.mult), [bsrc, bsc], [br1])
                S.op("pool", lambda e: e.tensor_tensor(out=t2, in0=x2v, in1=sb_, op=ALU.mult), [bsrc, bsc], [br2])
                S.op("dve", lambda e: e.tensor_tensor(out=ov[:, :, :, 0, :], in0=t1, in1=t2, op=ALU.subtract), [br1, br2], [bdst])
                S.op("dve", lambda e: e.tensor_tensor(out=t1, in0=x1v, in1=sb_, op=ALU.mult), [bsrc, bsc], [br1])
                S.op("pool", lambda e: e.tensor_tensor(out=t2, in0=x2v, in1=cb_, op=ALU.mult), [bsrc, bsc], [br2])
                S.op("dve", lambda e: e.tensor_tensor(out=ov[:, :, :, 1, :], in0=t1, in1=t2, op=ALU.add), [br1, br2], [bdst])

            def proj(hT, bhT, want_q, want_kv):
                if want_q:
                    for half in range(2):
                        for k in range(8):
                            S.op("pe", lambda e, half=half, k=k: e.matmul(pq_[half][:], lhsT=hT[:, k, :], rhs=Win[:, k, half * 512:(half + 1) * 512], start=(k == 0), stop=(k == 7)), [bhT, bW], [bpq[half]], signal=(k == 7))
                if want_kv:
                    for k in range(8):
                        S.op("pe", lambda e, k=k: e.matmul(pkv[:], lhsT=hT[:, k, :], rhs=Win[:, k, 1024:1536], start=(k == 0), stop=(k == 7)), [bhT, bW], [bpkv], signal=(k == 7))

            def q_part(rows1, rope):
                i = cn["t"]
                o3 = rr(o32, i); b3 = rr(bo32, i); o6 = rr(o16, i); b6 = rr(bo16, i)
                for half in range(2):
                    head_norm(hn, pq_[half][:], 8, qg2[:, half * 512:(half + 1) * 512], bqg, o3[:, half * 512:(half + 1) * 512], b3, [bpq[half]])
                if rope is not None:
                    rope_apply(o3[:], b3, 16, o6[:], b6, *rope)
                else:
                    S.op("act", lambda e: e.copy(out=o6[:], in_=o3[:]), [b3], [b6])
                S.dma("sp", q1[rows1, :], o6[:], reads=[b6])

            def kv_part(rows0, rope, prow=None):
                i = cn["t"]
                k3 = rr(k32, i); bk3 = rr(bk32, i); k6 = rr(k16, i); bk6 = rr(bk16, i); v6 = rr(v16, i); bv6 = rr(bv16, i)
                head_norm(hn, pkv[:, 0:256], 4, kg2, bkg, k3[:], bk3, [bpkv])
                if rope is not None:
                    rope_apply(k3[:], bk3, 4, k6[:], bk6, *rope)
                else:
                    S.op("act", lambda e: e.copy(out=k6[:], in_=k3[:]), [bk3], [bk6])
                S.dma("sp", k1[rows0, :], k6[:], reads=[bk6])
                S.op("act", lambda e: e.copy(out=v6[:], in_=pkv[:, 256:512]), [bpkv], [bv6])
                S.dma("sp", v1[rows0, :], v6[:], reads=[bv6])
                if prow is not None:
                    v3 = rr(v32, i); bv3 = rr(bv32, i)
                    S.dma("sp", gk_o[prow, :], k3[:], reads=[bk3])
                    S.op("dve", lambda e: e.tensor_copy(out=v3[:], in_=pkv[:, 256:512]), [bpkv], [bv3])
                    S.dma("sp", gv_o[prow, :], v3[:], reads=[bv3])

            for i in range(32):
                n = cn["t"]
                xt = rr(xts, n); bx = rr(bxs, n); hT = rr(hTs, n); bhT = rr(bhTs, n)
                S.dma("sp", xt[:], x2[i * 128:(i + 1) * 128, :], writes=[bx])
                norm_tile(ncx, xt, bx, 1, hT, bhT, 0)
                proj(hT, bhT, False, True)
                rope = rope_tables(pos_all[i * 128:(i + 1) * 128, :])
                kv_part(slice(i * 128, (i + 1) * 128), rope)
                cn["t"] += 1
            for lt in range(NW // 128):
                n = cn["t"]
                xt = rr(xts, n); bx = rr(bxs, n); xb = rr(xbs, n); bxb_ = rr(bxb, n); hT = rr(hTs, n); bhT = rr(bhTs, n)
                S.dma("sp", xt[:], x2[lt * 128:(lt + 1) * 128, :], writes=[bx])
                S.dma("sp", xb[:], x2[1920 + lt * 128:1920 + (lt + 1) * 128, :], writes=[bxb_])
                S.op("dve", lambda e, xt=xt: e.tensor_scalar(out=xt[:], in0=xt[:], scalar1=wsl[:, 0:1], scalar2=None, op0=ALU.mult), [bx, bwsl], [bx])
                S.op("dve", lambda e, xt=xt, xb=xb: e.scalar_tensor_tensor(out=xt[:], in0=xb[:], scalar=wsl[:, 1:2], in1=xt[:], op0=ALU.mult, op1=ALU.add), [bx, bxb_, bwsl], [bx])
                S.dma("sp", x2w[lt * 128:(lt + 1) * 128, :], xt[:], reads=[bx])
                norm_tile(ncx, xt, bx, 1, hT, bhT, 0)
                proj(hT, bhT, True, False)
                rope = rope_tables(pos_win[lt * 128:(lt + 1) * 128, :])
                q_part(slice(lt * 128, (lt + 1) * 128), rope)
                cn["t"] += 1
            for pt in range(4):
                n = cn["t"]
                xt = rr(xts, n); bx = rr(bxs, n); hT = rr(hTs, n); bhT = rr(bhTs, n)
                r0 = slice(NS + pt * 128, NS + (pt + 1) * 128)
                rw = slice(NW + pt * 128, NW + (pt + 1) * 128)
                S.dma("sp", xt[:], x2[r0, :], writes=[bx])
                S.dma("sp", x2w[rw, :], xt[:], reads=[bx])
                norm_tile(ncx, xt, bx, 0, hT, bhT, 0)
                proj(hT, bhT, True, True)
                q_part(rw, None)
                kv_part(r0, None, prow=slice(pt * 128, (pt + 1) * 128))
                cn["t"] += 1
            S.flush()

        with ExitStack() as es:
            NCH = (NS + 256) // 128
            KT = SB(es, "KT1", [128, 4, NS + 256], BF16); bKT = S.buf()
            VA = SB(es, "VA1", [128, NCH, 4, 128], BF16); bVA = S.buf()
            KTp = SB(es, "KTp1", [128, 4, 256], BF16); bKTp = S.buf()
            VAp = SB(es, "VAp1", [128, 2, 4, 128], BF16); bVAp = S.buf()
            Wo = SB(es, "Wo1", [64, 16, D], BF16); bWo = S.buf()
            for hh in range(2):
                S.dma("pool", Wo[:, hh * 8:(hh + 1) * 8, :], od_w_out[0].rearrange("(h d) n -> d h n", d=64)[:, hh * 8:(hh + 1) * 8, :], writes=[bWo])
            G1b = [mod_bc(es, 1, cnd, 2, "G1b%d" % cnd) for cnd in range(2)]
            for c8 in range(0, NCH, 8):
                S.op("pool", lambda e, c8=c8: e.memset(VA[:, c8:min(c8 + 8, NCH), :, :], 1.0), [], [bVA])
            S.op("pool", lambda e: e.memset(VAp[:], 1.0), [], [bVAp])
            kl = [SB(es, "kl1%d" % i, [128, 256], BF16) for i in range(2)]; bkl = S.bufs_n(2)
            vl = [SB(es, "vl1%d" % i, [128, 256], BF16) for i in range(2)]; bvl = S.bufs_n(2)
            kd = [SB(es, "kd1%d" % i, [128, 4, 2, 64], BF16) for i in range(2)]; bkd = S.bufs_n(2)
            ptr = PS(es, "ptr1", [128, 8, 128], BF16); bptr = S.buf()
            psS = [[PS(es, "psS%d_%d" % (i, e2), [128, 512]) for e2 in range(2)] for i in range(3)]
            bpsS = [S.bufs_n(2) for i in range(3)]
            pod = [PS(es, "pod%d" % i, [128, 512]) for i in range(1)]; bpod = S.bufs_n(1)
            cn = {"k": 0, "b": 0, "g": 0, "p": 0}

            def build_kv(k_src, v_src, KTd, bKTd, VAd, bVAd, chunk, cast_q):
                i = cn["k"]; cn["k"] += 1
                k_ = rr(kl, i); bk = rr(bkl, i); v_ = rr(vl, i); bv = rr(bvl, i); d_ = rr(kd, i); bd = rr(bkd, i)
                S.dma(cast_q, k_[:], k_src, writes=[bk])
                S.dma(cast_q, v_[:], v_src, writes=[bv])
                S.op("act", lambda e: e.copy(out=d_[:], in_=k_[:].rearrange("p (g d) -> p g d", d=64).unsqueeze(2).broadcast_to([128, 4, 2, 64])), [bk], [bd])
                for g_ in range(4):
                    S.op("pe", lambda e, g_=g_: e.transpose(out=ptr[:, g_, :], in_=d_[:, g_, :, :].rearrange("p a d -> p (a d)"), identity=ident[:]), [bd], [bptr], signal=(g_ == 3))
                S.op("act", lambda e: e.copy(out=KTd[:, :, chunk * 128:(chunk + 1) * 128], in_=ptr[:, 0:4, :]), [bptr], [bKTd])
                S.op("pool", lambda e: e.tensor_copy(out=VAd[:, chunk, :, 0:64], in_=v_[:].rearrange("p (g d) -> p g d", d=64)), [bv], [bVAd])

            for i in range(32):
                build_kv(k1[i * 128:(i + 1) * 128, :], v1[i * 128:(i + 1) * 128, :], KT, bKT, VA, bVA, i, "sp")
            for cc in range(2):
                build_kv(gq_kc[cc * 128:(cc + 1) * 128, :], gq_vc[cc * 128:(cc + 1) * 128, :], KT, bKT, VA, bVA, 32 + cc, "pool")

            ql = [SB(es, "ql1%d" % i, [128, D], BF16) for i in range(2)]; bql = S.bufs_n(2)
            qT = [SB(es, "qT1%d" % i, [128, 8, 128], BF16) for i in range(2)]; bqT = S.bufs_n(2)
            Pm = [SB(es, "Pm1%d" % i, [128, 512], BF16) for i in range(3)]; bPm = S.bufs_n(3)
            rc = [SB(es, "rc1%d" % i, [64, 512]) for i in range(2)]; brc = S.bufs_n(2)
            OT = [SB(es, "OT1%d" % i, [64, 16, 128], BF16) for i in range(2)]; bOT = S.bufs_n(2)
            xtl = [SB(es, "xt1%d" % i, [128, D]) for i in range(2)]; bxtl = S.bufs_n(2)
            tmo = [SB(es, "tm1%d" % i, [128, D]) for i in range(2)]; btmo = S.bufs_n(2)

            CSTG = 9

            def gqa_block(row0, KTx, bKTx, VAx, bVAx, nch, cnd):
                if CSTG < 2:
                    return
                b = cn["b"]; cn["b"] += 1
                q_ = rr(ql, b); bq = rr(bql, b); qt = rr(qT, b); bqt = rr(bqT, b); ot = rr(OT, b); bot = rr(bOT, b)
                xt = rr(xtl, b); bx = rr(bxtl, b); tm = rr(tmo, b); btm = rr(btmo, b)
                S.dma("sp", q_[:], q1[row0:row0 + 128, :], writes=[bq])
                S.dma("sp", xt[:], x2w[row0:row0 + 128, :], writes=[bx])
                for pr in range(8):
                    S.op("pe", lambda e, pr=pr: e.transpose(out=ptr[:, pr, :], in_=q_[:, pr * 128:(pr + 1) * 128], identity=ident[:]), [bq], [bptr], signal=(pr == 7))
                S.op("act", lambda e: e.copy(out=qt[:], in_=ptr[:]), [bptr], [bqt])
                for g_ in range(4 if CSTG >= 3 else 0):
                    gi = cn["g"]; cn["g"] += 1
                    od = rr(pod, gi); bod = rr(bpod, gi); r_ = rr(rc, gi); br_ = rr(brc, gi)

                    def QK(cc, g_=g_):
                        n = cn["p"] + cc
                        sp2 = rr(psS, n); bsp2 = rr(bpsS, n)
                        for a in (0, 2, 1, 3):
                            pr = 2 * g_ + a // 2; e2 = a % 2; pp = slice(e2 * 64, (e2 + 1) * 64)
                            sp_ = sp2[e2]
                            S.op("pe", lambda e, a=a, pr=pr, pp=pp, sp_=sp_, cc=cc: e.matmul(sp_[:, (a // 2) * 128:(a // 2 + 1) * 128], lhsT=KTx[pp, g_, cc * 128:(cc + 1) * 128], rhs=qt[pp, pr, :], start=True, stop=True), [bKTx, bqt], [bsp2[e2]], signal=(a >= 2))
                    QK(0)
                    if nch > 1:
                        QK(1)
                    for cc in range(nch):
                        if cc + 2 < nch:
                            QK(cc + 2)
                        n = cn["p"] + cc
                        sp2 = rr(psS, n); bsp2 = rr(bpsS, n); pm = rr(Pm, n); bpm = rr(bPm, n)
                        pmv = pm[:].rearrange("p (h e t) -> p h e t", e=2, t=128)
                        for e2 in range(2):
                            S.op("act", lambda e, sp2=sp2, pmv=pmv, e2=e2: e.activation(out=pmv[:, :, e2, :], in_=sp2[e2][:, 0:256].rearrange("p (h t) -> p h t", t=128), func=AF.Exp), [bsp2[e2]], [bpm])
                        S.op("pe", lambda e, cc=cc, pm=pm, od=od, g_=g_: e.matmul(od[:], lhsT=VAx[:, cc, g_, :], rhs=pm[:], start=(cc == 0), stop=(cc == nch - 1)), [bVAx, bpm], [bod], signal=(cc == nch - 1))
                    cn["p"] += nch
                    if CSTG < 4:
                        continue
                    S.op("act", lambda e, r_=r_, od=od: e.copy(out=r_[:], in_=od[64:128, :]), [bod], [br_])
                    S.op("dve", lambda e, r_=r_: e.reciprocal(out=r_[:], in_=r_[:]), [br_], [br_])
                    S.op("dve", lambda e, r_=r_, od=od, g_=g_: e.tensor_tensor(out=ot[0:64, 4 * g_:4 * g_ + 4, :].rearrange("p h t -> p (h t)"), in0=od[0:64, :], in1=r_[:], op=ALU.mult), [bod, br_], [bot])
                if CSTG < 5:
                    return
                for cgi in range(2):
                    cs = slice(cgi * 512, (cgi + 1) * 512)
                    pot = psS[0][cgi]; bpot = bpsS[0][cgi]
                    for hh in range(16):
                        S.op("pe", lambda e, hh=hh, cs=cs, pot=pot: e.matmul(pot[:, 0:512], lhsT=ot[0:64, hh, :], rhs=Wo[0:64, hh, cs], start=(hh == 0), stop=(hh == 15)), [bot, bWo], [bpot], signal=(hh == 15))
                Gt, bG = G1b[cnd]
                for cgi in range(2):
                    cs = slice(cgi * 512, (cgi + 1) * 512)
                    pot = psS[0][cgi]; bpot = bpsS[0][cgi]
                    S.op("dve", lambda e, cs=cs, pot=pot: e.tensor_tensor(out=tm[:, cs], in0=pot[:, 0:512], in1=Gt[:, cs], op=ALU.mult), [bpot, bG], [btm])
                S.op("pool", lambda e: e.tensor_tensor(out=tm[:], in0=tm[:], in1=xt[:], op=ALU.add), [btm, bx], [btm])
                S.dma("sp", x3[row0:row0 + 128, :], tm[:], reads=[btm])

            for wb in range(NW // 128):
                gqa_block(wb * 128, KT, bKT, VA, bVA, NCH, 1)
            for sq in range(2 if CSTG >= 6 else 0):
                for cc in range(2):
                    r0 = NS + sq * 256 + cc * 128
                    build_kv(k1[r0:r0 + 128, :], v1[r0:r0 + 128, :], KTp, bKTp, VAp, bVAp, cc, "sp")
                for qb in range(2):
                    gqa_block(NW + sq * 256 + qb * 128, KTp, bKTp, VAp, bVAp, 2, 0)
            S.flush()

        if CSTG >= 7:
          ffn_phase(1, x3, [(0, NW, 1, lambda t, n: ysw_o[t:t + n, :]),
                          (NW, 256, 0, lambda t, n: yp_o[t:t + n, :]),
                          (NW + 256, 256, 0, lambda t, n: yp_o[256 + t:256 + t + n, :])], "f1")

        S.nops_total = S.nops
    return nc


_NC = None


def _natb_host(rpb):
    w = np.arange(64)
    cs = np.clip(w - 8, 0, 48)
    wp = np.arange(64)[:, None]
    valid = (wp >= cs[None, :]) & (wp < cs[None, :] + 16)
    idx = np.clip(wp - w[None, :] + 15, 0, 30)
    out = rpb[0][:, :, idx]
    out = np.where(valid[None, None], out, np.float32(NEG)).astype(np.float32)
    return np.ascontiguousarray(out)


def kernel(**inp):
    global _NC
    if _NC is None:
        _NC = build_program()
    nc = _NC
    f = lambda a: np.ascontiguousarray(np.asarray(a, dtype=np.float32))
    t = np.arange(NS)
    pos_all = np.stack([t // 64, t % 64], axis=1).astype(np.float32)
    natb = _natb_host(np.asarray(inp["na_rpb"], dtype=np.float32))
    wnames = ["norm1_g", "norm2_g", "ada_w", "ada_b", "ffn_w_up", "ffn_conv_w", "ffn_conv_b", "ffn_w_down",
              "ev_w_in", "ev_w_out", "na_q_g", "na_k_g", "ssm_a_re", "ssm_a_im", "ssm_log_dt", "ssm_b_re", "ssm_b_im",
              "ssm_c_re", "ssm_c_im", "ssm_d", "ssm_w_glu", "ssm_b_glu", "od_w_in", "od_w_out", "gqa_q_g", "gqa_k_g"]
    shared = {n: f(inp[n]) for n in wnames}
    in_maps = []
    for c in range(8):
        b = c // 2
        hf = c % 2
        s0 = 1920 * hf
        m = dict(shared)
        m["xp"] = f(inp["x_prompt"][2 * c:2 * c + 2]).reshape(NPR, D)
        m["xs"] = f(inp["x_sample"][b])
        m["na_kc"] = f(inp["cache_na_k"][b, 0]).reshape(256, 512)
        m["na_vc"] = f(inp["cache_na_v"][b, 0]).reshape(256, 512)
        m["sre_in"] = f(inp["state_ssm_re"][b, 0])
        m["sim_in"] = f(inp["state_ssm_im"][b, 0])
        m["gq_kc"] = f(inp["cache_gqa_k"][b, 0]).reshape(256, 256)
        m["gq_vc"] = f(inp["cache_gqa_v"][b, 0]).reshape(256, 256)
        m["cvec"] = np.ascontiguousarray(np.stack([f(inp["c_ctx"]), f(inp["c"][b])], axis=0))
        m["wsel"] = np.ascontiguousarray(np.tile(np.array([[1.0 - hf, float(hf)]], np.float32), (128, 1)))
        m["pos_all"] = pos_all
        m["pos_win"] = np.ascontiguousarray(pos_all[s0:s0 + NW])
        m["natb"] = natb
        in_maps.append(m)
    res = run_bass_kernel_spmd(nc, in_maps, core_ids=list(range(8)))
    R = res.results
    y_prompt = np.zeros((16, 256, D), np.float32)
    y_sample = np.zeros((4, NS, D), np.float32)
    new_na_k = np.zeros((16, 1, 256, 8, 64), np.float32)
    new_na_v = np.zeros((16, 1, 256, 8, 64), np.float32)
    new_ssm_re = np.zeros((16, 1, 2, 32, 64), np.float32)
    new_ssm_im = np.zeros((16, 1, 2, 32, 64), np.float32)
    new_gqa_k = np.zeros((16, 1, 256, 4, 64), np.float32)
    new_gqa_v = np.zeros((16, 1, 256, 4, 64), np.float32)
    for c in range(8):
        b = c // 2
        hf = c % 2
        r = R[c]
        y_prompt[2 * c:2 * c + 2] = np.asarray(r["yp_o"]).reshape(2, 256, D)
        off = 128 * hf
        y_sample[b, hf * 2048:(hf + 1) * 2048] = np.asarray(r["ysw_o"])[off:off + 2048]
        new_na_k[2 * c:2 * c + 2, 0] = np.asarray(r["nak_o"]).reshape(2, 256, 8, 64)
        new_na_v[2 * c:2 * c + 2, 0] = np.asarray(r["nav_o"]).reshape(2, 256, 8, 64)
        new_ssm_re[2 * c:2 * c + 2, 0] = np.asarray(r["sre_o"]).reshape(2, 2, 32, 64)
        new_ssm_im[2 * c:2 * c + 2, 0] = np.asarray(r["sim_o"]).reshape(2, 2, 32, 64)
        new_gqa_k[2 * c:2 * c + 2, 0] = np.asarray(r["gk_o"]).reshape(2, 256, 4, 64)
        new_gqa_v[2 * c:2 * c + 2, 0] = np.asarray(r["gv_o"]).reshape(2, 256, 4, 64)
    return (y_prompt, y_sample, new_na_k, new_na_v, new_ssm_re, new_ssm_im, new_gqa_k, new_gqa_v)
```

```python
import numpy as np
from contextlib import ExitStack
import concourse.bass as bass
import concourse.mybir as mybir
from concourse.bass_utils import run_bass_kernel_spmd

F32 = mybir.dt.float32
BF16 = mybir.dt.bfloat16
I32 = mybir.dt.int32
ALU = mybir.AluOpType
AF = mybir.ActivationFunctionType
AX = mybir.AxisListType

SAME_ENGINE_SYNC = True


class Buf:
    __slots__ = ("w", "r", "name")

    def __init__(self, name=""):
        self.w = None
        self.r = {}
        self.name = name


class Sched:
    ENGS = ("pe", "act", "dve", "pool", "sp")
    NDMA = 8

    def __init__(self, nc, es):
        self.nc = nc
        self.sems = {}
        for e in ("pe", "act", "dve", "pool"):
            self.sems[e] = es.enter_context(nc.semaphore("c_" + e))
        self.cnt = {e: 0 for e in ("pe", "act", "dve", "pool")}
        self.dsems = {}
        self.dcnt = {}
        self.drr = {}
        for q in ("sp", "pool", "act"):
            self.dsems[q] = [es.enter_context(nc.semaphore("d_%s%d" % (q, i))) for i in range(self.NDMA)]
            self.dcnt[q] = [0] * self.NDMA
            self.drr[q] = 0
        self.semobj = {}
        for e, s in self.sems.items():
            self.semobj[("c", e)] = s
        for q, l in self.dsems.items():
            for i, s in enumerate(l):
                self.semobj[("d", q, i)] = s
        self.waited = {e: {} for e in self.ENGS}
        self.ops = {e: [] for e in self.ENGS}
        self.bufs = []
        self.nops = 0

    def buf(self, name=""):
        b = Buf(name)
        self.bufs.append(b)
        return b

    def bufs_n(self, n, name=""):
        return [self.buf(name + str(i)) for i in range(n)]

    def _deps(self, eng, reads, writes):
        deps = {}

        def add(tok):
            if tok is None:
                return
            k, v = tok
            if deps.get(k, 0) < v:
                deps[k] = v
        for b in reads:
            add(b.w)
        for b in writes:
            add(b.w)
            for k, v in b.r.items():
                add((k, v))
        out = []
        for k, v in deps.items():
            if k[0] == "c" and k[1] == eng:
                if eng == "pe" or not SAME_ENGINE_SYNC:
                    continue
            if self.waited[eng].get(k, 0) >= v:
                continue
            self.waited[eng][k] = v
            out.append((self.semobj[k], v))
        return out

    def _mark(self, tok, reads, writes):
        k, v = tok
        for b in reads:
            if b.r.get(k, 0) < v:
                b.r[k] = v
        for b in writes:
            b.w = tok
            b.r = {}

    def op(self, eng, fn, reads=(), writes=(), signal=True):
        waits = self._deps(eng, reads, writes)
        k = ("c", eng)
        if signal:
            self.cnt[eng] += 1
            tok = (k, self.cnt[eng])
        else:
            tok = (k, self.cnt[eng] + 1)
        sem = self.sems[eng]

        def emit(e):
            for s, v in waits:
                e.wait_ge(s, v)
            ins = fn(e)
            if signal:
                ins.then_inc(sem, 1)
        self.ops[eng].append(emit)
        self._mark(tok, reads, writes)
        self.nops += 1
        return tok

    def dma(self, q, out, in_, reads=(), writes=(), **kw):
        slot = self.drr[q]
        self.drr[q] = (slot + 1) % self.NDMA
        k = ("d", q, slot)
        prev = self.dcnt[q][slot]
        waits = self._deps(q, reads, writes)
        if prev > 0 and self.waited[q].get(k, 0) < prev:
            self.waited[q][k] = prev
            waits.append((self.semobj[k], prev))
        self.dcnt[q][slot] = prev + 16
        tok = (k, prev + 16)
        sem = self.semobj[k]

        def emit(e):
            for s, v in waits:
                e.wait_ge(s, v)
            e.dma_start(out=out, in_=in_, **kw).then_inc(sem, 16)
        self.ops[q].append(emit)
        self._mark(tok, reads, writes)
        self.nops += 1
        return tok

    def flush(self):
        nc = self.nc
        for q in ("sp", "pool", "act"):
            waits = []
            for i in range(self.NDMA):
                k = ("d", q, i)
                v = self.dcnt[q][i]
                if v > 0 and self.waited[q].get(k, 0) < v:
                    self.waited[q][k] = v
                    waits.append((self.semobj[k], v))
            if waits:
                def emit(e, waits=waits):
                    for s, v in waits:
                        e.wait_ge(s, v)
                self.ops[q].append(emit)
        ops = self.ops
        with nc.Block() as block:
            if ops["pe"]:
                @block.tensor
                def _(e):
                    for f in ops["pe"]:
                        f(e)
            if ops["act"]:
                @block.scalar
                def _(e):
                    for f in ops["act"]:
                        f(e)
            if ops["dve"]:
                @block.vector
                def _(e):
                    for f in ops["dve"]:
                        f(e)
            if ops["pool"]:
                @block.gpsimd
                def _(e):
                    for f in ops["pool"]:
                        f(e)
            if ops["sp"]:
                @block.sync
                def _(e):
                    for f in ops["sp"]:
                        f(e)
        self.ops = {e: [] for e in self.ENGS}
        for e in self.ENGS:
            for ce in ("pe", "act", "dve", "pool"):
                self.waited[e][("c", ce)] = self.cnt[ce]
        for b in self.bufs:
            b.w = None
            b.r = {}
        self.bufs = []


D = 1024
NS = 4096
NPR = 512
NT0 = NS + NPR
NW = 2176
NT1 = NW + NPR
DFF = 2816
NJ = 22
EPS = 1e-6
TWO_PI = 6.28318
NEG = -30000.0


class K:
    pass


def build_program():
    nc = bass.Bass("TRN2", target_bir_lowering=False)
    g = K()

    def din(name, shape, dt=F32):
        return nc.dram_tensor(name, list(shape), dt, kind="ExternalInput").ap()

    def dout(name, shape, dt=F32):
        return nc.dram_tensor(name, list(shape), dt, kind="ExternalOutput").ap()

    def dscr(name, shape, dt=F32):
        return nc.dram_tensor(name, list(shape), dt, kind="Internal").ap()

    xp = din("xp", [NPR, D]); xs = din("xs", [NS, D])
    na_kc = din("na_kc", [256, 512]); na_vc = din("na_vc", [256, 512])
    sre_in = din("sre_in", [2, 32, 64]); sim_in = din("sim_in", [2, 32, 64])
    gq_kc = din("gq_kc", [256, 256]); gq_vc = din("gq_vc", [256, 256])
    cvec = din("cvec", [2, D])
    wsel = din("wsel", [128, 2])
    pos_all = din("pos_all", [NS, 2]); pos_win = din("pos_win", [NW, 2])
    natb = din("natb", [8, 15, 64, 64])
    norm1_g = din("norm1_g", [2, D]); norm2_g = din("norm2_g", [2, D])
    ada_w = din("ada_w", [2, D, 6 * D]); ada_b = din("ada_b", [2, 6 * D])
    ffn_w_up = din("ffn_w_up", [2, D, 2 * DFF]); ffn_conv_w = din("ffn_conv_w", [2, 3, DFF])
    ffn_conv_b = din("ffn_conv_b", [2, DFF]); ffn_w_down = din("ffn_w_down", [2, DFF, D])
    ev_w_in = din("ev_w_in", [1, D, 2048]); ev_w_out = din("ev_w_out", [1, D, D])
    na_q_g = din("na_q_g", [1, 64]); na_k_g = din("na_k_g", [1, 64])
    ssm_a_re = din("ssm_a_re", [1, 2, 32, 64]); ssm_a_im = din("ssm_a_im", [1, 2, 32, 64])
    ssm_log_dt = din("ssm_log_dt", [1, 2, 32])
    ssm_b_re = din("ssm_b_re", [1, 2, 32, 64, 16]); ssm_b_im = din("ssm_b_im", [1, 2, 32, 64, 16])
    ssm_c_re = din("ssm_c_re", [1, 2, 32, 16, 64]); ssm_c_im = din("ssm_c_im", [1, 2, 32, 16, 64])
    ssm_d = din("ssm_d", [1, 32, 16]); ssm_w_glu = din("ssm_w_glu", [1, 512, 512]); ssm_b_glu = din("ssm_b_glu", [1, 512])
    od_w_in = din("od_w_in", [1, D, 1536]); od_w_out = din("od_w_out", [1, D, D])
    gqa_q_g = din("gqa_q_g", [1, 64]); gqa_k_g = din("gqa_k_g", [1, 64])
    yp_o = dout("yp_o", [NPR, D]); ysw_o = dout("ysw_o", [NW, D])
    nak_o = dout("nak_o", [NPR, 512]); nav_o = dout("nav_o", [NPR, 512])
    sre_o = dout("sre_o", [2, 2, 32, 64]); sim_o = dout("sim_o", [2, 2, 32, 64])
    gk_o = dout("gk_o", [NPR, 256]); gv_o = dout("gv_o", [NPR, 256])
    modv = dscr("modv", [2, 2, 6, D])
    q0 = dscr("q0", [NT0, 512], BF16); k0 = dscr("k0", [NT0, 512], BF16)
    v0 = dscr("v0", [NT0, 512], BF16); u0 = dscr("u0", [NT0, 512], BF16)
    yssmT = dscr("yssmT", [512, NT0], BF16)
    x1 = dscr("x1", [NT0, D]); x2 = dscr("x2", [NT0, D])
    x2w = dscr("x2w", [NT1, D]); x3 = dscr("x3", [NT1, D])
    q1 = dscr("q1", [NT1, D], BF16); k1 = dscr("k1", [NT0, 256], BF16); v1 = dscr("v1", [NT0, 256], BF16)
    wup_bf = dscr("wup_bf", [2, D, 2 * DFF], BF16); wdn_bf = dscr("wdn_bf", [2, DFF, D], BF16)

    with ExitStack() as top:
        S = Sched(nc, top)
        top.enter_context(nc.allow_non_contiguous_dma("small strided parameter loads"))

        def SB(es, name, shape, dt=F32):
            return es.enter_context(nc.sbuf_tensor(name, list(shape), dt))

        def PS(es, name, shape, dt=F32):
            return es.enter_context(nc.psum_tensor(name, list(shape), dt))

        ident = SB(top, "ident", [128, 128], BF16); identf = SB(top, "identf", [128, 128], F32)
        ones_b = SB(top, "ones_b", [128, 128], BF16)
        BTre = SB(top, "BTre", [128, 32, 128], BF16); BTim = SB(top, "BTim", [128, 32, 128], BF16)
        CTre = SB(top, "CTre", [128, 32, 128], BF16); CTim = SB(top, "CTim", [128, 32, 128], BF16)
        RHO = SB(top, "RHO", [128, 32]); FR = SB(top, "FR", [128, 32])
        H0re = SB(top, "H0re", [128, 32]); H0im = SB(top, "H0im", [128, 32])

        def rr(lst, i):
            return lst[i % len(lst)]

        _sc_tmp = {}

        def sincos(es, tag, f_ap, shape, sin_out, cos_out, bufs_r, bufs_w, eng="dve"):
            key = (id(es), tuple(shape))
            if key not in _sc_tmp:
                nm = "sct%d" % len(_sc_tmp)
                _sc_tmp[key] = (SB(es, nm + "_ti", shape, I32), SB(es, nm + "_tf", shape), SB(es, nm + "_tg", shape), S.buf(), S.buf(), S.buf())
            ti, tf, tg, b1, b2, b3 = _sc_tmp[key]
            if b1 not in S.bufs:
                S.bufs.extend([b1, b2, b3])
            for (off, outp) in ((0.0, sin_out), (0.25, cos_out)):
                S.op(eng, lambda e, off=off: e.tensor_scalar(out=tg[:], in0=f_ap, scalar1=1.0, scalar2=off, op0=ALU.mult, op1=ALU.add), bufs_r, [b3])
                S.op(eng, lambda e: e.tensor_copy(out=ti[:], in_=tg[:]), [b3], [b1])
                S.op(eng, lambda e: e.tensor_copy(out=tf[:], in_=ti[:]), [b1], [b2])
                S.op(eng, lambda e: e.tensor_tensor(out=tg[:], in0=tg[:], in1=tf[:], op=ALU.subtract), [b2, b3], [b3])
                S.op("act", lambda e, outp=outp: e.activation(out=outp, in_=tg[:], func=AF.Sin, scale=TWO_PI), [b3], bufs_w)

        def load_bc(es, name, src_row_ap, n=D, q="sp"):
            t = SB(es, name, [128, n]); b = S.buf()
            S.dma(q, t[:], src_row_ap.partition_broadcast(128), writes=[b])
            return t, b

        with ExitStack() as es:
            bI = S.buf()
            S.op("pool", lambda e: e.memset(ident[:], 0.0), [], [bI])
            S.op("pool", lambda e: e.affine_select(out=ident[:], in_=ident[:], pattern=[[-1, 128]], compare_op=ALU.not_equal, fill=1.0, base=0, channel_multiplier=1), [bI], [bI])
            S.op("pool", lambda e: e.memset(identf[:], 0.0), [], [bI])
            S.op("pool", lambda e: e.affine_select(out=identf[:], in_=identf[:], pattern=[[-1, 128]], compare_op=ALU.not_equal, fill=1.0, base=0, channel_multiplier=1), [bI], [bI])
            S.op("pool", lambda e: e.memset(ones_b[:], 1.0), [], [bI])
            for l in range(2):
                src = ffn_w_up[l].rearrange("a (b c) -> (a b) c", c=1408)
                dst = wup_bf[l].rearrange("a (b c) -> (a b) c", c=1408)
                for i in range(4):
                    S.dma("pool", dst[i * 1024:(i + 1) * 1024, :], src[i * 1024:(i + 1) * 1024, :])
                for i in range(2):
                    S.dma("pool", wdn_bf[l, i * 1408:(i + 1) * 1408, :], ffn_w_down[l, i * 1408:(i + 1) * 1408, :])
            ct = SB(es, "ct", [128, 8, 2]); bct = S.buf()
            for cnd in range(2):
                S.dma("sp", ct[:, :, cnd], cvec[cnd].rearrange("(k p) -> p k", p=128), writes=[bct])
            S.op("act", lambda e: e.activation(out=ct[:], in_=ct[:], func=AF.Silu), [bct], [bct])
            slabs = [SB(es, "adas%d" % i, [128, 8, 512]) for i in range(2)]; bsl = S.bufs_n(2)
            pm = [PS(es, "pm%d" % i, [2, 512]) for i in range(2)]; bpm = S.bufs_n(2)
            for l in range(2):
                mrow = SB(es, "mrow%d" % l, [2, 6 * D]); bm = S.buf()
                adab = SB(es, "adab%d" % l, [2, 6 * D]); bab = S.buf()
                S.dma("sp", adab[:], ada_b[l].partition_broadcast(2), writes=[bab])
                ng = SB(es, "ng%d" % l, [2, 2, D]); bng = S.buf()
                S.dma("sp", ng[:, 0, :], norm1_g[l].partition_broadcast(2), writes=[bng])
                S.dma("sp", ng[:, 1, :], norm2_g[l].partition_broadcast(2), writes=[bng])
                for cgi in range(12):
                    n = l * 12 + cgi
                    sl = rr(slabs, n); bs = rr(bsl, n); p = rr(pm, n); bp = rr(bpm, n)
                    for kh in range(2):
                        S.dma("sp", sl[:, kh * 4:(kh + 1) * 4, :], ada_w[l].rearrange("(k p) n -> p k n", p=128)[:, kh * 4:(kh + 1) * 4, cgi * 512:(cgi + 1) * 512], writes=[bs])
                    for k in range(8):
                        S.op("pe", lambda e, p=p, sl=sl, k=k: e.matmul(p[:], lhsT=ct[:, k, :], rhs=sl[:, k, :], start=(k == 0), stop=(k == 7)), [bct, bs], [bp], signal=(k == 7))
                    S.op("dve", lambda e, p=p, cgi=cgi, mrow=mrow, adab=adab: e.tensor_tensor(out=mrow[:, cgi * 512:(cgi + 1) * 512], in0=p[:], in1=adab[:, cgi * 512:(cgi + 1) * 512], op=ALU.add), [bp, bab], [bm])
                for (slot, gi) in ((1, 0), (4, 1)):
                    S.op("dve", lambda e, slot=slot, gi=gi, mrow=mrow, ng=ng: e.scalar_tensor_tensor(out=mrow[:, slot * D:(slot + 1) * D], in0=mrow[:, slot * D:(slot + 1) * D], scalar=1.0, in1=ng[:, gi, :], op0=ALU.add, op1=ALU.mult), [bm, bng], [bm])
                S.dma("sp", modv[l].rearrange("c s d -> c (s d)"), mrow[:], reads=[bm])
            S.flush()

        def mod_bc(es, l, cnd, slot, name):
            return load_bc(es, name, modv[l, cnd, slot])

        class NormCtx:
            pass

        def make_norm_ctx(es, l, which, tag):
            c = NormCtx()
            c.A = []; c.B = []
            for cnd in range(2):
                a, ba = mod_bc(es, l, cnd, 1 + 3 * which, "%s_A%d" % (tag, cnd))
                b, bb = mod_bc(es, l, cnd, 0 + 3 * which, "%s_B%d" % (tag, cnd))
                c.A.append((a, ba)); c.B.append((b, bb))
            c.junk = SB(es, tag + "_junk", [128, D], BF16); c.bjunk = S.buf()
            c.ss = [SB(es, tag + "_ss%d" % i, [128, 4]) for i in range(2)]; c.bss = S.bufs_n(2)
            c.h32 = [SB(es, tag + "_h32%d" % i, [128, D]) for i in range(2)]; c.bh32 = S.bufs_n(2)
            c.hb = [SB(es, tag + "_hb%d" % i, [128, D], BF16) for i in range(2)]; c.bhb = S.bufs_n(2)
            c.pst = [PS(es, tag + "_pst%d" % i, [128, 8, 128], BF16) for i in range(2)]; c.bpst = S.bufs_n(2)
            c.n = 0
            return c

        def norm_tile(c, xt, bx, cnd, hT_out, bhT, col0, nrows=128):
            i = c.n; c.n += 1
            ss = rr(c.ss, i); bss = rr(c.bss, i); h32 = rr(c.h32, i); bh32 = rr(c.bh32, i)
            hb = rr(c.hb, i); bhb = rr(c.bhb, i); pst = rr(c.pst, i); bpst = rr(c.bpst, i)
            A, bA = c.A[cnd]; Bt, bB = c.B[cnd]
            r = nrows
            S.op("act", lambda e: e.activation(out=c.junk[0:r, :], in_=xt[0:r, :], func=AF.Square, accum_out=ss[0:r, 0:1]), [bx], [c.bjunk, bss])
            S.op("act", lambda e: e.activation(out=ss[0:r, 1:2], in_=ss[0:r, 0:1], func=AF.Sqrt, scale=1.0 / D, bias=EPS), [bss], [bss])
            S.op("dve", lambda e: e.reciprocal(out=ss[0:r, 2:3], in_=ss[0:r, 1:2]), [bss], [bss])
            S.op("dve", lambda e: e.scalar_tensor_tensor(out=h32[0:r, :], in0=xt[0:r, :], scalar=ss[0:r, 2:3], in1=A[0:r, :], op0=ALU.mult, op1=ALU.mult), [bx, bss, bA], [bh32])
            S.op("dve", lambda e: e.tensor_tensor(out=hb[0:r, :], in0=h32[0:r, :], in1=Bt[0:r, :], op=ALU.add), [bh32, bB], [bhb])
            for k in range(8):
                S.op("pe", lambda e, k=k: e.transpose(out=pst[:, k, 0:r], in_=hb[0:r, k * 128:(k + 1) * 128], identity=ident[0:r, 0:r]), [bhb], [bpst], signal=(k == 7))
            S.op("act", lambda e: e.copy(out=hT_out[:, :, col0:col0 + r], in_=pst[:, :, 0:r]), [bpst], [bhT])

        def head_norm(wk, ps_ap, nh, g_bc, bg, out_f32, bout, rd, extra_w=()):
            i = wk.n; wk.n += 1
            sq = rr(wk.sq, i); bsq = rr(wk.bsq, i); st = rr(wk.st, i); bst = rr(wk.bst, i)
            w = nh * 64
            S.op("act", lambda e: e.activation(out=sq[:, 0:w], in_=ps_ap, func=AF.Square), rd, [bsq])
            S.op("dve", lambda e: e.tensor_reduce(out=st[:, 0:nh], in_=sq[:, 0:w].rearrange("p (h d) -> p h d", d=64), axis=AX.X, op=ALU.add), [bsq], [bst])
            S.op("act", lambda e: e.activation(out=st[:, 16:16 + nh], in_=st[:, 0:nh], func=AF.Sqrt, scale=1.0 / 64, bias=EPS), [bst], [bst])
            S.op("dve", lambda e: e.reciprocal(out=st[:, 32:32 + nh], in_=st[:, 16:16 + nh]), [bst], [bst])
            S.op("dve", lambda e: e.tensor_tensor(out=sq[:, 0:w].rearrange("p (h d) -> p h d", d=64), in0=ps_ap.rearrange("p (h d) -> p h d", d=64), in1=st[:, 32:32 + nh].unsqueeze(2).broadcast_to([128, nh, 64]), op=ALU.mult), rd + [bst], [bsq])
            S.op("dve", lambda e: e.tensor_tensor(out=out_f32, in0=sq[:, 0:w], in1=g_bc[:, 0:w], op=ALU.mult), [bsq, bg], [bout] + list(extra_w))

        class HN:
            pass

        def make_hn(es, tag):
            wk = HN(); wk.n = 0
            wk.sq = [SB(es, tag + "_sq%d" % i, [128, D]) for i in range(2)]; wk.bsq = S.bufs_n(2)
            wk.st = [SB(es, tag + "_st%d" % i, [128, 48]) for i in range(2)]; wk.bst = S.bufs_n(2)
            return wk

        def load_gbc(es, name, g_ap, nh, scale=None):
            t = SB(es, name, [128, nh, 64]); b = S.buf()
            S.dma("sp", t[:], g_ap.partition_broadcast(128).unsqueeze(1).broadcast_to([128, nh, 64]), writes=[b])
            if scale is not None:
                S.op("act", lambda e: e.mul(out=t[:], in_=t[:], mul=scale), [b], [b])
            return t, b

        with ExitStack() as es:
            Win = SB(es, "Win0", [128, 8, 2048], BF16); bW = S.buf()
            for kh in range(4):
                S.dma("pool", Win[:, kh * 2:(kh + 1) * 2, :], ev_w_in[0].rearrange("(k p) n -> p k n", p=128)[:, kh * 2:(kh + 1) * 2, :], writes=[bW])
            ncx = make_norm_ctx(es, 0, 0, "nA")
            hn = make_hn(es, "hA")
            qg, bqg = load_gbc(es, "qg0", na_q_g[0], 8, scale=0.125)
            kg, bkg = load_gbc(es, "kg0", na_k_g[0], 8)
            qg2 = qg[:].rearrange("p h d -> p (h d)"); kg2 = kg[:].rearrange("p h d -> p (h d)")
            xts = [SB(es, "xtA%d" % i, [128, D]) for i in range(2)]; bxs = S.bufs_n(2)
            hTs = [SB(es, "hTA%d" % i, [128, 8, 128], BF16) for i in range(2)]; bhTs = S.bufs_n(2)
            pcs = [PS(es, "pcA%d" % i, [128, 512]) for i in range(4)]; bpcs = S.bufs_n(4)
            of32 = [SB(es, "ofA%d" % i, [128, 512]) for i in range(2)]; bof = S.bufs_n(2)
            ob = [SB(es, "obA%d" % i, [128, 512], BF16) for i in range(4)]; bob = S.bufs_n(4)
            no = 0; nb = 0
            for i in range(NT0 // 128):
                cnd = 1 if i < 32 else 0
                src = xs[i * 128:(i + 1) * 128, :] if i < 32 else xp[(i - 32) * 128:(i - 31) * 128, :]
                xt = rr(xts, i); bx = rr(bxs, i); hT = rr(hTs, i); bhT = rr(bhTs, i)
                S.dma("sp", xt[:], src, writes=[bx])
                norm_tile(ncx, xt, bx, cnd, hT, bhT, 0)
                for cgi in range(4):
                    for k in range(8):
                        S.op("pe", lambda e, cgi=cgi, k=k, hT=hT: e.matmul(pcs[cgi][:], lhsT=hT[:, k, :], rhs=Win[:, k, cgi * 512:(cgi + 1) * 512], start=(k == 0), stop=(k == 7)), [bhT, bW], [bpcs[cgi]], signal=(k == 7))
                rows = slice(i * 128, (i + 1) * 128)
                prow = slice((i - 32) * 128, (i - 31) * 128)
                for cgi, (dst, gb, bg) in enumerate(((q0, qg2, bqg), (k0, kg2, bkg))):
                    o32 = rr(of32, no); bo = rr(bof, no); no += 1
                    head_norm(hn, pcs[cgi][:], 8, gb, bg, o32[:], bo, [bpcs[cgi]])
                    o16 = rr(ob, nb); b16 = rr(bob, nb); nb += 1
                    S.op("act", lambda e, o16=o16, o32=o32: e.copy(out=o16[:], in_=o32[:]), [bo], [b16])
                    S.dma("sp", dst[rows, :], o16[:], reads=[b16])
                    if cgi == 1 and i >= 32:
                        S.dma("sp", nak_o[prow, :], o32[:], reads=[bo])
                for cgi, dst in ((2, v0), (3, u0)):
                    o16 = rr(ob, nb); b16 = rr(bob, nb); nb += 1
                    S.op("act", lambda e, o16=o16, cgi=cgi: e.copy(out=o16[:], in_=pcs[cgi][:]), [bpcs[cgi]], [b16])
                    S.dma("sp", dst[rows, :], o16[:], reads=[b16])
                    if cgi == 2 and i >= 32:
                        o32 = rr(of32, no); bo = rr(bof, no); no += 1
                        S.op("dve", lambda e, o32=o32, cgi=cgi: e.tensor_copy(out=o32[:], in_=pcs[cgi][:]), [bpcs[cgi]], [bo])
                        S.dma("sp", nav_o[prow, :], o32[:], reads=[bo])
            S.flush()

        with ExitStack() as es:
            def pq(ap):
                return ap.rearrange("d (P gl) p -> (gl p) (d P)", gl=2)
            ARE = SB(es, "ARE", [128, 32]); AIM = SB(es, "AIM", [128, 32]); LDT = SB(es, "LDT", [128, 32])
            bare = S.buf(); baim = S.buf(); bldt = S.buf(); bh0 = S.buf()
            for qq in range(4):
                qsl = slice(qq * 8, (qq + 1) * 8)
                S.dma("sp", ARE[:, qsl], pq(ssm_a_re[0])[:, qsl], writes=[bare])
                S.dma("sp", AIM[:, qsl], pq(ssm_a_im[0])[:, qsl], writes=[baim])
                S.dma("sp", H0re[:, qsl], pq(sre_in)[:, qsl], writes=[bh0])
                S.dma("sp", H0im[:, qsl], pq(sim_in)[:, qsl], writes=[bh0])
            for gl in range(2):
                S.dma("sp", LDT[gl * 64:(gl + 1) * 64, :], ssm_log_dt[0].rearrange("d (P gl) -> gl (d P)", gl=2)[gl].partition_broadcast(64), writes=[bldt])
            BRE = SB(es, "BRE", [128, 32, 16]); BIM = SB(es, "BIM", [128, 32, 16]); bbre = S.buf(); bbim = S.buf()
            for qq in range(4):
                qsl = slice(qq * 8, (qq + 1) * 8)
                S.dma("sp", BRE[:, qsl, :], ssm_b_re[0].rearrange("d (P gl) p c -> (gl p) (d P) c", gl=2)[:, qsl, :], writes=[bbre])
                S.dma("sp", BIM[:, qsl, :], ssm_b_im[0].rearrange("d (P gl) p c -> (gl p) (d P) c", gl=2)[:, qsl, :], writes=[bbim])
            CBDr = SB(es, "CBDr", [32, 32, 128]); CBDi = SB(es, "CBDi", [32, 32, 128]); bcr = S.buf(); bci = S.buf()
            S.op("pool", lambda e: e.memset(CBDr[:], 0.0), [], [bcr])
            S.op("pool", lambda e: e.memset(CBDi[:], 0.0), [], [bci])
            for gl in range(2):
                S.dma("sp", CBDr[gl * 16:(gl + 1) * 16, :, gl * 64:(gl + 1) * 64], ssm_c_re[0].rearrange("d (P gl) c p -> gl c (d P) p", gl=2)[gl], writes=[bcr])
                S.dma("sp", CBDi[gl * 16:(gl + 1) * 16, :, gl * 64:(gl + 1) * 64], ssm_c_im[0].rearrange("d (P gl) c p -> gl c (d P) p", gl=2)[gl], writes=[bci])
            sm = {}
            for nm in ("DT", "ARDT", "AIDT", "F", "TF", "SIN", "COS", "ABR", "ABI", "DEN", "T1", "T2", "KR", "KI"):
                sm[nm] = (SB(es, "s_" + nm, [128, 32]), S.buf())
            TI = SB(es, "s_TI", [128, 32], I32); bti = S.buf()

            def tt(o, a, b, op, eng="dve"):
                S.op(eng, lambda e: e.tensor_tensor(out=sm[o][0][:], in0=sm[a][0][:] if isinstance(a, str) else a[0][:], in1=sm[b][0][:] if isinstance(b, str) else b[0][:], op=op),
                     [sm[a][1] if isinstance(a, str) else a[1], sm[b][1] if isinstance(b, str) else b[1]], [sm[o][1]])
            S.op("act", lambda e: e.activation(out=sm["DT"][0][:], in_=LDT[:], func=AF.Exp), [bldt], [sm["DT"][1]])
            tt("ARDT", (ARE, bare), "DT", ALU.mult)
            tt("AIDT", (AIM, baim), "DT", ALU.mult)
            S.op("act", lambda e: e.activation(out=RHO[:], in_=sm["ARDT"][0][:], func=AF.Exp), [sm["ARDT"][1]], [bh0])
            S.op("dve", lambda e: e.tensor_scalar(out=sm["F"][0][:], in0=sm["AIDT"][0][:], scalar1=1.0 / (2.0 * np.pi), scalar2=None, op0=ALU.mult), [sm["AIDT"][1]], [sm["F"][1]])
            S.op("dve", lambda e: e.tensor_copy(out=TI[:], in_=sm["F"][0][:]), [sm["F"][1]], [bti])
            S.op("dve", lambda e: e.tensor_copy(out=sm["TF"][0][:], in_=TI[:]), [bti], [sm["TF"][1]])
            S.op("dve", lambda e: e.tensor_tensor(out=FR[:], in0=sm["F"][0][:], in1=sm["TF"][0][:], op=ALU.subtract), [sm["F"][1], sm["TF"][1]], [bh0])
            sincos(es, "sc0", FR[:], [128, 32], sm["SIN"][0][:], sm["COS"][0][:], [bh0], [sm["SIN"][1], sm["COS"][1]])
            RHOb = (RHO, bh0)
            tt("ABR", RHOb, "COS", ALU.mult)
            tt("ABI", RHOb, "SIN", ALU.mult)
            S.op("dve", lambda e: e.tensor_scalar(out=sm["ABR"][0][:], in0=sm["ABR"][0][:], scalar1=-1.0, scalar2=None, op0=ALU.add), [sm["ABR"][1]], [sm["ABR"][1]])
            tt("DEN", (ARE, bare), (ARE, bare), ALU.mult)
            tt("T1", (AIM, baim), (AIM, baim), ALU.mult)
            tt("DEN", "DEN", "T1", ALU.add)
            S.op("dve", lambda e: e.reciprocal(out=sm["DEN"][0][:], in_=sm["DEN"][0][:]), [sm["DEN"][1]], [sm["DEN"][1]])
            tt("T1", "ABR", (ARE, bare), ALU.mult); tt("T2", "ABI", (AIM, baim), ALU.mult); tt("KR", "T1", "T2", ALU.add); tt("KR", "KR", "DEN", ALU.mult)
            tt("T1", "ABI", (ARE, bare), ALU.mult); tt("T2", "ABR", (AIM, baim), ALU.mult); tt("KI", "T1", "T2", ALU.subtract); tt("KI", "KI", "DEN", ALU.mult)
            BBr = SB(es, "BBr", [128, 32, 16]); BBi = SB(es, "BBi", [128, 32, 16]); Tb1 = SB(es, "Tb1", [128, 32, 16]); Tb2 = SB(es, "Tb2", [128, 32, 16])
            bBBr = S.buf(); bBBi = S.buf(); bT1 = S.buf(); bT2 = S.buf()
            krb = sm["KR"][0][:].unsqueeze(2).broadcast_to([128, 32, 16]); kib = sm["KI"][0][:].unsqueeze(2).broadcast_to([128, 32, 16])
            S.op("dve", lambda e: e.tensor_tensor(out=Tb1[:], in0=BRE[:], in1=krb, op=ALU.mult), [bbre, sm["KR"][1]], [bT1])
            S.op("dve", lambda e: e.tensor_tensor(out=Tb2[:], in0=BIM[:], in1=kib, op=ALU.mult), [bbim, sm["KI"][1]], [bT2])
            S.op("dve", lambda e: e.tensor_tensor(out=BBr[:], in0=Tb1[:], in1=Tb2[:], op=ALU.subtract), [bT1, bT2], [bBBr])
            S.op("dve", lambda e: e.tensor_tensor(out=Tb1[:], in0=BIM[:], in1=krb, op=ALU.mult), [bbim, sm["KR"][1]], [bT1])
            S.op("dve", lambda e: e.tensor_tensor(out=Tb2[:], in0=BRE[:], in1=kib, op=ALU.mult), [bbre, sm["KI"][1]], [bT2])
            S.op("dve", lambda e: e.tensor_tensor(out=BBi[:], in0=Tb1[:], in1=Tb2[:], op=ALU.add), [bT1, bT2], [bBBi])
            bCT = S.buf(); bBT = S.buf()
            S.op("pool", lambda e: e.memset(CTre[:], 0.0), [], [bCT])
            S.op("pool", lambda e: e.memset(CTim[:], 0.0), [], [bCT])
            SRC = [SB(es, "SRC%d" % i, [128, 128]) for i in range(4)]; bSRC = S.bufs_n(4)
            pT = [PS(es, "pT%d" % i, [128, 128]) for i in range(2)]; bpT = S.bufs_n(2)
            pC = [PS(es, "pC%d" % i, [128, 32]) for i in range(2)]; bpC = S.bufs_n(2)
            n = 0
            for q in range(32):
                P = q % 16; slot = P % 4
                for (BB, bBB, BT) in ((BBr, bBBr, BTre), (BBi, bBBi, BTim)):
                    src = rr(SRC, n); bs = rr(bSRC, n); pt = rr(pT, n); bp = rr(bpT, n); n += 1
                    S.op("pool", lambda e, src=src: e.memset(src[:], 0.0), [], [bs])
                    for gl in range(2):
                        S.op("pool", lambda e, src=src, gl=gl, BB=BB, q=q, slot=slot: e.tensor_copy(out=src[gl * 64:(gl + 1) * 64, slot * 32 + gl * 16: slot * 32 + gl * 16 + 16], in_=BB[gl * 64:(gl + 1) * 64, q, :]), [bBB], [bs])
                    S.op("pe", lambda e, src=src, pt=pt: e.matmul(pt[:], lhsT=src[:], rhs=identf[:], start=True, stop=True), [bs, bI], [bp])
                    S.op("act", lambda e, pt=pt, BT=BT, q=q: e.copy(out=BT[:, q, :], in_=pt[:]), [bp], [bBT])
                for (CBD, bcb, CT, sgn) in ((CBDr, bcr, CTre, 1.0), (CBDi, bci, CTim, -1.0)):
                    pc = rr(pC, n); bp = rr(bpC, n); n += 1
                    S.op("pe", lambda e, CBD=CBD, pc=pc, q=q: e.matmul(pc[:], lhsT=CBD[:, q, :], rhs=identf[0:32, 0:32], start=True, stop=True), [bcb, bI], [bp])
                    S.op("act", lambda e, pc=pc, CT=CT, q=q, slot=slot, sgn=sgn: e.mul(out=CT[:, q, slot * 32:(slot + 1) * 32], in_=pc[:], mul=sgn), [bp], [bCT])
            S.flush()

        with ExitStack() as es:
            J1 = SB(es, "J1", [128, 512]); bJ1 = S.buf()
            J1i = SB(es, "J1i", [128, 512], I32)
            S.op("pool", lambda e: e.iota(J1i[:], pattern=[[1, 512]], base=1, channel_multiplier=0), [], [bJ1])
            S.op("pool", lambda e: e.tensor_copy(out=J1[:], in_=J1i[:]), [bJ1], [bJ1])
            ONES = SB(es, "ONESf", [128, 512]); bON = S.buf()
            S.op("pool", lambda e: e.memset(ONES[:], 1.0), [], [bON])
            DCOL = SB(es, "DCOL", [128, 4]); bDC = S.buf()
            S.dma("sp", DCOL[:], ssm_d[0].rearrange("g c -> (g c)").rearrange("(k p) -> p k", p=128), writes=[bDC])
            BGL = SB(es, "BGL", [128, 4]); bBG = S.buf()
            S.dma("sp", BGL[:], ssm_b_glu[0].rearrange("(k p) -> p k", p=128), writes=[bBG])
            WGL = SB(es, "WGL", [128, 4, 512], BF16); bWG = S.buf()
            S.dma("pool", WGL[:], ssm_w_glu[0].rearrange("(k p) n -> p k n", p=128), writes=[bWG])
            uT = SB(es, "uT", [128, NT0], BF16); buT = S.buf()
            yacc = SB(es, "yacc", [128, NT0]); bya = S.buf()
            ygT = SB(es, "ygT", [128, 4, NT0], BF16); byg = S.buf()
            utl = [SB(es, "utl%d" % i, [128, 4, 128], BF16) for i in range(2)]; butl = S.bufs_n(2)
            ptr = PS(es, "ptrS", [128, 4, 128], BF16); bptr = S.buf()
            COS = [SB(es, "COS%d" % s, [128, 512]) for s in range(4)]; SIN = [SB(es, "SIN%d" % s, [128, 512]) for s in range(4)]
            RT = [SB(es, "RT%d" % s, [128, 512]) for s in range(4)]
            bCOS = S.bufs_n(4); bSIN = S.bufs_n(4); bRT = S.bufs_n(4)
            FT = SB(es, "FT", [128, 512]); bFT = S.buf()
            CAR = SB(es, "CAR", [128, 4, 2]); bCAR = S.bufs_n(4)
            FRE = SB(es, "FRE", [128, 2, 32]); FIM = SB(es, "FIM", [128, 2, 32]); bFRE = S.buf()
            pA = [PS(es, "pA%d" % i, [128, 512]) for i in range(2)]; bpA = S.bufs_n(2)
            pB = [PS(es, "pB%d" % i, [128, 512]) for i in range(2)]; bpB = S.bufs_n(2)
            pY = [PS(es, "pY%d" % i, [128, 512]) for i in range(2)]; bpY = S.bufs_n(2)
            W2 = lambda nm, k=2, dt=F32: ([SB(es, "%s%d" % (nm, i), [128, 512], dt) for i in range(k)], S.bufs_n(k))
            br_, bbr_ = W2("wbr"); bi_, bbi_ = W2("wbi"); wr_, bwr_ = W2("wwr"); wi_, bwi_ = W2("wwi")
            ta_, bta_ = W2("wta", 1); tb_, btb_ = W2("wtb", 1); gr_, bgr_ = W2("wgr", 1); gi_, bgi_ = W2("wgi", 1)
            tc_, btc_ = W2("wtc", 1); td_, btd_ = W2("wtd", 1); hr_, bhr_ = W2("whr", 1); hi_, bhi_ = W2("whi", 1)
            hrb = [SB(es, "hrb%d" % s, [128, 512], BF16) for s in range(4)]; hib = [SB(es, "hib%d" % s, [128, 512], BF16) for s in range(4)]
            bhrb = S.bufs_n(4); bhib = S.bufs_n(4)
            seqs = [(0, NS, True, -1), (NS, 256, False, 0), (NS + 256, 256, False, 1)]
            nck = 0; nypc = [0]
            for ctile in range(4):
                for i4 in range(NT0 // 512):
                    ut = rr(utl, i4); bu = rr(butl, i4)
                    S.dma("sp", ut[:], u0[i4 * 512:(i4 + 1) * 512, ctile * 128:(ctile + 1) * 128].rearrange("(a p) c -> p a c", p=128), writes=[bu])
                    for a in range(4):
                        S.op("pe", lambda e, ut=ut, a=a: e.transpose(out=ptr[:, a, :], in_=ut[:, a, :], identity=ident[:]), [bu, bI], [bptr], signal=(a == 3))
                    S.op("act", lambda e, i4=i4: e.copy(out=uT[:, i4 * 512:(i4 + 1) * 512], in_=ptr[:].rearrange("p a t -> p (a t)")), [bptr], [buT])
                S.op("dve", lambda e, ctile=ctile: e.tensor_scalar(out=yacc[:], in0=uT[:], scalar1=DCOL[:, ctile:ctile + 1], scalar2=None, op0=ALU.mult), [buT, bDC], [bya])
                for d in range(2):
                    qs = [d * 16 + ctile * 4 + s for s in range(4)]
                    for s in range(4):
                        q = qs[s]
                        S.op("dve", lambda e, q=q: e.tensor_scalar(out=FT[:], in0=J1[:], scalar1=FR[:, q:q + 1], scalar2=None, op0=ALU.mult), [bJ1], [bFT])
                        sincos(es, "sc_%d_%d_%d" % (ctile, d, s), FT[:], [128, 512], SIN[s][:], COS[s][:], [bFT], [bSIN[s], bCOS[s]])
                        S.op("dve", lambda e, q=q, s=s: e.tensor_scalar(out=RT[s][:], in0=ONES[:], scalar1=RHO[:, q:q + 1], scalar2=None, op0=ALU.mult), [bON], [bRT[s]])
                    items = []
                    for (base, L, has_init, sidx) in seqs:
                        n = min(512, L)
                        starts = list(range(0, L, n))
                        if d == 1:
                            starts = starts[::-1]
                        for ci, c0 in enumerate(starts):
                            lo = base + c0
                            for s in range(4):
                                q = qs[s]
                                it = {"pre": [], "post": []}
                                if ci == 0:
                                    if has_init:
                                        def init(s=s, q=q):
                                            S.op("act", lambda e: e.copy(out=CAR[:, s, 0:1], in_=H0re[:, q:q + 1]), [], [bCAR[s]])
                                            S.op("act", lambda e: e.copy(out=CAR[:, s, 1:2], in_=H0im[:, q:q + 1]), [], [bCAR[s]])
                                    else:
                                        def init(s=s):
                                            S.op("pool", lambda e: e.memset(CAR[:, s, :], 0.0), [], [bCAR[s]])
                                    it["pre"].append(init)
                                k_ = nck; nck += 1
                                A = rr(pA, k_); bA_ = rr(bpA, k_); B = rr(pB, k_); bB_ = rr(bpB, k_)
                                br = rr(br_, k_); bbr = rr(bbr_, k_); bi = rr(bi_, k_); bbi = rr(bbi_, k_)
                                wr = rr(wr_, k_); bwr = rr(bwr_, k_); wi = rr(wi_, k_); bwi = rr(bwi_, k_)

                                def rvs(t, n=n, d=d):
                                    if d == 0:
                                        return t[:, 0:n]
                                    return t[:, 0:n][:, ::-1]

                                def TT(eng, o, a, b, op, rd, wrb, n=n):
                                    S.op(eng, lambda e: e.tensor_tensor(out=o[:, 0:n], in0=a[:, 0:n], in1=b[:, 0:n], op=op), rd, wrb)

                                def s1(s=s, q=q, A=A, bA_=bA_, B=B, bB_=bB_, br=br, bbr=bbr, bi=bi, bbi=bbi, wr=wr, bwr=bwr, wi=wi, bwi=bwi, lo=lo, n=n, rvs=rvs, TT=TT):
                                    ta = ta_[0]; bta = bta_[0]; tb = tb_[0]; btb = btb_[0]
                                    cs_, sn_ = COS[s], SIN[s]
                                    S.op("pe", lambda e: e.matmul(A[:, 0:n], lhsT=BTre[:, q, :], rhs=uT[:, lo:lo + n], start=True, stop=True), [buT], [bA_])
                                    S.op("pe", lambda e: e.matmul(B[:, 0:n], lhsT=BTim[:, q, :], rhs=uT[:, lo:lo + n], start=True, stop=True), [buT], [bB_])
                                    S.op("act", lambda e: e.copy(out=br[:, 0:n], in_=rvs(A)), [bA_], [bbr])
                                    S.op("act", lambda e: e.copy(out=bi[:, 0:n], in_=rvs(B)), [bB_], [bbi])
                                    TT("pool", ta, br, cs_, ALU.mult, [bbr, bCOS[s]], [bta])
                                    TT("pool", tb, bi, sn_, ALU.mult, [bbi, bSIN[s]], [btb])
                                    TT("pool", wr, ta, tb, ALU.add, [bta, btb], [bwr])
                                    TT("pool", ta, bi, cs_, ALU.mult, [bbi, bCOS[s]], [bta])
                                    TT("pool", tb, br, sn_, ALU.mult, [bbr, bSIN[s]], [btb])
                                    TT("pool", wi, ta, tb, ALU.subtract, [bta, btb], [bwi])

                                def s2(s=s, q=q, wr=wr, bwr=bwr, wi=wi, bwi=bwi, n=n, rvs=rvs, TT=TT):
                                    gr = gr_[0]; bgr = bgr_[0]; gi = gi_[0]; bgi = bgi_[0]
                                    tc = tc_[0]; btc = btc_[0]; td = td_[0]; btd = btd_[0]; hr = hr_[0]; bhr = bhr_[0]; hi = hi_[0]; bhi = bhi_[0]
                                    cs_, sn_ = COS[s], SIN[s]
                                    S.op("dve", lambda e: e.tensor_tensor_scan(out=gr[:, 0:n], data0=RT[s][:, 0:n], data1=wr[:, 0:n], initial=CAR[:, s, 0:1], op0=ALU.mult, op1=ALU.add), [bRT[s], bwr, bCAR[s]], [bgr])
                                    S.op("dve", lambda e: e.tensor_tensor_scan(out=gi[:, 0:n], data0=RT[s][:, 0:n], data1=wi[:, 0:n], initial=CAR[:, s, 1:2], op0=ALU.mult, op1=ALU.add), [bRT[s], bwi, bCAR[s]], [bgi])
                                    TT("dve", tc, gr, cs_, ALU.mult, [bgr, bCOS[s]], [btc])
                                    TT("dve", td, gi, sn_, ALU.mult, [bgi, bSIN[s]], [btd])
                                    TT("dve", hr, tc, td, ALU.subtract, [btc, btd], [bhr])
                                    TT("dve", tc, gr, sn_, ALU.mult, [bgr, bSIN[s]], [btc])
                                    TT("dve", td, gi, cs_, ALU.mult, [bgi, bCOS[s]], [btd])
                                    TT("dve", hi, tc, td, ALU.add, [btc, btd], [bhi])
                                    S.op("act", lambda e: e.copy(out=CAR[:, s, 0:1], in_=hr[:, n - 1:n]), [bhr], [bCAR[s]])
                                    S.op("act", lambda e: e.copy(out=CAR[:, s, 1:2], in_=hi[:, n - 1:n]), [bhi], [bCAR[s]])
                                    S.op("act", lambda e: e.copy(out=hrb[s][:, 0:n], in_=rvs(hr)), [bhr], [bhrb[s]])
                                    S.op("act", lambda e: e.copy(out=hib[s][:, 0:n], in_=rvs(hi)), [bhi], [bhib[s]])
                                it["s1"] = s1; it["s2"] = s2
                                if s == 3:
                                    def tail(lo=lo, n=n):
                                        nonlocal_n = nypc[0]; nypc[0] += 1
                                        py = rr(pY, nonlocal_n); bpy = rr(bpY, nonlocal_n)
                                        for s_ in range(4):
                                            q_ = qs[s_]
                                            S.op("pe", lambda e, s_=s_, q_=q_: e.matmul(py[:, 0:n], lhsT=CTre[:, q_, :], rhs=hrb[s_][:, 0:n], start=(s_ == 0), stop=False), [bhrb[s_]], [bpy], signal=False)
                                            S.op("pe", lambda e, s_=s_, q_=q_: e.matmul(py[:, 0:n], lhsT=CTim[:, q_, :], rhs=hib[s_][:, 0:n], start=False, stop=(s_ == 3)), [bhib[s_]], [bpy], signal=(s_ == 3))
                                        S.op("dve", lambda e: e.tensor_tensor(out=yacc[:, lo:lo + n], in0=py[:, 0:n], in1=yacc[:, lo:lo + n], op=ALU.add), [bpy, bya], [bya])
                                    it["post"].append(tail)
                                if (not has_init) and ci == len(starts) - 1:
                                    def fin(s=s, q=q, sidx=sidx):
                                        S.op("act", lambda e: e.copy(out=FRE[:, sidx, q:q + 1], in_=CAR[:, s, 0:1]), [bCAR[s]], [bFRE])
                                        S.op("act", lambda e: e.copy(out=FIM[:, sidx, q:q + 1], in_=CAR[:, s, 1:2]), [bCAR[s]], [bFRE])
                                    it["post"].append(fin)
                                items.append(it)
                    for i_, it in enumerate(items):
                        if i_ == 0:
                            it["s1"]()
                        if i_ + 1 < len(items):
                            items[i_ + 1]["s1"]()
                        for f_ in it["pre"]:
                            f_()
                        it["s2"]()
                        for f_ in it["post"]:
                            f_()
                for i9 in range(NT0 // 512):
                    sl = slice(i9 * 512, (i9 + 1) * 512)
                    ta = ta_[0]; bta = bta_[0]; tb = tb_[0]; btb = btb_[0]
                    S.op("dve", lambda e, sl=sl: e.tensor_tensor(out=ta[:], in0=yacc[:, sl], in1=yacc[:, sl], op=ALU.mult), [bya], [bta])
                    S.op("dve", lambda e: e.tensor_scalar(out=ta[:], in0=ta[:], scalar1=0.044715, scalar2=1.0, op0=ALU.mult, op1=ALU.add), [bta], [bta])
                    S.op("dve", lambda e, sl=sl: e.tensor_tensor(out=ta[:], in0=ta[:], in1=yacc[:, sl], op=ALU.mult), [bta, bya], [bta])
                    S.op("act", lambda e: e.activation(out=tb[:], in_=ta[:], func=AF.Sigmoid, scale=1.5957691216), [bta], [btb])
                    S.op("dve", lambda e, sl=sl, ctile=ctile: e.tensor_tensor(out=ygT[:, ctile, sl], in0=tb[:], in1=yacc[:, sl], op=ALU.mult), [btb, bya], [byg])
            yo = [SB(es, "yo%d" % i, [128, 512], BF16) for i in range(2)]; byo = S.bufs_n(2)
            n = 0
            for cho in range(4):
                for i9 in range(NT0 // 512):
                    sl = slice(i9 * 512, (i9 + 1) * 512)
                    py = rr(pY, n); bpy = rr(bpY, n); o = rr(yo, n); bo = rr(byo, n); n += 1
                    for k in range(4):
                        S.op("pe", lambda e, py=py, k=k, cho=cho, sl=sl: e.matmul(py[:], lhsT=WGL[:, k, cho * 128:(cho + 1) * 128], rhs=ygT[:, k, sl], start=(k == 0), stop=(k == 3)), [bWG, byg], [bpy], signal=(k == 3))
                    tb = tb_[0]; btb = btb_[0]
                    S.op("act", lambda e, py=py, cho=cho: e.activation(out=tb[:], in_=py[:], func=AF.Sigmoid, bias=BGL[:, cho:cho + 1]), [bpy, bBG], [btb])
                    S.op("dve", lambda e, o=o, cho=cho, sl=sl: e.tensor_tensor(out=o[:], in0=tb[:], in1=ygT[:, cho, sl], op=ALU.mult), [btb, byg], [bo])
                    S.dma("sp", yssmT[cho * 128:(cho + 1) * 128, sl], o[:], reads=[bo])
            for sq in range(2):
                for qq in range(4):
                    qsl = slice(qq * 8, (qq + 1) * 8)
                    S.dma("sp", sre_o.rearrange("s d (P gl) p -> (gl p) s (d P)", gl=2)[:, sq, qsl], FRE[:, sq, qsl], reads=[bFRE])
                    S.dma("sp", sim_o.rearrange("s d (P gl) p -> (gl p) s (d P)", gl=2)[:, sq, qsl], FIM[:, sq, qsl], reads=[bFRE])
            S.flush()

        with ExitStack() as es:
            kT = SB(es, "kT", [128, 4, NS], BF16); bkT = S.buf()
            kTc = SB(es, "kTc", [128, 4, 256], BF16); bkTc = S.buf()
            Vctx = SB(es, "Vctx", [128, 2, 512], BF16); bVc = S.buf()
            NATB = SB(es, "NATB", [128, 8, 15, 64], BF16); bNB = S.buf()
            WoN = SB(es, "WoN", [64, 8, D], BF16); WoS = SB(es, "WoS", [128, 4, D], BF16); bWo = S.buf()
            S.dma("pool", Vctx[:], na_vc.rearrange("(c p) f -> p c f", p=128), writes=[bVc])
            for h in range(8):
                S.dma("pool", NATB[0:64, h, :, :], natb[h].rearrange("r wp w -> wp r w"), writes=[bNB])
                S.dma("pool", NATB[64:128, h, :, :], natb[h].rearrange("r wp w -> wp r w"), writes=[bNB])
            S.dma("pool", WoN[:], ev_w_out[0, 0:512, :].rearrange("(h d) n -> d h n", d=64), writes=[bWo])
            S.dma("pool", WoS[:], ev_w_out[0, 512:1024, :].rearrange("(k p) n -> p k n", p=128), writes=[bWo])
            G1 = [mod_bc(es, 0, cnd, 2, "G1c%d" % cnd) for cnd in range(2)]
            ktl = [SB(es, "ktl%d" % i, [128, 4, 512], BF16) for i in range(2)]; bktl = S.bufs_n(2)
            ptr = PS(es, "ptrC", [128, 4, 128], BF16); bptr = S.buf()
            sbp = [PS(es, "sbp%d" % i, [128, 512]) for i in range(2)]; bsbp = S.bufs_n(2)
            msc = [PS(es, "msc%d" % i, [128, 512]) for i in range(2)]; bmsc = S.bufs_n(2)
            po = PS(es, "poC", [128, D]); bpo = S.buf()
            def build_kT(src_rows_ap, dst, bdst, ntile4, n0):
                for i4 in range(ntile4):
                    kt = rr(ktl, n0 + i4); bk = rr(bktl, n0 + i4)
                    S.dma("sp", kt[:], src_rows_ap[i4 * 512:(i4 + 1) * 512, :].rearrange("(a p) f -> p a f", p=128), writes=[bk])
                    for j in range(4):
                        for a in range(4):
                            S.op("pe", lambda e, kt=kt, a=a, j=j: e.transpose(out=ptr[:, a, :], in_=kt[:, a, j * 128:(j + 1) * 128], identity=ident[:]), [bk], [bptr], signal=(a == 3))
                        S.op("act", lambda e, j=j, i4=i4, dst=dst: e.copy(out=dst[:, j, i4 * 512:(i4 + 1) * 512], in_=ptr[:].rearrange("p a t -> p (a t)")), [bptr], [bdst])
            build_kT(k0[0:NS, :], kT, bkT, 8, 0)
            kcl = SB(es, "kcl", [128, 2, 512], BF16); bkcl = S.buf()
            S.dma("pool", kcl[:], na_kc.rearrange("(c p) f -> p c f", p=128), writes=[bkcl])
            for j in range(4):
                for a in range(2):
                    S.op("pe", lambda e, a=a, j=j: e.transpose(out=ptr[:, a, :], in_=kcl[:, a, j * 128:(j + 1) * 128], identity=ident[:]), [bkcl], [bptr], signal=(a == 1))
                S.op("act", lambda e, j=j: e.copy(out=kTc[:, j, :], in_=ptr[:, 0:2, :].rearrange("p a t -> p (a t)")), [bptr], [bkTc])
            qtl = [SB(es, "qtl%d" % i, [128, 512], BF16) for i in range(2)]; bqtl = S.bufs_n(2)
            qT = [SB(es, "qT%d" % i, [128, 4, 128], BF16) for i in range(2)]; bqT = S.bufs_n(2)
            VB = [SB(es, "VB%d" % i, [64, 8, 512], BF16) for i in range(2)]; bVB = S.bufs_n(2)
            PB = [SB(es, "PB%d" % i, [128, 512], BF16) for i in range(2)]; bPB = S.bufs_n(2)
            PC = [SB(es, "PC%d" % i, [128, 128], BF16) for i in range(2)]; bPC = S.bufs_n(2)
            rec = [SB(es, "rec%d" % i, [64, 128]) for i in range(2)]; brec = S.bufs_n(2)
            OT = [SB(es, "OT%d" % i, [64, 8, 128], BF16) for i in range(2)]; bOT = S.bufs_n(2)
            YT = [SB(es, "YT%d" % i, [128, 4, 128], BF16) for i in range(2)]; bYT = S.bufs_n(2)
            xtl = [SB(es, "xtC%d" % i, [128, D]) for i in range(2)]; bxtl = S.bufs_n(2)
            tmo = [SB(es, "tmo%d" % i, [128, D]) for i in range(2)]; btmo = S.bufs_n(2)
            kTp = SB(es, "kTp", [128, 4, 256], BF16); bkTp = S.buf()
            Vp = SB(es, "Vp", [128, 2, 512], BF16); bVp = S.buf()
            cnt = {"h": 0, "b": 0, "v": 0}

            def load_qT(row0):
                i = cnt["b"]
                ql = rr(qtl, i); bq = rr(bqtl, i); qt = rr(qT, i); bqt = rr(bqT, i)
                S.dma("sp", ql[:], q0[row0:row0 + 128, :], writes=[bq])
                for j in range(4):
                    S.op("pe", lambda e, j=j, ql=ql: e.transpose(out=ptr[:, j, :], in_=ql[:, j * 128:(j + 1) * 128], identity=ident[:]), [bq], [bptr], signal=(j == 3))
                S.op("act", lambda e, qt=qt: e.copy(out=qt[:], in_=ptr[:]), [bptr], [bqt])
                return qt, bqt

            def finish_block(ot, bot, col0, x_src, cnd, dst_rows):
                i = cnt["b"]; cnt["b"] += 1
                yt = rr(YT, i); byt = rr(bYT, i); xt = rr(xtl, i); bx = rr(bxtl, i); tm = rr(tmo, i); btm = rr(btmo, i)
                S.dma("sp", yt[:], yssmT[:, col0:col0 + 128].rearrange("(k p) t -> p k t", p=128), writes=[byt])
                S.dma("sp", xt[:], x_src, writes=[bx])
                for cgi in range(2):
                    cs = slice(cgi * 512, (cgi + 1) * 512)
                    for h in range(8):
                        S.op("pe", lambda e, h=h, cs=cs: e.matmul(po[:, cs], lhsT=ot[0:64, h, :], rhs=WoN[0:64, h, cs], start=(h == 0), stop=False), [bot, bWo], [bpo], signal=False)
                    for k in range(4):
                        S.op("pe", lambda e, k=k, cs=cs: e.matmul(po[:, cs], lhsT=yt[:, k, :], rhs=WoS[:, k, cs], start=False, stop=(k == 3)), [byt, bWo], [bpo], signal=(k == 3))
                Gt, bG = G1[cnd]
                for cgi in range(2):
                    cs = slice(cgi * 512, (cgi + 1) * 512)
                    S.op("dve", lambda e, cs=cs: e.tensor_tensor(out=tm[:, cs], in0=po[:, cs], in1=Gt[:, cs], op=ALU.mult), [bpo, bG], [btm])
                S.op("dve", lambda e: e.tensor_tensor(out=tm[:], in0=tm[:], in1=xt[:], op=ALU.add), [btm, bx], [btm])
                S.dma("sp", dst_rows, tm[:], reads=[btm])

            for m in range(32):
                qt, bqt = load_qT(m * 128)
                ot = rr(OT, m); bot = rr(bOT, m)
                for r2 in range(2):
                    r = 2 * m + r2
                    rs = min(max(r - 4, 0), 56); v = r - rs
                    vb = rr(VB, cnt["v"]); bvb = rr(bVB, cnt["v"]); cnt["v"] += 1
                    S.dma("sp", vb[:], v0[rs * 64:rs * 64 + 512, :].rearrange("(i w) f -> w i f", w=64), writes=[bvb])
                    qc = slice(r2 * 64, (r2 + 1) * 64)
                    NSTG = 9
                    for h in range(8 if NSTG >= 2 else 0):
                        e2 = h % 2; j = h // 2; pp = slice(e2 * 64, (e2 + 1) * 64)
                        n = cnt["h"]; cnt["h"] += 1
                        sp_ = rr(sbp, n); bsp = rr(bsbp, n); ms = rr(msc, n); bms = rr(bmsc, n)
                        pb = rr(PB, n); bpb = rr(bPB, n); pc = rr(PC, n); bpc = rr(bPC, n); rc = rr(rec, n); brc = rr(brec, n)
                        sband = sp_[0:64, :].rearrange("p (i w) -> p i w", w=64)
                        sctx = ms[:, 0:128].rearrange("p (c w) -> p c w", w=64)
                        od = ms[0:64, 128:256].rearrange("p (c w) -> p c w", w=64)
                        for i in range(8):
                            S.op("pe", lambda e, i=i, j=j, pp=pp, sband=sband, rs=rs, qt=qt, qc=qc: e.matmul(sband[:, i, :], lhsT=kT[pp, j, (rs + i) * 64:(rs + i + 1) * 64], rhs=qt[pp, j, qc], start=True, stop=False), [bkT, bqt], [bsp], signal=False)
                            if NSTG >= 3:
                                S.op("pe", lambda e, i=i, h=h, v=v, sband=sband: e.matmul(sband[:, i, :], lhsT=ident[(h % 2) * 64:(h % 2) * 64 + 64, (h % 2) * 64:(h % 2) * 64 + 64], rhs=NATB[(h % 2) * 64:(h % 2) * 64 + 64, h, i - v + 7, :], start=False, stop=True), [bNB], [bsp], signal=(i == 7))
                        for cc in range(2 if NSTG >= 4 else 0):
                            S.op("pe", lambda e, cc=cc, j=j, pp=pp, sctx=sctx, qt=qt, qc=qc: e.matmul(sctx[:, cc, :], lhsT=kTc[pp, j, cc * 128:(cc + 1) * 128], rhs=qt[pp, j, qc], start=True, stop=True), [bkTc, bqt], [bms], signal=(cc == 1))
                        if NSTG < 5:
                            continue
                        S.op("act", lambda e, pb=pb, sp_=sp_: e.activation(out=pb[0:64, :], in_=sp_[0:64, :], func=AF.Exp), [bsp], [bpb])
                        S.op("act", lambda e, pc=pc, ms=ms: e.activation(out=pc[:], in_=ms[:, 0:128], func=AF.Exp), [bms], [bpc])
                        if NSTG < 6:
                            continue
                        pbv = pb[0:64, :].rearrange("p (i w) -> p i w", w=64)
                        pcv = pc[:].rearrange("p (c w) -> p c w", w=64)
                        for part in range(2):
                            for i in range(8):
                                lh = (lambda vb=vb, i=i, h=h: vb[:, i, h * 64:(h + 1) * 64]) if part == 0 else (lambda: ones_b[0:64, 0:64])
                                S.op("pe", lambda e, part=part, i=i, lh=lh, pbv=pbv, od=od: e.matmul(od[:, part, :], lhsT=lh(), rhs=pbv[:, i, :], start=(i == 0), stop=False), [bvb, bpb], [bms], signal=False)
                            for cc in range(2):
                                lh = (lambda cc=cc, h=h: Vctx[:, cc, h * 64:(h + 1) * 64]) if part == 0 else (lambda: ones_b[:, 0:64])
                                S.op("pe", lambda e, part=part, cc=cc, lh=lh, pcv=pcv, od=od: e.matmul(od[:, part, :], lhsT=lh(), rhs=pcv[:, cc, :], start=False, stop=(cc == 1)), [bVc, bpc], [bms], signal=(cc == 1 and part == 1))
                        if NSTG < 7:
                            continue
                        S.op("act", lambda e, rc=rc, od=od: e.copy(out=rc[:, 0:64], in_=od[:, 1, :]), [bms], [brc])
                        S.op("dve", lambda e, rc=rc: e.reciprocal(out=rc[:, 0:64], in_=rc[:, 0:64]), [brc], [brc])
                        S.op("dve", lambda e, rc=rc, od=od, h=h, qc=qc, ot=ot: e.tensor_tensor(out=ot[0:64, h, qc], in0=od[:, 0, :], in1=rc[:, 0:64], op=ALU.mult), [bms, brc], [bot])
                if NSTG >= 8:
                    finish_block(ot, bot, m * 128, xs[m * 128:(m + 1) * 128, :], 1, x1[m * 128:(m + 1) * 128, :])

            for sq in range(2):
                base = NS + 256 * sq
                kt = rr(ktl, sq); bk = rr(bktl, sq)
                S.dma("sp", kt[:, 0:2, :], k0[base:base + 256, :].rearrange("(a p) f -> p a f", p=128), writes=[bk])
                for j in range(4):
                    for a in range(2):
                        S.op("pe", lambda e, kt=kt, a=a, j=j: e.transpose(out=ptr[:, a, :], in_=kt[:, a, j * 128:(j + 1) * 128], identity=ident[:]), [bk], [bptr], signal=(a == 1))
                    S.op("act", lambda e, j=j: e.copy(out=kTp[:, j, :], in_=ptr[:, 0:2, :].rearrange("p a t -> p (a t)")), [bptr], [bkTp])
                S.dma("sp", Vp[:], v0[base:base + 256, :].rearrange("(c p) f -> p c f", p=128), writes=[bVp])
                STG = 9
                for qb in range(2 if STG >= 2 else 0):
                    row0 = base + qb * 128
                    qt, bqt = load_qT(row0)
                    ot = rr(OT, qb); bot = rr(bOT, qb)
                    for h in range(8 if STG >= 3 else 0):
                        e2 = h % 2; j = h // 2; pp = slice(e2 * 64, (e2 + 1) * 64)
                        n = cnt["h"]; cnt["h"] += 1
                        sp_ = rr(sbp, n); bsp = rr(bsbp, n); ms = rr(msc, n); bms = rr(bmsc, n)
                        pb = rr(PB, n); bpb = rr(bPB, n); rc = rr(rec, n); brc = rr(brec, n)
                        sv = sp_[:, 0:256].rearrange("p (c t) -> p c t", t=128)
                        od = ms[0:64, 0:256].rearrange("p (c t) -> p c t", t=128)
                        for cc in range(2):
                            S.op("pe", lambda e, cc=cc, j=j, pp=pp, sv=sv, qt=qt: e.matmul(sv[:, cc, :], lhsT=kTp[pp, j, cc * 128:(cc + 1) * 128], rhs=qt[pp, j, :], start=True, stop=True), [bkTp, bqt], [bsp], signal=(cc == 1))
                        if STG < 4:
                            continue
                        S.op("act", lambda e, pb=pb, sp_=sp_: e.activation(out=pb[:, 0:256], in_=sp_[:, 0:256], func=AF.Exp), [bsp], [bpb])
                        if STG < 5:
                            continue
                        pbv = pb[:, 0:256].rearrange("p (c t) -> p c t", t=128)
                        for part in range(2):
                            for cc in range(2):
                                lh = (lambda cc=cc, h=h: Vp[:, cc, h * 64:(h + 1) * 64]) if part == 0 else (lambda: ones_b[:, 0:64])
                                S.op("pe", lambda e, part=part, cc=cc, lh=lh, pbv=pbv, od=od: e.matmul(od[:, part, :], lhsT=lh(), rhs=pbv[:, cc, :], start=(cc == 0), stop=(cc == 1)), [bVp, bpb], [bms], signal=(cc == 1 and part == 1))
                        if STG < 6:
                            continue
                        S.op("act", lambda e, rc=rc, od=od: e.copy(out=rc[:, :], in_=od[:, 1, :]), [bms], [brc])
                        S.op("dve", lambda e, rc=rc: e.reciprocal(out=rc[:, :], in_=rc[:, :]), [brc], [brc])
                        S.op("dve", lambda e, rc=rc, od=od, h=h, ot=ot: e.tensor_tensor(out=ot[0:64, h, :], in0=od[:, 0, :], in1=rc[:, :], op=ALU.mult), [bms, brc], [bot])
                    prow = 256 * sq + qb * 128
                    if STG < 7:
                        continue
                    finish_block(ot, bot, row0, xp[prow:prow + 128, :], 0, x1[row0:row0 + 128, :])
            S.flush()

        def ffn_phase(l, src, seqs, tag):
            with ExitStack() as es:
                Wdn = SB(es, tag + "Wdn", [128, NJ, D], BF16); bWdn = S.buf()
                for jj in range(2):
                    S.dma("sp", Wdn[:, jj * 11:(jj + 1) * 11, :], wdn_bf[l].rearrange("(j p) n -> p j n", p=128)[:, jj * 11:(jj + 1) * 11, :], writes=[bWdn])
                cw = SB(es, tag + "cw", [128, 3, NJ]); cb = SB(es, tag + "cb", [128, NJ]); bcw = S.buf()
                for i in range(3):
                    for (ja, jb) in ((0, 8), (8, 16), (16, NJ)):
                        S.dma("sp", cw[:, i, ja:jb], ffn_conv_w[l, i].rearrange("(j p) -> p j", p=128)[:, ja:jb], writes=[bcw])
                for (ja, jb) in ((0, 8), (8, 16), (16, NJ)):
                    S.dma("sp", cb[:, ja:jb], ffn_conv_b[l].rearrange("(j p) -> p j", p=128)[:, ja:jb], writes=[bcw])
                ncx = make_norm_ctx(es, l, 1, tag + "n")
                G2 = [mod_bc(es, l, cnd, 5, "%sG2c%d" % (tag, cnd)) for cnd in range(2)]
                hT = SB(es, tag + "hT", [128, 8, 512], BF16); bhT = S.buf()
                hid = SB(es, tag + "hid", [128, NJ, 512], BF16); bhid = S.buf()
                Wg = [SB(es, "%sWg%d" % (tag, i), [128, 8, 256], BF16) for i in range(2)]; bWg = S.bufs_n(2)
                Wv = [SB(es, "%sWv%d" % (tag, i), [128, 8, 256], BF16) for i in range(2)]; bWv = S.bufs_n(2)
                pg = [PS(es, "%spg%d" % (tag, i), [128, 512]) for i in range(2)]; bpg = S.bufs_n(2)
                pv = [PS(es, "%spv%d" % (tag, i), [128, 512]) for i in range(2)]; bpv = S.bufs_n(2)
                po = PS(es, tag + "po", [128, D]); bpo = S.buf()
                acc = [SB(es, "%sacc%d" % (tag, i), [128, 512]) for i in range(2)]; bacc = S.bufs_n(2)
                sg = [SB(es, "%ssg%d" % (tag, i), [128, 512]) for i in range(2)]; bsg = S.bufs_n(2)
                xn = [SB(es, "%sxn%d" % (tag, i), [128, D]) for i in range(2)]; bxn = S.bufs_n(2)
                xr = [SB(es, "%sxr%d" % (tag, i), [128, D]) for i in range(2)]; bxr = S.bufs_n(2)
                tm = [SB(es, "%stm%d" % (tag, i), [128, D]) for i in range(2)]; btm = S.bufs_n(2)
                wsrc = wup_bf[l].rearrange("(k p) n -> p k n", p=128)
                c = {"x": 0, "g": 0, "j": 0, "o": 0}
                for (row_base, L, cnd, dst_fn) in seqs:
                    t0 = 0
                    while t0 < L:
                        T = min(510, L - t0)
                        span = T + 2
                        c_lo = 1 if t0 == 0 else 0
                        c_hi = span - (1 if t0 + T == L else 0)
                        if c_lo == 1:
                            S.op("pool", lambda e: e.memset(hT[:, :, 0:1], 0.0), [], [bhT])
                        if c_hi == span - 1:
                            S.op("pool", lambda e, span=span: e.memset(hT[:, :, span - 1:span], 0.0), [], [bhT])
                        cc = c_lo
                        while cc < c_hi:
                            nr = min(128, c_hi - cc)
                            tok = t0 - 1 + cc
                            xt = rr(xn, c["x"]); bx = rr(bxn, c["x"]); c["x"] += 1
                            S.dma("sp", xt[0:nr, :], src[row_base + tok:row_base + tok + nr, :], writes=[bx])
                            norm_tile(ncx, xt, bx, cnd, hT, bhT, cc, nrows=nr)
                            cc += nr
                        for jg in range(NJ // 2):
                            wg = rr(Wg, c["g"]); bwg = rr(bWg, c["g"]); wv = rr(Wv, c["g"]); bwv = rr(bWv, c["g"]); c["g"] += 1
                            S.dma("sp", wg[:], wsrc[:, :, jg * 256:(jg + 1) * 256], writes=[bwg])
                            S.dma("sp", wv[:], wsrc[:, :, DFF + jg * 256:DFF + (jg + 1) * 256], writes=[bwv])
                            for jj in range(2):
                                j = jg * 2 + jj
                                n = c["j"]; c["j"] += 1
                                g_ = rr(pg, n); bg_ = rr(bpg, n); v_ = rr(pv, n); bv_ = rr(bpv, n)
                                ac = rr(acc, n); bac = rr(bacc, n); s_ = rr(sg, n); bs_ = rr(bsg, n)
                                for k in range(8):
                                    S.op("pe", lambda e, k=k, jj=jj, g_=g_, wg=wg, span=span: e.matmul(g_[:, 0:span], lhsT=wg[:, k, jj * 128:(jj + 1) * 128], rhs=hT[:, k, 0:span], start=(k == 0), stop=(k == 7)), [bwg, bhT], [bg_], signal=(k == 7))
                                for k in range(8):
                                    S.op("pe", lambda e, k=k, jj=jj, v_=v_, wv=wv, T=T: e.matmul(v_[:, 0:T], lhsT=wv[:, k, jj * 128:(jj + 1) * 128], rhs=hT[:, k, 1:T + 1], start=(k == 0), stop=(k == 7)), [bwv, bhT], [bv_], signal=(k == 7))
                                S.op("act", lambda e, ac=ac, g_=g_, j=j, T=T: e.activation(out=ac[:, 0:T], in_=g_[:, 1:T + 1], func=AF.Identity, scale=cw[:, 1, j:j + 1], bias=cb[:, j:j + 1]), [bg_, bcw], [bac])
                                S.op("dve", lambda e, ac=ac, g_=g_, j=j, T=T: e.scalar_tensor_tensor(out=ac[:, 0:T], in0=g_[:, 0:T], scalar=cw[:, 0, j:j + 1], in1=ac[:, 0:T], op0=ALU.mult, op1=ALU.add), [bg_, bcw, bac], [bac])
                                S.op("dve", lambda e, ac=ac, g_=g_, j=j, T=T: e.scalar_tensor_tensor(out=ac[:, 0:T], in0=g_[:, 2:T + 2], scalar=cw[:, 2, j:j + 1], in1=ac[:, 0:T], op0=ALU.mult, op1=ALU.add), [bg_, bcw, bac], [bac])
                                S.op("act", lambda e, ac=ac, s_=s_, T=T: e.activation(out=s_[:, 0:T], in_=ac[:, 0:T], func=AF.Silu), [bac], [bs_])
                                S.op("dve", lambda e, s_=s_, v_=v_, j=j, T=T: e.tensor_tensor(out=hid[:, j, 0:T], in0=v_[:, 0:T], in1=s_[:, 0:T], op=ALU.mult), [bv_, bs_], [bhid])
                        Gt, bG = G2[cnd]
                        for mb in range((T + 127) // 128):
                            nt = min(128, T - mb * 128)
                            o = c["o"]; c["o"] += 1
                            x2_ = rr(xr, o); bx2 = rr(bxr, o); tmo_ = rr(tm, o); btmo = rr(btm, o)
                            S.dma("sp", x2_[0:nt, :], src[row_base + t0 + mb * 128:row_base + t0 + mb * 128 + nt, :], writes=[bx2])
                            for cgi in range(2):
                                cs = slice(cgi * 512, (cgi + 1) * 512)
                                for j in range(NJ):
                                    S.op("pe", lambda e, j=j, cs=cs, mb=mb, nt=nt: e.matmul(po[0:nt, cs], lhsT=hid[:, j, mb * 128:mb * 128 + nt], rhs=Wdn[:, j, cs], start=(j == 0), stop=(j == NJ - 1)), [bhid, bWdn], [bpo], signal=(j == NJ - 1))
                            for cgi in range(2):
                                cs = slice(cgi * 512, (cgi + 1) * 512)
                                S.op("dve", lambda e, cs=cs, nt=nt, tmo_=tmo_, Gt=Gt: e.tensor_tensor(out=tmo_[0:nt, cs], in0=po[0:nt, cs], in1=Gt[0:nt, cs], op=ALU.mult), [bpo, bG], [btmo])
                            S.op("dve", lambda e, nt=nt, tmo_=tmo_, x2_=x2_: e.tensor_tensor(out=tmo_[0:nt, :], in0=tmo_[0:nt, :], in1=x2_[0:nt, :], op=ALU.add), [btmo, bx2], [btmo])
                            S.dma("sp", dst_fn(t0 + mb * 128, nt), tmo_[0:nt, :], reads=[btmo])
                        t0 += T
                S.flush()

        ffn_phase(0, x1, [(0, NS, 1, lambda t, n: x2[t:t + n, :]),
                          (NS, 256, 0, lambda t, n: x2[NS + t:NS + t + n, :]),
                          (NS + 256, 256, 0, lambda t, n: x2[NS + 256 + t:NS + 256 + t + n, :])], "f0")

        with ExitStack() as es:
            Win = SB(es, "Win1", [128, 8, 1536], BF16); bW = S.buf()
            for kh in range(4):
                S.dma("pool", Win[:, kh * 2:(kh + 1) * 2, :], od_w_in[0].rearrange("(k p) n -> p k n", p=128)[:, kh * 2:(kh + 1) * 2, :], writes=[bW])
            ncx = make_norm_ctx(es, 1, 0, "nB")
            hn = make_hn(es, "hB")
            qg, bqg = load_gbc(es, "qg1", gqa_q_g[0], 16, scale=0.125)
            kg, bkg = load_gbc(es, "kg1", gqa_k_g[0], 4)
            qg2 = qg[:].rearrange("p h d -> p (h d)"); kg2 = kg[:].rearrange("p h d -> p (h d)")
            wsl = SB(es, "wsl", [128, 2]); bwsl = S.buf()
            S.dma("sp", wsl[:], wsel, writes=[bwsl])
            fqi = SB(es, "fqi", [128, 16], I32); fq = SB(es, "fq", [128, 16]); bfq = S.buf()
            S.op("pool", lambda e: e.iota(fqi[:], pattern=[[1, 16]], base=0, channel_multiplier=0), [], [bfq])
            S.op("pool", lambda e: e.tensor_copy(out=fq[:], in_=fqi[:]), [bfq], [bfq])
            S.op("act", lambda e: e.activation(out=fq[:], in_=fq[:], func=AF.Exp, scale=-float(np.log(10000.0)) / 16.0), [bfq], [bfq])
            S.op("act", lambda e: e.mul(out=fq[:], in_=fq[:], mul=1.0 / (2.0 * np.pi)), [bfq], [bfq])
            xts = [SB(es, "xtB%d" % i, [128, D]) for i in range(2)]; bxs = S.bufs_n(2)
            xbs = [SB(es, "xbB%d" % i, [128, D]) for i in range(2)]; bxb = S.bufs_n(2)
            hTs = [SB(es, "hTB%d" % i, [128, 8, 128], BF16) for i in range(2)]; bhTs = S.bufs_n(2)
            pq_ = [PS(es, "pqB%d" % i, [128, 512]) for i in range(2)]; bpq = S.bufs_n(2)
            pkv = PS(es, "pkvB", [128, 512]); bpkv = S.buf()
            o32 = [SB(es, "o32B%d" % i, [128, D]) for i in range(2)]; bo32 = S.bufs_n(2)
            o16 = [SB(es, "o16B%d" % i, [128, D], BF16) for i in range(2)]; bo16 = S.bufs_n(2)
            k32 = [SB(es, "k32B%d" % i, [128, 256]) for i in range(2)]; bk32 = S.bufs_n(2)
            k16 = [SB(es, "k16B%d" % i, [128, 256], BF16) for i in range(2)]; bk16 = S.bufs_n(2)
            v16 = [SB(es, "v16B%d" % i, [128, 256], BF16) for i in range(2)]; bv16 = S.bufs_n(2)
            v32 = [SB(es, "v32B%d" % i, [128, 256]) for i in range(2)]; bv32 = S.bufs_n(2)
            pos = [SB(es, "posB%d" % i, [128, 2]) for i in range(2)]; bpos = S.bufs_n(2)
            ang = SB(es, "angB", [128, 2, 16]); bang = S.buf()
            SINt = [SB(es, "sinB%d" % i, [128, 2, 16]) for i in range(2)]; COSt = [SB(es, "cosB%d" % i, [128, 2, 16]) for i in range(2)]
            bSC = S.bufs_n(2)
            r1 = SB(es, "r1B", [128, 16, 2, 16]); r2 = SB(es, "r2B", [128, 16, 2, 16]); br1 = S.buf(); br2 = S.buf()
            cn = {"t": 0, "r": 0}

            def rope_tables(pos_rows_ap):
                i = cn["r"]; cn["r"] += 1
                p_ = rr(pos, i); bp = rr(bpos, i); sn = rr(SINt, i); cs = rr(COSt, i); bsc = rr(bSC, i)
                S.dma("sp", p_[:], pos_rows_ap, writes=[bp])
                for a in range(2):
                    S.op("dve", lambda e, a=a, p_=p_: e.tensor_scalar(out=ang[:, a, :], in0=fq[:], scalar1=p_[:, a:a + 1], scalar2=None, op0=ALU.mult), [bfq, bp], [bang])
                sincos(es, "scB", ang[:], [128, 2, 16], sn[:], cs[:], [bang], [bsc])
                return sn, cs, bsc

            def rope_apply(src32, bsrc, nh, dst16, bdst, sn, cs, bsc):
                xv = src32.rearrange("p (h a b i) -> p h a b i", a=2, b=2, i=16)
                ov = dst16.rearrange("p (h a b i) -> p h a b i", a=2, b=2, i=16)
                x1v = xv[:, :, :, 0, :]; x2v = xv[:, :, :, 1, :]
                cb_ = cs[:].unsqueeze(1).broadcast_to([128, nh, 2, 16]); sb_ = sn[:].unsqueeze(1).broadcast_to([128, nh, 2, 16])
                t1 = r1[:, 0:nh, :, :]; t2 = r2[:, 0:nh, :, :]
                S.op("dve", lambda e: e.tensor_tensor(out=t1, in0=x1v, in1=cb_, op=ALU.mult), [bsrc, bsc], [br1])
                S.op("pool", lambda e: e.tensor_tensor(out=t2, in0=x2v, in1=sb_, op=ALU.mult), [bsrc, bsc], [br2])
                S.op("dve", lambda e: e.tensor_tensor(out=ov[:, :, :, 0, :], in0=t1, in1=t2, op=ALU.subtract), [br1, br2], [bdst])
                S.op("dve", lambda e: e.tensor_tensor(out=t1, in0=x1v, in1=sb_, op=ALU.mult), [bsrc, bsc], [br1])
                S.op("pool", lambda e: e.tensor_tensor(out=t2, in0=x2v, in1=cb_, op=ALU.mult), [bsrc, bsc], [br2])
                S.op("dve", lambda e: e.tensor_tensor(out=ov[:, :, :, 1, :], in0=t1, in1=t2, op=ALU.add), [br1, br2], [bdst])

            def proj(hT, bhT, want_q, want_kv):
                if want_q:
                    for half in range(2):
                        for k in range(8):
                            S.op("pe", lambda e, half=half, k=k: e.matmul(pq_[half][:], lhsT=hT[:, k, :], rhs=Win[:, k, half * 512:(half + 1) * 512], start=(k == 0), stop=(k == 7)), [bhT, bW], [bpq[half]], signal=(k == 7))
                if want_kv:
                    for k in range(8):
                        S.op("pe", lambda e, k=k: e.matmul(pkv[:], lhsT=hT[:, k, :], rhs=Win[:, k, 1024:1536], start=(k == 0), stop=(k == 7)), [bhT, bW], [bpkv], signal=(k == 7))

            def q_part(rows1, rope):
                i = cn["t"]
                o3 = rr(o32, i); b3 = rr(bo32, i); o6 = rr(o16, i); b6 = rr(bo16, i)
                for half in range(2):
                    head_norm(hn, pq_[half][:], 8, qg2[:, half * 512:(half + 1) * 512], bqg, o3[:, half * 512:(half + 1) * 512], b3, [bpq[half]])
                if rope is not None:
                    rope_apply(o3[:], b3, 16, o6[:], b6, *rope)
                else:
                    S.op("act", lambda e: e.copy(out=o6[:], in_=o3[:]), [b3], [b6])
                S.dma("sp", q1[rows1, :], o6[:], reads=[b6])

            def kv_part(rows0, rope, prow=None):
                i = cn["t"]
                k3 = rr(k32, i); bk3 = rr(bk32, i); k6 = rr(k16, i); bk6 = rr(bk16, i); v6 = rr(v16, i); bv6 = rr(bv16, i)
                head_norm(hn, pkv[:, 0:256], 4, kg2, bkg, k3[:], bk3, [bpkv])
                if rope is not None:
                    rope_apply(k3[:], bk3, 4, k6[:], bk6, *rope)
                else:
                    S.op("act", lambda e: e.copy(out=k6[:], in_=k3[:]), [bk3], [bk6])
                S.dma("sp", k1[rows0, :], k6[:], reads=[bk6])
                S.op("act", lambda e: e.copy(out=v6[:], in_=pkv[:, 256:512]), [bpkv], [bv6])
                S.dma("sp", v1[rows0, :], v6[:], reads=[bv6])
                if prow is not None:
                    v3 = rr(v32, i); bv3 = rr(bv32, i)
                    S.dma("sp", gk_o[prow, :], k3[:], reads=[bk3])
                    S.op("dve", lambda e: e.tensor_copy(out=v3[:], in_=pkv[:, 256:512]), [bpkv], [bv3])
                    S.dma("sp", gv_o[prow, :], v3[:], reads=[bv3])

            for i in range(32):
                n = cn["t"]
                xt = rr(xts, n); bx = rr(bxs, n); hT = rr(hTs, n); bhT = rr(bhTs, n)
                S.dma("sp", xt[:], x2[i * 128:(i + 1) * 128, :], writes=[bx])
                norm_tile(ncx, xt, bx, 1, hT, bhT, 0)
                proj(hT, bhT, False, True)
                rope = rope_tables(pos_all[i * 128:(i + 1) * 128, :])
                kv_part(slice(i * 128, (i + 1) * 128), rope)
                cn["t"] += 1
            for lt in range(NW // 128):
                n = cn["t"]
                xt = rr(xts, n); bx = rr(bxs, n); xb = rr(xbs, n); bxb_ = rr(bxb, n); hT = rr(hTs, n); bhT = rr(bhTs, n)
                S.dma("sp", xt[:], x2[lt * 128:(lt + 1) * 128, :], writes=[bx])
                S.dma("sp", xb[:], x2[1920 + lt * 128:1920 + (lt + 1) * 128, :], writes=[bxb_])
                S.op("dve", lambda e, xt=xt: e.tensor_scalar(out=xt[:], in0=xt[:], scalar1=wsl[:, 0:1], scalar2=None, op0=ALU.mult), [bx, bwsl], [bx])
                S.op("dve", lambda e, xt=xt, xb=xb: e.scalar_tensor_tensor(out=xt[:], in0=xb[:], scalar=wsl[:, 1:2], in1=xt[:], op0=ALU.mult, op1=ALU.add), [bx, bxb_, bwsl], [bx])
                S.dma("sp", x2w[lt * 128:(lt + 1) * 128, :], xt[:], reads=[bx])
                norm_tile(ncx, xt, bx, 1, hT, bhT, 0)
                proj(hT, bhT, True, False)
                rope = rope_tables(pos_win[lt * 128:(lt + 1) * 128, :])
                q_part(slice(lt * 128, (lt + 1) * 128), rope)
                cn["t"] += 1
            for pt in range(4):
                n = cn["t"]
                xt = rr(xts, n); bx = rr(bxs, n); hT = rr(hTs, n); bhT = rr(bhTs, n)
                r0 = slice(NS + pt * 128, NS + (pt + 1) * 128)
                rw = slice(NW + pt * 128, NW + (pt + 1) * 128)
                S.dma("sp", xt[:], x2[r0, :], writes=[bx])
                S.dma("sp", x2w[rw, :], xt[:], reads=[bx])
                norm_tile(ncx, xt, bx, 0, hT, bhT, 0)
                proj(hT, bhT, True, True)
                q_part(rw, None)
                kv_part(r0, None, prow=slice(pt * 128, (pt + 1) * 128))
                cn["t"] += 1
            S.flush()

        with ExitStack() as es:
            NCH = (NS + 256) // 128
            KT = SB(es, "KT1", [128, 4, NS + 256], BF16); bKT = S.buf()
            VA = SB(es, "VA1", [128, NCH, 4, 128], BF16); bVA = S.buf()
            KTp = SB(es, "KTp1", [128, 4, 256], BF16); bKTp = S.buf()
            VAp = SB(es, "VAp1", [128, 2, 4, 128], BF16); bVAp = S.buf()
            Wo = SB(es, "Wo1", [64, 16, D], BF16); bWo = S.buf()
            for hh in range(2):
                S.dma("pool", Wo[:, hh * 8:(hh + 1) * 8, :], od_w_out[0].rearrange("(h d) n -> d h n", d=64)[:, hh * 8:(hh + 1) * 8, :], writes=[bWo])
            G1b = [mod_bc(es, 1, cnd, 2, "G1b%d" % cnd) for cnd in range(2)]
            for c8 in range(0, NCH, 8):
                S.op("pool", lambda e, c8=c8: e.memset(VA[:, c8:min(c8 + 8, NCH), :, :], 1.0), [], [bVA])
            S.op("pool", lambda e: e.memset(VAp[:], 1.0), [], [bVAp])
            kl = [SB(es, "kl1%d" % i, [128, 256], BF16) for i in range(2)]; bkl = S.bufs_n(2)
            vl = [SB(es, "vl1%d" % i, [128, 256], BF16) for i in range(2)]; bvl = S.bufs_n(2)
            kd = [SB(es, "kd1%d" % i, [128, 4, 2, 64], BF16) for i in range(2)]; bkd = S.bufs_n(2)
            ptr = PS(es, "ptr1", [128, 8, 128], BF16); bptr = S.buf()
            psS = [[PS(es, "psS%d_%d" % (i, e2), [128, 512]) for e2 in range(2)] for i in range(3)]
            bpsS = [S.bufs_n(2) for i in range(3)]
            pod = [PS(es, "pod%d" % i, [128, 512]) for i in range(1)]; bpod = S.bufs_n(1)
            cn = {"k": 0, "b": 0, "g": 0, "p": 0}

            def build_kv(k_src, v_src, KTd, bKTd, VAd, bVAd, chunk, cast_q):
                i = cn["k"]; cn["k"] += 1
                k_ = rr(kl, i); bk = rr(bkl, i); v_ = rr(vl, i); bv = rr(bvl, i); d_ = rr(kd, i); bd = rr(bkd, i)
                S.dma(cast_q, k_[:], k_src, writes=[bk])
                S.dma(cast_q, v_[:], v_src, writes=[bv])
                S.op("act", lambda e: e.copy(out=d_[:], in_=k_[:].rearrange("p (g d) -> p g d", d=64).unsqueeze(2).broadcast_to([128, 4, 2, 64])), [bk], [bd])
                for g_ in range(4):
                    S.op("pe", lambda e, g_=g_: e.transpose(out=ptr[:, g_, :], in_=d_[:, g_, :, :].rearrange("p a d -> p (a d)"), identity=ident[:]), [bd], [bptr], signal=(g_ == 3))
                S.op("act", lambda e: e.copy(out=KTd[:, :, chunk * 128:(chunk + 1) * 128], in_=ptr[:, 0:4, :]), [bptr], [bKTd])
                S.op("pool", lambda e: e.tensor_copy(out=VAd[:, chunk, :, 0:64], in_=v_[:].rearrange("p (g d) -> p g d", d=64)), [bv], [bVAd])

            for i in range(32):
                build_kv(k1[i * 128:(i + 1) * 128, :], v1[i * 128:(i + 1) * 128, :], KT, bKT, VA, bVA, i, "sp")
            for cc in range(2):
                build_kv(gq_kc[cc * 128:(cc + 1) * 128, :], gq_vc[cc * 128:(cc + 1) * 128, :], KT, bKT, VA, bVA, 32 + cc, "pool")

            ql = [SB(es, "ql1%d" % i, [128, D], BF16) for i in range(2)]; bql = S.bufs_n(2)
            qT = [SB(es, "qT1%d" % i, [128, 8, 128], BF16) for i in range(2)]; bqT = S.bufs_n(2)
            Pm = [SB(es, "Pm1%d" % i, [128, 512], BF16) for i in range(3)]; bPm = S.bufs_n(3)
            rc = [SB(es, "rc1%d" % i, [64, 512]) for i in range(2)]; brc = S.bufs_n(2)
            OT = [SB(es, "OT1%d" % i, [64, 16, 128], BF16) for i in range(2)]; bOT = S.bufs_n(2)
            xtl = [SB(es, "xt1%d" % i, [128, D]) for i in range(2)]; bxtl = S.bufs_n(2)
            tmo = [SB(es, "tm1%d" % i, [128, D]) for i in range(2)]; btmo = S.bufs_n(2)

            CSTG = 9

            def gqa_block(row0, KTx, bKTx, VAx, bVAx, nch, cnd):
                if CSTG < 2:
                    return
                b = cn["b"]; cn["b"] += 1
                q_ = rr(ql, b); bq = rr(bql, b); qt = rr(qT, b); bqt = rr(bqT, b); ot = rr(OT, b); bot = rr(bOT, b)
                xt = rr(xtl, b); bx = rr(bxtl, b); tm = rr(tmo, b); btm = rr(btmo, b)
                S.dma("sp", q_[:], q1[row0:row0 + 128, :], writes=[bq])
                S.dma("sp", xt[:], x2w[row0:row0 + 128, :], writes=[bx])
                for pr in range(8):
                    S.op("pe", lambda e, pr=pr: e.transpose(out=ptr[:, pr, :], in_=q_[:, pr * 128:(pr + 1) * 128], identity=ident[:]), [bq], [bptr], signal=(pr == 7))
                S.op("act", lambda e: e.copy(out=qt[:], in_=ptr[:]), [bptr], [bqt])
                for g_ in range(4 if CSTG >= 3 else 0):
                    gi = cn["g"]; cn["g"] += 1
                    od = rr(pod, gi); bod = rr(bpod, gi); r_ = rr(rc, gi); br_ = rr(brc, gi)

                    def QK(cc, g_=g_):
                        n = cn["p"] + cc
                        sp2 = rr(psS, n); bsp2 = rr(bpsS, n)
                        for a in (0, 2, 1, 3):
                            pr = 2 * g_ + a // 2; e2 = a % 2; pp = slice(e2 * 64, (e2 + 1) * 64)
                            sp_ = sp2[e2]
                            S.op("pe", lambda e, a=a, pr=pr, pp=pp, sp_=sp_, cc=cc: e.matmul(sp_[:, (a // 2) * 128:(a // 2 + 1) * 128], lhsT=KTx[pp, g_, cc * 128:(cc + 1) * 128], rhs=qt[pp, pr, :], start=True, stop=True), [bKTx, bqt], [bsp2[e2]], signal=(a >= 2))
                    QK(0)
                    if nch > 1:
                        QK(1)
                    for cc in range(nch):
                        if cc + 2 < nch:
                            QK(cc + 2)
                        n = cn["p"] + cc
                        sp2 = rr(psS, n); bsp2 = rr(bpsS, n); pm = rr(Pm, n); bpm = rr(bPm, n)
                        pmv = pm[:].rearrange("p (h e t) -> p h e t", e=2, t=128)
                        for e2 in range(2):
                            S.op("act", lambda e, sp2=sp2, pmv=pmv, e2=e2: e.activation(out=pmv[:, :, e2, :], in_=sp2[e2][:, 0:256].rearrange("p (h t) -> p h t", t=128), func=AF.Exp), [bsp2[e2]], [bpm])
                        S.op("pe", lambda e, cc=cc, pm=pm, od=od, g_=g_: e.matmul(od[:], lhsT=VAx[:, cc, g_, :], rhs=pm[:], start=(cc == 0), stop=(cc == nch - 1)), [bVAx, bpm], [bod], signal=(cc == nch - 1))
                    cn["p"] += nch
                    if CSTG < 4:
                        continue
                    S.op("act", lambda e, r_=r_, od=od: e.copy(out=r_[:], in_=od[64:128, :]), [bod], [br_])
                    S.op("dve", lambda e, r_=r_: e.reciprocal(out=r_[:], in_=r_[:]), [br_], [br_])
                    S.op("dve", lambda e, r_=r_, od=od, g_=g_: e.tensor_tensor(out=ot[0:64, 4 * g_:4 * g_ + 4, :].rearrange("p h t -> p (h t)"), in0=od[0:64, :], in1=r_[:], op=ALU.mult), [bod, br_], [bot])
                if CSTG < 5:
                    return
                for cgi in range(2):
                    cs = slice(cgi * 512, (cgi + 1) * 512)
                    pot = psS[0][cgi]; bpot = bpsS[0][cgi]
                    for hh in range(16):
                        S.op("pe", lambda e, hh=hh, cs=cs, pot=pot: e.matmul(pot[:, 0:512], lhsT=ot[0:64, hh, :], rhs=Wo[0:64, hh, cs], start=(hh == 0), stop=(hh == 15)), [bot, bWo], [bpot], signal=(hh == 15))
                Gt, bG = G1b[cnd]
                for cgi in range(2):
                    cs = slice(cgi * 512, (cgi + 1) * 512)
                    pot = psS[0][cgi]; bpot = bpsS[0][cgi]
                    S.op("dve", lambda e, cs=cs, pot=pot: e.tensor_tensor(out=tm[:, cs], in0=pot[:, 0:512], in1=Gt[:, cs], op=ALU.mult), [bpot, bG], [btm])
                S.op("dve", lambda e: e.tensor_tensor(out=tm[:], in0=tm[:], in1=xt[:], op=ALU.add), [btm, bx], [btm])
                S.dma("sp", x3[row0:row0 + 128, :], tm[:], reads=[btm])

            for wb in range(NW // 128):
                gqa_block(wb * 128, KT, bKT, VA, bVA, NCH, 1)
            for sq in range(2 if CSTG >= 6 else 0):
                for cc in range(2):
                    r0 = NS + sq * 256 + cc * 128
                    build_kv(k1[r0:r0 + 128, :], v1[r0:r0 + 128, :], KTp, bKTp, VAp, bVAp, cc, "sp")
                for qb in range(2):
                    gqa_block(NW + sq * 256 + qb * 128, KTp, bKTp, VAp, bVAp, 2, 0)
            S.flush()

        if CSTG >= 7:
          ffn_phase(1, x3, [(0, NW, 1, lambda t, n: ysw_o[t:t + n, :]),
                          (NW, 256, 0, lambda t, n: yp_o[t:t + n, :]),
                          (NW + 256, 256, 0, lambda t, n: yp_o[256 + t:256 + t + n, :])], "f1")

        S.nops_total = S.nops
    return nc


_NC = None


def _natb_host(rpb):
    w = np.arange(64)
    cs = np.clip(w - 8, 0, 48)
    wp = np.arange(64)[:, None]
    valid = (wp >= cs[None, :]) & (wp < cs[None, :] + 16)
    idx = np.clip(wp - w[None, :] + 15, 0, 30)
    out = rpb[0][:, :, idx]
    out = np.where(valid[None, None], out, np.float32(NEG)).astype(np.float32)
    return np.ascontiguousarray(out)


def kernel(**inp):
    global _NC
    if _NC is None:
        _NC = build_program()
    nc = _NC
    f = lambda a: np.ascontiguousarray(np.asarray(a, dtype=np.float32))
    t = np.arange(NS)
    pos_all = np.stack([t // 64, t % 64], axis=1).astype(np.float32)
    natb = _natb_host(np.asarray(inp["na_rpb"], dtype=np.float32))
    wnames = ["norm1_g", "norm2_g", "ada_w", "ada_b", "ffn_w_up", "ffn_conv_w", "ffn_conv_b", "ffn_w_down",
              "ev_w_in", "ev_w_out", "na_q_g", "na_k_g", "ssm_a_re", "ssm_a_im", "ssm_log_dt", "ssm_b_re", "ssm_b_im",
              "ssm_c_re", "ssm_c_im", "ssm_d", "ssm_w_glu", "ssm_b_glu", "od_w_in", "od_w_out", "gqa_q_g", "gqa_k_g"]
    shared = {n: f(inp[n]) for n in wnames}
    in_maps = []
    for c in range(8):
        b = c // 2
        hf = c % 2
        s0 = 1920 * hf
        m = dict(shared)
        m["xp"] = f(inp["x_prompt"][2 * c:2 * c + 2]).reshape(NPR, D)
        m["xs"] = f(inp["x_sample"][b])
        m["na_kc"] = f(inp["cache_na_k"][b, 0]).reshape(256, 512)
        m["na_vc"] = f(inp["cache_na_v"][b, 0]).reshape(256, 512)
        m["sre_in"] = f(inp["state_ssm_re"][b, 0])
        m["sim_in"] = f(inp["state_ssm_im"][b, 0])
        m["gq_kc"] = f(inp["cache_gqa_k"][b, 0]).reshape(256, 256)
        m["gq_vc"] = f(inp["cache_gqa_v"][b, 0]).reshape(256, 256)
        m["cvec"] = np.ascontiguousarray(np.stack([f(inp["c_ctx"]), f(inp["c"][b])], axis=0))
        m["wsel"] = np.ascontiguousarray(np.tile(np.array([[1.0 - hf, float(hf)]], np.float32), (128, 1)))
        m["pos_all"] = pos_all
        m["pos_win"] = np.ascontiguousarray(pos_all[s0:s0 + NW])
        m["natb"] = natb
        in_maps.append(m)
    res = run_bass_kernel_spmd(nc, in_maps, core_ids=list(range(8)))
    R = res.results
    y_prompt = np.zeros((16, 256, D), np.float32)
    y_sample = np.zeros((4, NS, D), np.float32)
    new_na_k = np.zeros((16, 1, 256, 8, 64), np.float32)
    new_na_v = np.zeros((16, 1, 256, 8, 64), np.float32)
    new_ssm_re = np.zeros((16, 1, 2, 32, 64), np.float32)
    new_ssm_im = np.zeros((16, 1, 2, 32, 64), np.float32)
    new_gqa_k = np.zeros((16, 1, 256, 4, 64), np.float32)
    new_gqa_v = np.zeros((16, 1, 256, 4, 64), np.float32)
    for c in range(8):
        b = c // 2
        hf = c % 2
        r = R[c]
        y_prompt[2 * c:2 * c + 2] = np.asarray(r["yp_o"]).reshape(2, 256, D)
        off = 128 * hf
        y_sample[b, hf * 2048:(hf + 1) * 2048] = np.asarray(r["ysw_o"])[off:off + 2048]
        new_na_k[2 * c:2 * c + 2, 0] = np.asarray(r["nak_o"]).reshape(2, 256, 8, 64)
        new_na_v[2 * c:2 * c + 2, 0] = np.asarray(r["nav_o"]).reshape(2, 256, 8, 64)
        new_ssm_re[2 * c:2 * c + 2, 0] = np.asarray(r["sre_o"]).reshape(2, 2, 32, 64)
        new_ssm_im[2 * c:2 * c + 2, 0] = np.asarray(r["sim_o"]).reshape(2, 2, 32, 64)
        new_gqa_k[2 * c:2 * c + 2, 0] = np.asarray(r["gk_o"]).reshape(2, 256, 4, 64)
        new_gqa_v[2 * c:2 * c + 2, 0] = np.asarray(r["gv_o"]).reshape(2, 256, 4, 64)
    return (y_prompt, y_sample, new_na_k, new_na_v, new_ssm_re, new_ssm_im, new_gqa_k, new_gqa_v)
```
